# Optimizing a Trainium2 kernel written in Bass

```python
import math
import jax
import jax.numpy as jnp
from jax import lax
import numpy as np

D_MODEL = 1024
BATCH = 2
SEQ = 16384
DEPTH = 4

HEAD_DIM = 64
GROUP_WIDTH = D_MODEL // 4
A_HEADS = GROUP_WIDTH // HEAD_DIM
A_KV_HEADS = A_HEADS // 2
A_HALF_WINDOW = 128
A_BLOCK = 128
S5_GROUP = 16
S5_GROUPS = GROUP_WIDTH // S5_GROUP
S5_STATE = 64
S5_DT_MIN = 1e-3
S5_DT_MAX = 1e-1
C_HEADS = GROUP_WIDTH // HEAD_DIM
DILATED_CONFIGS = ((128, 1), (512, 4), (2048, 16))
C_BLOCK = 64
D_HEADS = GROUP_WIDTH // HEAD_DIM
GRID_W = 64
NA_ROWS = 8
NA_COLS = 16
NA_COL_BLOCK = 16
ALIBI_HEADS = A_HEADS + C_HEADS
D_FF = 2816
RMS_EPS = 1e-6
NEG_INF = -1e30
PROJ_SIZES = (A_HEADS * HEAD_DIM, A_KV_HEADS * HEAD_DIM, A_KV_HEADS * HEAD_DIM, GROUP_WIDTH,
              C_HEADS * HEAD_DIM, C_HEADS * HEAD_DIM, C_HEADS * HEAD_DIM,
              D_HEADS * HEAD_DIM, D_HEADS * HEAD_DIM, D_HEADS * HEAD_DIM)
D_IN_PROJ = sum(PROJ_SIZES)

kernel_name = 'hybrid_parallel_heads_encoder'


def rms_norm(x, gain):
    xf = x.astype(jnp.float32)
    y = xf * lax.rsqrt(jnp.mean(xf * xf, axis=-1, keepdims=True) + RMS_EPS)
    return (y * gain.astype(jnp.float32)).astype(x.dtype)


def swiglu(x, w_in, w_out):
    gate, up = jnp.split(x @ w_in, 2, axis=-1)
    return (jax.nn.silu(gate) * up) @ w_out


def alibi_slopes(n):
    return jnp.exp2(-8.0 * jnp.arange(1, n + 1, dtype=jnp.float32) / n)


def banded_attention(q, k, v, half_window, block, slopes, pos_scale):
    n, length, h, hd = q.shape
    hk = k.shape[2]
    rep = h // hk
    nb = -(-length // block)
    lp = nb * block
    pad = lp - length
    qb = jnp.pad(q, ((0, 0), (0, pad), (0, 0), (0, 0))).reshape(n, nb, block, hk, rep, hd)
    kvpad = ((0, 0), (block, pad + block), (0, 0), (0, 0))
    kb = jnp.pad(k, kvpad).reshape(n, nb + 2, block, hk, hd)
    vb = jnp.pad(v, kvpad).reshape(n, nb + 2, block, hk, hd)
    kwin = jnp.concatenate([kb[:, :-2], kb[:, 1:-1], kb[:, 2:]], axis=2)
    vwin = jnp.concatenate([vb[:, :-2], vb[:, 1:-1], vb[:, 2:]], axis=2)
    scores = jnp.einsum('nbqgrd,nbkgd->nbgrqk', qb, kwin, preferred_element_type=jnp.float32) * (hd ** -0.5)
    rel = np.arange(3 * block)[None, :] - block - np.arange(block)[:, None]
    key_pos = (np.arange(nb)[:, None] - 1) * block + np.arange(3 * block)[None, :]
    valid = (np.abs(rel) <= half_window)[None] & ((key_pos >= 0) & (key_pos < length))[:, None, :]
    dist = np.abs(rel).astype(np.float32)
    alibi = (-(slopes.astype(jnp.float32) * pos_scale)[:, None, None] * dist).reshape(hk, rep, block, 3 * block)
    scores = jnp.where(valid[None, :, None, None], scores + alibi[None, None], NEG_INF)
    m = jnp.max(scores, axis=-1, keepdims=True)
    p = jnp.exp(scores - m)
    l = jnp.sum(p, axis=-1, keepdims=True)
    o = jnp.einsum('nbgrqk,nbkgd->nbqgrd', p / l, vwin.astype(jnp.float32))
    o = o.reshape(n, lp, h, hd)[:, :length].astype(v.dtype)
    lse = (m + jnp.log(l))[..., 0].transpose(0, 1, 4, 2, 3).reshape(n, lp, h)[:, :length]
    return o, lse


def dilated_attention(q, k, v, slopes):
    b, s, h, hd = q.shape

    def by_stride(t, dil):
        return t.reshape(b, s // dil, dil, h, hd).transpose(0, 2, 1, 3, 4).reshape(b * dil, s // dil, h, hd)

    outs, lses = [], []
    for window, dil in DILATED_CONFIGS:
        o, lse = banded_attention(by_stride(q, dil), by_stride(k, dil), by_stride(v, dil),
                                  window // (2 * dil), C_BLOCK, slopes, float(dil))
        outs.append(o.reshape(b, dil, s // dil, h, hd).transpose(0, 2, 1, 3, 4).reshape(b, s, h, hd))
        lses.append(lse.reshape(b, dil, s // dil, h).transpose(0, 2, 1, 3).reshape(b, s, h))
    weights = jax.nn.softmax(jnp.stack(lses), axis=0)
    o = jnp.einsum('gbsh,gbshd->bshd', weights, jnp.stack(outs).astype(jnp.float32))
    return o.astype(q.dtype)


def _complex_linear_combine(first, second):
    a1r, a1i, b1r, b1i = first
    a2r, a2i, b2r, b2i = second
    return (a2r * a1r - a2i * a1i,
            a2r * a1i + a2i * a1r,
            a2r * b1r - a2i * b1i + b2r,
            a2r * b1i + a2i * b1r + b2i)


def s5_bidirectional(u, lam_re, lam_im, log_step, b_re, b_im, c_re, c_im, d_skip):
    bsz, s, width = u.shape
    uf = u.astype(jnp.float32).reshape(bsz, s, S5_GROUPS, S5_GROUP).transpose(1, 0, 2, 3)
    y = uf * d_skip.astype(jnp.float32).reshape(S5_GROUPS, S5_GROUP)
    for direction in range(2):
        lr = lam_re[direction].astype(jnp.float32)
        li = lam_im[direction].astype(jnp.float32)
        step = jnp.exp(log_step[direction].astype(jnp.float32))[:, None]
        mag = jnp.exp(lr * step)
        ar = mag * jnp.cos(li * step)
        ai = mag * jnp.sin(li * step)
        den = lr * lr + li * li
        xr = ar - 1.0
        coef_r = (xr * lr + ai * li) / den
        coef_i = (ai * lr - xr * li) / den
        br = b_re[direction].astype(jnp.float32)
        bi = b_im[direction].astype(jnp.float32)
        bbar_r = coef_r[..., None] * br - coef_i[..., None] * bi
        bbar_i = coef_r[..., None] * bi + coef_i[..., None] * br
        bu_r = jnp.einsum('sbgc,gpc->sbgp', uf, bbar_r)
        bu_i = jnp.einsum('sbgc,gpc->sbgp', uf, bbar_i)
        a_shape = (s, 1, S5_GROUPS, S5_STATE)
        _, _, xs_r, xs_i = lax.associative_scan(
            _complex_linear_combine,
            (jnp.broadcast_to(ar[None, None], a_shape), jnp.broadcast_to(ai[None, None], a_shape), bu_r, bu_i),
            reverse=bool(direction), axis=0)
        y = y + jnp.einsum('sbgp,gcp->sbgc', xs_r, c_re[direction].astype(jnp.float32)) \
              - jnp.einsum('sbgp,gcp->sbgc', xs_i, c_im[direction].astype(jnp.float32))
    return y.transpose(1, 0, 2, 3).reshape(bsz, s, width)


def _na_axis(n, k, qblk):
    nblk = n // qblk
    width = min(qblk + k, n)
    q_idx = np.arange(n).reshape(nblk, qblk)
    win_start = np.clip(q_idx - k // 2, 0, n - k)
    slab_start = np.clip(np.arange(nblk) * qblk - k // 2, 0, n - width)
    key_idx = slab_start[:, None] + np.arange(width)[None, :]
    keys = key_idx[:, None, :]
    valid = (keys >= win_start[..., None]) & (keys < win_start[..., None] + k)
    offset = keys - q_idx[..., None]
    return key_idx, valid, offset


def neighborhood_attention(q, k, v, rel_bias):
    bsz, s, h, hd = q.shape
    rows = s // GRID_W
    kr = min(NA_ROWS, rows)
    qr = math.gcd(rows, NA_ROWS)
    row_keys, row_valid, row_off = _na_axis(rows, kr, qr)
    col_keys, col_valid, col_off = _na_axis(GRID_W, NA_COLS, NA_COL_BLOCK)
    nrb, ncb = rows // qr, GRID_W // NA_COL_BLOCK
    ri = np.clip(row_off + NA_ROWS - 1, 0, 2 * NA_ROWS - 2)
    ci = np.clip(col_off + NA_COLS - 1, 0, 2 * NA_COLS - 2)
    bias = rel_bias.astype(jnp.float32)[:, ri[:, None, :, None, :, None], ci[None, :, None, :, None, :]]
    mask = row_valid[:, None, :, None, :, None] & col_valid[None, :, None, :, None, :]
    bias = jnp.where(mask[None], bias, NEG_INF).transpose(1, 2, 0, 3, 4, 5, 6)
    qg = q.reshape(bsz, nrb, qr, ncb, NA_COL_BLOCK, h, hd)
    idx_r = row_keys[:, None, :, None]
    idx_c = col_keys[None, :, None, :]
    kg = k.reshape(bsz, rows, GRID_W, h, hd)[:, idx_r, idx_c]
    vg = v.reshape(bsz, rows, GRID_W, h, hd)[:, idx_r, idx_c]
    scores = jnp.einsum('birjchd,bijuvhd->bijhrcuv', qg, kg, preferred_element_type=jnp.float32) * (hd ** -0.5) + bias[None]
    p = jax.nn.softmax(scores, axis=(-2, -1))
    o = jnp.einsum('bijhrcuv,bijuvhd->birjchd', p, vg.astype(jnp.float32))
    return o.reshape(bsz, s, h, hd).astype(v.dtype)


def hybrid_mixer(h, w_in, qk_gain, a_sink, lam_re, lam_im, log_step, b_re, b_im, c_re, c_im, d_skip,
                 w_glu, rel_bias, w_out, slopes):
    bsz, s, _ = h.shape
    splits = np.cumsum(PROJ_SIZES)[:-1].tolist()
    aq, ak, av, bu, cq, ck, cv, dq, dk, dv = jnp.split(h @ w_in, splits, axis=-1)

    def heads(t):
        return t.reshape(bsz, s, -1, HEAD_DIM)

    o_a, lse_a = banded_attention(rms_norm(heads(aq), qk_gain[0]), rms_norm(heads(ak), qk_gain[1]), heads(av),
                                  A_HALF_WINDOW, A_BLOCK, slopes[:A_HEADS], 1.0)
    o_a = o_a.astype(jnp.float32) * jax.nn.sigmoid(lse_a - a_sink.astype(jnp.float32))[..., None]
    y_b = s5_bidirectional(bu, lam_re, lam_im, log_step, b_re, b_im, c_re, c_im, d_skip)
    g = jax.nn.gelu(y_b)
    y_b = g * jax.nn.sigmoid(g @ w_glu.astype(jnp.float32))
    o_c = dilated_attention(rms_norm(heads(cq), qk_gain[2]), rms_norm(heads(ck), qk_gain[3]), heads(cv),
                            slopes[A_HEADS:])
    o_d = neighborhood_attention(rms_norm(heads(dq), qk_gain[4]), rms_norm(heads(dk), qk_gain[5]), heads(dv), rel_bias)
    mixed = jnp.concatenate([o_a.reshape(bsz, s, -1).astype(h.dtype), y_b.astype(h.dtype),
                             o_c.reshape(bsz, s, -1).astype(h.dtype), o_d.reshape(bsz, s, -1).astype(h.dtype)], axis=-1)
    return mixed @ w_out


def setup_inputs(seed: int = 0) -> dict:
    key = jax.random.key(seed)
    ks = jax.random.split(key, 24)
    f32 = jnp.float32

    def normal(k, shape, scale):
        return scale * jax.random.normal(k, shape, f32)

    def gain(k, shape):
        return 1.0 + normal(k, shape, 0.02)

    x = normal(ks[0], (BATCH, SEQ, D_MODEL), 1.0)
    ffn1_norm = gain(ks[1], (DEPTH, D_MODEL))
    ffn1_w_in = normal(ks[2], (DEPTH, D_MODEL, 2 * D_FF), D_MODEL ** -0.5)
    ffn1_w_out = normal(ks[3], (DEPTH, D_FF, D_MODEL), D_FF ** -0.5)
    mix_norm = gain(ks[4], (DEPTH, D_MODEL))
    w_in = normal(ks[5], (DEPTH, D_MODEL, D_IN_PROJ), D_MODEL ** -0.5)
    qk_gain = gain(ks[6], (DEPTH, 6, HEAD_DIM))
    a_sink = normal(ks[7], (DEPTH, A_HEADS), 1.0)
    s5_lam_re = -0.5 + normal(ks[8], (DEPTH, 2, S5_GROUPS, S5_STATE), 0.01)
    s5_lam_im = jnp.broadcast_to(math.pi * jnp.arange(S5_STATE, dtype=f32), (DEPTH, 2, S5_GROUPS, S5_STATE))
    s5_log_step = jax.random.uniform(ks[9], (DEPTH, 2, S5_GROUPS), f32, math.log(S5_DT_MIN), math.log(S5_DT_MAX))
    s5_b_re = normal(ks[10], (DEPTH, 2, S5_GROUPS, S5_STATE, S5_GROUP), (2 * S5_GROUP) ** -0.5)
    s5_b_im = normal(ks[11], (DEPTH, 2, S5_GROUPS, S5_STATE, S5_GROUP), (2 * S5_GROUP) ** -0.5)
    s5_c_re = normal(ks[12], (DEPTH, 2, S5_GROUPS, S5_GROUP, S5_STATE), S5_STATE ** -0.5)
    s5_c_im = normal(ks[13], (DEPTH, 2, S5_GROUPS, S5_GROUP, S5_STATE), S5_STATE ** -0.5)
    s5_d = normal(ks[14], (DEPTH, GROUP_WIDTH), 1.0)
    s5_w_glu = normal(ks[15], (DEPTH, GROUP_WIDTH, GROUP_WIDTH), GROUP_WIDTH ** -0.5)
    na_rel_bias = normal(ks[16], (DEPTH, D_HEADS, 2 * NA_ROWS - 1, 2 * NA_COLS - 1), 0.5)
    w_out = normal(ks[17], (DEPTH, D_MODEL, D_MODEL), D_MODEL ** -0.5)
    ffn2_norm = gain(ks[18], (DEPTH, D_MODEL))
    ffn2_w_in = normal(ks[19], (DEPTH, D_MODEL, 2 * D_FF), D_MODEL ** -0.5)
    ffn2_w_out = normal(ks[20], (DEPTH, D_FF, D_MODEL), D_FF ** -0.5)
    return {'x': x, 'ffn1_norm': ffn1_norm, 'ffn1_w_in': ffn1_w_in, 'ffn1_w_out': ffn1_w_out,
            'mix_norm': mix_norm, 'w_in': w_in, 'qk_gain': qk_gain, 'a_sink': a_sink,
            's5_lam_re': s5_lam_re, 's5_lam_im': s5_lam_im, 's5_log_step': s5_log_step,
            's5_b_re': s5_b_re, 's5_b_im': s5_b_im, 's5_c_re': s5_c_re, 's5_c_im': s5_c_im,
            's5_d': s5_d, 's5_w_glu': s5_w_glu, 'na_rel_bias': na_rel_bias, 'w_out': w_out,
            'ffn2_norm': ffn2_norm, 'ffn2_w_in': ffn2_w_in, 'ffn2_w_out': ffn2_w_out}


def reference(x, ffn1_norm, ffn1_w_in, ffn1_w_out, mix_norm, w_in, qk_gain, a_sink,
              s5_lam_re, s5_lam_im, s5_log_step, s5_b_re, s5_b_im, s5_c_re, s5_c_im,
              s5_d, s5_w_glu, na_rel_bias, w_out, ffn2_norm, ffn2_w_in, ffn2_w_out):
    slopes = alibi_slopes(ALIBI_HEADS)
    for layer in range(DEPTH):
        x = x + 0.5 * swiglu(rms_norm(x, ffn1_norm[layer]), ffn1_w_in[layer], ffn1_w_out[layer])
        x = x + hybrid_mixer(rms_norm(x, mix_norm[layer]), w_in[layer], qk_gain[layer], a_sink[layer],
                             s5_lam_re[layer], s5_lam_im[layer], s5_log_step[layer], s5_b_re[layer], s5_b_im[layer],
                             s5_c_re[layer], s5_c_im[layer], s5_d[layer], s5_w_glu[layer], na_rel_bias[layer],
                             w_out[layer], slopes)
        x = x + 0.5 * swiglu(rms_norm(x, ffn2_norm[layer]), ffn2_w_in[layer], ffn2_w_out[layer])
    return x
```

```python
import numpy as np
import ml_dtypes
import concourse.bass as bass
import concourse.mybir as mybir
from concourse.bass_utils import run_bass_kernel_spmd

F32 = mybir.dt.float32
BF16 = mybir.dt.bfloat16
I32 = mybir.dt.int32
AF = mybir.ActivationFunctionType
ALU = mybir.AluOpType
NPBF = ml_dtypes.bfloat16

NCORES = 8
D = 1024
DFF = 2816
NFC = DFF // 128
B = 2
S = 16384
DEPTH = 4
TPC = B * S // NCORES
TT = 512
NT = TPC // TT
WSLOT = 4096
NSLOT = 5
EPS = 1e-6
NEG = -30000.0


class Buf:
    __slots__ = ("w", "r")

    def __init__(self):
        self.w = None
        self.r = {}


class Eng:
    def __init__(self, name, sem, selfsync):
        self.name = name
        self.sem = sem
        self.cnt = 0
        self.ops = []
        self.waited = {}
        self.selfsync = selfsync


class DSem:
    def __init__(self, sem):
        self.sem = sem
        self.cnt = 0


class Prog:
    def __init__(self, nc, sems):
        self.nc = nc
        self.pe = Eng("tensor", sems["tensor"], False)
        self.act = Eng("scalar", sems["scalar"], True)
        self.dve = Eng("vector", sems["vector"], True)
        self.pool = Eng("gpsimd", sems["gpsimd"], True)
        self.sp = Eng("sync", sems["sync"], False)
        self.engs = [self.pe, self.act, self.dve, self.pool, self.sp]

    def _wait(self, eng, tok):
        sem, val = tok
        k = id(sem)
        if eng.waited.get(k, 0) >= val:
            return
        eng.waited[k] = val
        eng.ops.append(lambda h, sm=sem, v=val: h.wait_ge(sm, v))

    def _deps(self, eng, reads, writes):
        for b in reads:
            if b.w is not None:
                if b.w[0] is eng.sem and not eng.selfsync:
                    continue
                self._wait(eng, b.w)
        for b in writes:
            for sem, val in b.r.values():
                if sem is eng.sem and not eng.selfsync:
                    continue
                self._wait(eng, (sem, val))
            if b.w is not None:
                if b.w[0] is eng.sem and not eng.selfsync:
                    continue
                self._wait(eng, b.w)

    def _mark(self, tok, reads, writes):
        for b in reads:
            k = id(tok[0])
            if k not in b.r or b.r[k][1] < tok[1]:
                b.r[k] = tok
        for b in writes:
            b.w = tok
            b.r = {}

    def op(self, eng, fn, reads=(), writes=()):
        self._deps(eng, reads, writes)
        eng.cnt += 1
        tok = (eng.sem, eng.cnt)
        eng.ops.append(lambda h, f=fn, sm=eng.sem: f(h).then_inc(sm, 1))
        self._mark(tok, reads, writes)
        return tok

    def dma(self, eng, dsem, out, in_, reads=(), writes=()):
        self._deps(eng, reads, writes)
        dsem.cnt += 16
        tok = (dsem.sem, dsem.cnt)
        eng.ops.append(lambda h, o=out, i=in_, sm=dsem.sem: h.dma_start(out=o, in_=i).then_inc(sm, 16))
        self._mark(tok, reads, writes)
        return tok

    def finish(self, eng, toks):
        for t in toks:
            self._wait(eng, t)

    def replay(self, block):
        for e in self.engs:
            def body(h, e=e):
                for f in e.ops:
                    f(h)
            getattr(block, e.name)(body)


class Env:
    def __init__(self, nc, stack):
        self.nc = nc
        self.stack = stack
        self.nsem = 0
        sems = {n: self.sem("p_" + n) for n in ("tensor", "scalar", "vector", "gpsimd", "sync")}
        self.p = Prog(nc, sems)

    def sem(self, name):
        self.nsem += 1
        return self.stack.enter_context(self.nc.semaphore(name))

    def dsem(self, name):
        return DSem(self.sem(name))

    def sb(self, name, shape, dt):
        return self.stack.enter_context(self.nc.sbuf_tensor(name, list(shape), dt))

    def ps(self, name, shape, dt=F32):
        return self.stack.enter_context(self.nc.psum_tensor(name, list(shape), dt))


def dram_in(nc, name, shape, dt):
    return nc.dram_tensor(name, list(shape), dt, kind="ExternalInput").ap()


def dram_out(nc, name, shape, dt):
    return nc.dram_tensor(name, list(shape), dt, kind="ExternalOutput").ap()


def build_w(nblk):
    from contextlib import ExitStack
    nc = bass.Bass("TRN2", target_bir_lowering=False)
    rows = nblk * 256
    wf = dram_in(nc, "wf", [rows, 2048], F32)
    wb = dram_out(nc, "wb", [rows, 2048], BF16)
    with ExitStack() as st:
        env = Env(nc, st)
        p = env.p
        ds = env.dsem("d")
        toks = []
        for b in range(nblk):
            toks.append(p.dma(p.pool, ds, wb[b * 256:(b + 1) * 256, :], wf[b * 256:(b + 1) * 256, :]))
        p.finish(p.pool, toks[-1:])
        with nc.Block() as block:
            p.replay(block)
    return nc


def _pad_block(a):
    a = a.reshape(128, -1)
    out = np.zeros((128, WSLOT), np.float32)
    out[:, : a.shape[1]] = a
    return out


def ffn_blocks(w_in, w_out):
    blocks = []
    wi = w_in.reshape(8, 128, 2, NFC, 128)
    for blk in range(NFC // 2):
        sub = wi[:, :, :, 2 * blk:2 * blk + 2, :]
        blocks.append(_pad_block(sub.transpose(1, 3, 2, 0, 4)))
    wo = w_out.reshape(NFC, 128, 8, 128)
    for dc in range(8):
        blocks.append(_pad_block(wo[:, :, dc, :].transpose(1, 0, 2)))
    return blocks


def proj_blocks(w, nmc):
    nk = w.shape[0] // 128
    wr = w.reshape(nk, 128, nmc, 128)
    blocks = []
    for b0 in range(0, nmc, 4):
        sub = wr[:, :, b0:b0 + 4, :]
        blocks.append(_pad_block(sub.transpose(1, 2, 0, 3)))
    return blocks


def stream_for_launch(j, inp):
    blocks = []
    if j >= 1:
        l = j - 1
        blocks += proj_blocks(inp["s5_w_glu"][l], 2)
        blocks += proj_blocks(inp["w_out"][l], 8)
        blocks += ffn_blocks(inp["ffn2_w_in"][l], inp["ffn2_w_out"][l])
    if j <= DEPTH - 1:
        l = j
        blocks += ffn_blocks(inp["ffn1_w_in"][l], inp["ffn1_w_out"][l])
        blocks += proj_blocks(inp["w_in"][l], 18)
    return blocks


QK_CHUNK_GAIN = {0: 0, 1: 0, 2: 1, 6: 2, 7: 2, 8: 3, 9: 3, 12: 4, 13: 4, 14: 5, 15: 5}


def cols_for_launch(j, inp):
    c = np.ones((128, 44), np.float32)
    if j >= 1:
        l = j - 1
        c[:, 0:8] = inp["ffn2_norm"][l].reshape(8, 128).T
        c[:, 42:44] = inp["s5_d"][l].reshape(2, 128).T
    if j <= DEPTH - 1:
        l = j
        c[:, 8:16] = inp["ffn1_norm"][l].reshape(8, 128).T
        c[:, 16:24] = inp["mix_norm"][l].reshape(8, 128).T
        for mc, gi in QK_CHUNK_GAIN.items():
            c[:, 24 + mc] = np.tile(inp["qk_gain"][l, gi], 2)
    return c


def build_t(has_tail, has_head):
    from contextlib import ExitStack
    nc = bass.Bass("TRN2", target_bir_lowering=False)
    nblk = (22 if has_tail else 0) + (24 if has_head else 0)
    xin = dram_in(nc, "xin", [D, TPC], F32)
    wst = dram_in(nc, "wst", [nblk, 128, WSLOT], BF16)
    cols_d = dram_in(nc, "cols", [128, 44], F32)
    cst_d = dram_in(nc, "cst", [128, 256], BF16)
    if has_tail:
        o_d = dram_in(nc, "o_in", [768, TPC], BF16)
        yf_d = dram_in(nc, "yf_in", [256, TPC], F32)
        yb_d = dram_in(nc, "yb_in", [256, TPC], F32)
        u_d = dram_in(nc, "u_in", [256, TPC], BF16)
    xout = dram_out(nc, "xout", [D, TPC], F32)
    if has_head:
        pout_d = dram_out(nc, "pout", [2304, TPC], BF16)

    with ExitStack() as st:
        env = Env(nc, st)
        p = env.p
        x_sb = [env.sb(f"x{i}", [128, 8, TT], F32) for i in range(2)]
        h_sb = env.sb("h", [128, 8, TT], BF16)
        sq_sb = env.sb("sq", [128, 8, TT], BF16)
        a_sb = env.sb("a", [128, NFC, TT], BF16)
        rs_sb = [env.sb(f"rs{i}", [128, TT], F32) for i in range(2)]
        s_sb = [env.sb(f"s{i}", [128, TT], F32) for i in range(2)]
        ring = env.sb("ring", [128, NSLOT, WSLOT], BF16)
        cols = env.sb("colsb", [128, 44], F32)
        cst = env.sb("cstb", [128, 256], BF16)
        epsc = env.sb("epsc", [128, 1], F32)
        if has_head:
            po_sb = env.sb("po", [128, 18, TT], BF16)
        if has_tail:
            mx_sb = env.sb("mx", [128, 8, TT], BF16)
            yf_sb = env.sb("yf", [128, 2, TT], F32)
            yb_sb = env.sb("yb", [128, 2, TT], F32)
            u_sb = env.sb("u", [128, 2, TT], BF16)
            g_sb = env.sb("g", [128, 2, TT], F32)
            t_sb = env.sb("tg", [128, 2, TT], F32)
            gb_sb = env.sb("gb", [128, 2, TT], BF16)
        bank = [env.ps(f"bk{i}", [128, TT]) for i in range(8)]
        b_x = [[Buf() for _ in range(8)] for _ in range(2)]
        b_h = [Buf() for _ in range(8)]
        b_sq = [Buf() for _ in range(8)]
        b_a = [Buf() for _ in range(NFC)]
        b_rs = [Buf(), Buf()]
        b_s = [Buf(), Buf()]
        b_ring = [Buf() for _ in range(NSLOT)]
        b_bank = [Buf() for _ in range(8)]
        b_const = Buf()
        b_po = [Buf() for _ in range(18)]
        b_mx = [Buf() for _ in range(8)]
        b_y = Buf()
        b_g = Buf()
        b_tg = Buf()
        b_gb = Buf()
        ds_ring = [env.dsem(f"dr{i}") for i in range(NSLOT)]
        ds_x = [env.dsem(f"dx{i}") for i in range(2)]
        ds_c = env.dsem("dc")
        ds_out = env.dsem("dout")
        ds_pout = env.dsem("dpo")
        ds_tail = env.dsem("dtl")

        ones_mean = cst[:, 0:128]
        blk_ones = cst[:, 128:256]

        p.dma(p.pool, ds_c, cols[:], cols_d[:, :], writes=[b_const])
        p.dma(p.pool, ds_c, cst[:], cst_d[:, :], writes=[b_const])
        p.op(p.dve, lambda h: h.memset(epsc[:], EPS), writes=[b_const])

        wstate = {"n": 0}

        def next_block(tile_idx, bidx):
            n = wstate["n"]
            wstate["n"] = n + 1
            s = n % NSLOT
            p.dma(p.sp, ds_ring[s], ring[:, s, :], wst[bidx, :, :], writes=[b_ring[s]])
            return s

        order = []
        per_tile = list(range(nblk))
        PREF = NSLOT - 1
        issued = {"k": 0}
        total_blocks = NT * nblk

        def ensure_issued(upto):
            while issued["k"] < min(upto, total_blocks):
                k = issued["k"]
                next_block(k // nblk, k % nblk)
                issued["k"] = k + 1

        used = {"k": 0}

        def take_block():
            k = used["k"]
            used["k"] = k + 1
            ensure_issued(k + 1)
            s = k % NSLOT
            return s

        def release_prefetch():
            ensure_issued(used["k"] + PREF)

        def load_x(t, xb):
            for c in range(8):
                pass
            src = xin[:, t * TT:(t + 1) * TT].rearrange("(c p) t -> p c t", p=128)
            p.dma(p.pool, ds_x[xb], x_sb[xb][:], src, writes=b_x[xb])

        def rmsnorm_to_h(xb, gcol0):
            xs = x_sb[xb]
            for c in range(8):
                p.op(p.act, lambda h, c=c: h.activation(out=sq_sb[:, c, :], in_=xs[:, c, :], func=AF.Square),
                     reads=[b_x[xb][c]], writes=[b_sq[c]])
            for c in range(8):
                p.op(p.pe, lambda h, c=c: h.matmul(bank[6][:], lhsT=ones_mean, rhs=sq_sb[:, c, :],
                                                   start=(c == 0), stop=(c == 7)),
                     reads=[b_sq[c], b_const], writes=[b_bank[6]])
            p.op(p.act, lambda h: h.activation(out=rs_sb[0][:], in_=bank[6][:], func=AF.Sqrt, bias=epsc[:, 0:1]),
                 reads=[b_bank[6], b_const], writes=[b_rs[0]])
            p.op(p.dve, lambda h: h.reciprocal(out=rs_sb[1][:], in_=rs_sb[0][:]), reads=[b_rs[0]], writes=[b_rs[1]])
            for c in range(8):
                p.op(p.dve, lambda h, c=c: h.scalar_tensor_tensor(
                    out=h_sb[:, c, :], in0=xs[:, c, :], scalar=cols[:, gcol0 + c:gcol0 + c + 1],
                    in1=rs_sb[1][:], op0=ALU.mult, op1=ALU.mult),
                    reads=[b_x[xb][c], b_rs[1], b_const], writes=[b_h[c]])

        def ffn(xb, gcol0):
            xs = x_sb[xb]
            rmsnorm_to_h(xb, gcol0)
            for blk in range(NFC // 2):
                s = take_block()
                wv = ring[:, s, :].rearrange("p (f g k m) -> p f g k m", f=2, g=2, k=8)
                for fcl in range(2):
                    fc = 2 * blk + fcl
                    pb = fc % 2
                    G, U = bank[2 * pb], bank[2 * pb + 1]
                    bG, bU = b_bank[2 * pb], b_bank[2 * pb + 1]
                    for kc in range(8):
                        p.op(p.pe, lambda h, wv=wv, kc=kc, fcl=fcl, G=G: h.matmul(
                            G[:], lhsT=wv[:, fcl, 0, kc, :], rhs=h_sb[:, kc, :], start=(kc == 0), stop=(kc == 7)),
                            reads=[b_ring[s], b_h[kc]], writes=[bG])
                    for kc in range(8):
                        p.op(p.pe, lambda h, wv=wv, kc=kc, fcl=fcl, U=U: h.matmul(
                            U[:], lhsT=wv[:, fcl, 1, kc, :], rhs=h_sb[:, kc, :], start=(kc == 0), stop=(kc == 7)),
                            reads=[b_ring[s], b_h[kc]], writes=[bU])
                    p.op(p.act, lambda h, G=G, pb=pb: h.activation(out=s_sb[pb][:], in_=G[:], func=AF.Silu),
                         reads=[bG], writes=[b_s[pb]])
                    p.op(p.dve, lambda h, U=U, pb=pb, fc=fc: h.tensor_tensor(
                        out=a_sb[:, fc, :], in0=U[:], in1=s_sb[pb][:], op=ALU.mult),
                        reads=[bU, b_s[pb]], writes=[b_a[fc]])
                release_prefetch()
            for dc in range(8):
                s = take_block()
                wv = ring[:, s, 0:NFC * 128].rearrange("p (f m) -> p f m", f=NFC)
                O = bank[4 + dc % 2]
                bO = b_bank[4 + dc % 2]
                for fc in range(NFC):
                    p.op(p.pe, lambda h, wv=wv, fc=fc, O=O: h.matmul(
                        O[:], lhsT=wv[:, fc, :], rhs=a_sb[:, fc, :], start=(fc == 0), stop=(fc == NFC - 1)),
                        reads=[b_ring[s], b_a[fc]], writes=[bO])
                p.op(p.dve, lambda h, dc=dc, O=O: h.scalar_tensor_tensor(
                    out=xs[:, dc, :], in0=O[:], scalar=0.5, in1=xs[:, dc, :], op0=ALU.mult, op1=ALU.add),
                    reads=[bO, b_x[xb][dc]], writes=[b_x[xb][dc]])
                release_prefetch()

        def head_proj(xb, t):
            xs = x_sb[xb]
            rmsnorm_to_h(xb, 16)
            for b0 in range(5):
                s = take_block()
                wv = ring[:, s, :].rearrange("p (c k m) -> p c k m", c=4, k=8)
                for mcl in range(4):
                    mc = 4 * b0 + mcl
                    if mc >= 18:
                        break
                    P = bank[mc % 4]
                    bP = b_bank[mc % 4]
                    for kc in range(8):
                        p.op(p.pe, lambda h, wv=wv, kc=kc, mcl=mcl, P=P: h.matmul(
                            P[:], lhsT=wv[:, mcl, kc, :], rhs=h_sb[:, kc, :], start=(kc == 0), stop=(kc == 7)),
                            reads=[b_ring[s], b_h[kc]], writes=[bP])
                    if mc in QK_CHUNK_GAIN:
                        q = mc % 2
                        sqb = s_sb[q]
                        p.op(p.act, lambda h, P=P, q=q: h.activation(out=sq_sb[:, q, :], in_=P[:], func=AF.Square),
                             reads=[bP], writes=[b_sq[q]])
                        p.op(p.pe, lambda h, q=q: h.matmul(bank[7][:], lhsT=blk_ones, rhs=sq_sb[:, q, :],
                                                           start=True, stop=True),
                             reads=[b_sq[q], b_const], writes=[b_bank[7]])
                        p.op(p.act, lambda h: h.activation(out=rs_sb[0][:], in_=bank[7][:], func=AF.Sqrt,
                                                           bias=epsc[:, 0:1]),
                             reads=[b_bank[7], b_const], writes=[b_rs[0]])
                        p.op(p.dve, lambda h: h.reciprocal(out=rs_sb[1][:], in_=rs_sb[0][:]),
                             reads=[b_rs[0]], writes=[b_rs[1]])
                        p.op(p.dve, lambda h, P=P, mc=mc: h.scalar_tensor_tensor(
                            out=po_sb[:, mc, :], in0=P[:], scalar=cols[:, 24 + mc:25 + mc], in1=rs_sb[1][:],
                            op0=ALU.mult, op1=ALU.mult),
                            reads=[bP, b_rs[1], b_const], writes=[b_po[mc]])
                    else:
                        p.op(p.act, lambda h, P=P, mc=mc: h.activation(out=po_sb[:, mc, :], in_=P[:], func=AF.Copy),
                             reads=[bP], writes=[b_po[mc]])
                release_prefetch()
            dst = pout_d[:, t * TT:(t + 1) * TT].rearrange("(c p) t -> p c t", p=128)
            return p.dma(p.pool, ds_pout, dst, po_sb[:], reads=b_po)

        def tail(xb, t):
            xs = x_sb[xb]
            tsl = slice(t * TT, (t + 1) * TT)
            for (r0, c0) in ((0, 0), (256, 4), (512, 6)):
                src = o_d[r0:r0 + 256, tsl].rearrange("(c p) t -> p c t", p=128)
                p.dma(p.pool, ds_tail, mx_sb[:, c0:c0 + 2, :], src, writes=[b_mx[c0], b_mx[c0 + 1]])
            p.dma(p.pool, ds_tail, yf_sb[:], yf_d[:, tsl].rearrange("(c p) t -> p c t", p=128), writes=[b_y])
            p.dma(p.pool, ds_tail, yb_sb[:], yb_d[:, tsl].rearrange("(c p) t -> p c t", p=128), writes=[b_y])
            ltok = p.dma(p.pool, ds_tail, u_sb[:], u_d[:, tsl].rearrange("(c p) t -> p c t", p=128), writes=[b_y])
            for bb in (b_mx[0], b_mx[1], b_mx[4], b_mx[5], b_mx[6], b_mx[7], b_y):
                bb.w = ltok
            p.op(p.dve, lambda h: h.tensor_tensor(out=yf_sb[:], in0=yf_sb[:], in1=yb_sb[:], op=ALU.add),
                 reads=[b_y], writes=[b_y])
            for c in range(2):
                p.op(p.dve, lambda h, c=c: h.scalar_tensor_tensor(
                    out=yf_sb[:, c, :], in0=u_sb[:, c, :], scalar=cols[:, 42 + c:43 + c], in1=yf_sb[:, c, :],
                    op0=ALU.mult, op1=ALU.add), reads=[b_y, b_const], writes=[b_y])
            p.op(p.dve, lambda h: h.tensor_tensor(out=t_sb[:], in0=yf_sb[:], in1=yf_sb[:], op=ALU.mult),
                 reads=[b_y], writes=[b_tg])
            p.op(p.dve, lambda h: h.tensor_scalar(out=t_sb[:], in0=t_sb[:], scalar1=0.044715, scalar2=1.0,
                                                  op0=ALU.mult, op1=ALU.add), reads=[b_tg], writes=[b_tg])
            p.op(p.dve, lambda h: h.tensor_tensor(out=t_sb[:], in0=t_sb[:], in1=yf_sb[:], op=ALU.mult),
                 reads=[b_tg, b_y], writes=[b_tg])
            p.op(p.act, lambda h: h.activation(out=t_sb[:], in_=t_sb[:], func=AF.Sigmoid, scale=1.5957691216057308),
                 reads=[b_tg], writes=[b_tg])
            p.op(p.dve, lambda h: h.tensor_tensor(out=g_sb[:], in0=t_sb[:], in1=yf_sb[:], op=ALU.mult),
                 reads=[b_tg, b_y], writes=[b_g])
            p.op(p.act, lambda h: h.activation(out=gb_sb[:], in_=g_sb[:], func=AF.Copy), reads=[b_g], writes=[b_gb])
            s = take_block()
            wv = ring[:, s, 0:512].rearrange("p (c k m) -> p c k m", c=2, k=2)
            for mc in range(2):
                Z = bank[mc]
                for kc in range(2):
                    p.op(p.pe, lambda h, wv=wv, mc=mc, kc=kc, Z=Z: h.matmul(
                        Z[:], lhsT=wv[:, mc, kc, :], rhs=gb_sb[:, kc, :], start=(kc == 0), stop=(kc == 1)),
                        reads=[b_ring[s], b_gb], writes=[b_bank[mc]])
                p.op(p.act, lambda h, mc=mc, Z=Z: h.activation(out=s_sb[mc][:], in_=Z[:], func=AF.Sigmoid),
                     reads=[b_bank[mc]], writes=[b_s[mc]])
                p.op(p.dve, lambda h, mc=mc: h.tensor_tensor(out=mx_sb[:, 2 + mc, :], in0=g_sb[:, mc, :],
                                                             in1=s_sb[mc][:], op=ALU.mult),
                     reads=[b_g, b_s[mc]], writes=[b_mx[2 + mc]])
            release_prefetch()
            for b0 in range(2):
                s = take_block()
                wv = ring[:, s, :].rearrange("p (c k m) -> p c k m", c=4, k=8)
                for mcl in range(4):
                    dc = 4 * b0 + mcl
                    O = bank[4 + dc % 2]
                    bO = b_bank[4 + dc % 2]
                    for kc in range(8):
                        p.op(p.pe, lambda h, wv=wv, kc=kc, mcl=mcl, O=O: h.matmul(
                            O[:], lhsT=wv[:, mcl, kc, :], rhs=mx_sb[:, kc, :], start=(kc == 0), stop=(kc == 7)),
                            reads=[b_ring[s], b_mx[kc]], writes=[bO])
                    p.op(p.dve, lambda h, dc=dc, O=O: h.tensor_tensor(
                        out=xs[:, dc, :], in0=O[:], in1=xs[:, dc, :], op=ALU.add),
                        reads=[bO, b_x[xb][dc]], writes=[b_x[xb][dc]])
                release_prefetch()

        out_toks = []
        load_x(0, 0)
        ensure_issued(PREF)
        for t in range(NT):
            xb = t % 2
            if t + 1 < NT:
                load_x(t + 1, 1 - xb)
            if has_tail:
                tail(xb, t)
                ffn(xb, 0)
            if has_head:
                ffn(xb, 8)
                out_toks.append(head_proj(xb, t))
            dst = xout[:, t * TT:(t + 1) * TT].rearrange("(c p) t -> p c t", p=128)
            out_toks.append(p.dma(p.pool, ds_out, dst, x_sb[xb][:], reads=b_x[xb]))
        p.finish(p.pool, [(ds_out.sem, ds_out.cnt), (ds_pout.sem, ds_pout.cnt)] if has_head
                 else [(ds_out.sem, ds_out.cnt)])
        with nc.Block() as block:
            p.replay(block)
    return nc


PAD = 1024
SP_ = S + 2 * PAD
SBLK = 2048
NVT = 405
GRID_W = 64


def fap(ap2d, off, pat):
    return bass.AP(ap2d.tensor, ap2d.offset + off, [list(ap2d.ap[0])] + [list(x) for x in pat])


def attn_plan():
    plans = []
    vt = [(PAD + 128 * t, [[1, 128]]) for t in range(128)]
    qb = []
    for b in range(128):
        kts = []
        for dt_, bi in ((-1, 0), (0, 1), (1, 2)):
            t = b + dt_
            if 0 <= t < 128:
                kts.append((PAD + 128 * t, [[1, 128]], t, bi))
        sb = (128 * b) // SBLK
        qb.append(dict(q=(128 * b, [[1, 128]]), kts=kts, dst=(128 * b - sb * SBLK, [[1, 128]]), add=False, sb=sb))
    plans.append((vt, qb))
    vt = []
    vidx = {}
    for ci, d in enumerate((1, 4, 16)):
        L = S // d
        for rho in range(d):
            for bp in range(L // 128 + 1):
                vidx[(ci, rho, bp)] = len(vt)
                vt.append((PAD + d * 64 * (2 * bp - 1) + rho, [[d, 128]]))
    assert len(vt) == NVT
    qb = []
    for sb in range(S // SBLK):
        for ci, d in enumerate((1, 4, 16)):
            L = S // d
            nb = L // 128
            for rho in range(d):
                for b in range(nb):
                    t0 = d * 128 * b + rho
                    if t0 // SBLK != sb:
                        continue
                    kts = []
                    for bp, base in ((b, 0), (b + 1, 1)):
                        bi = 4 * ci + base
                        if base == 0 and b == 0:
                            bi = 4 * ci + 2
                        if base == 1 and b == nb - 1:
                            bi = 4 * ci + 3
                        off, pat = vt[vidx[(ci, rho, bp)]]
                        kts.append((off, pat, vidx[(ci, rho, bp)], bi))
                    qb.append(dict(q=(t0, [[d, 128]]), kts=kts, dst=(t0 - sb * SBLK, [[d, 128]]), add=(ci > 0), sb=sb))
    plans.append((vt, qb))
    vt = [(PAD + 128 * t, [[1, 128]]) for t in range(128)]
    qb = []
    for i in range(32):
        rs = min(max(8 * i - 4, 0), 240)
        icls = 0 if i == 0 else (2 if i == 31 else 1)
        for j in range(4):
            kts = []
            for m in range(8):
                t = rs // 2 + m
                kts.append((PAD + 128 * t, [[1, 128]], t, (icls * 4 + j) * 8 + m))
            t0 = 8 * i * GRID_W + 16 * j
            sb = t0 // SBLK
            qb.append(dict(q=((i * 4 + j) * 128, [[1, 128]]), kts=kts, dst=(t0 - sb * SBLK, [[GRID_W, 8], [1, 16]]),
                           add=False, sb=sb))
    plans.append((vt, qb))
    return plans


def attn_bias_consts(hd):
    slopes = 2.0 ** (-np.arange(1, 9, dtype=np.float64))
    kk = np.arange(128)[:, None]
    qq = np.arange(128)[None, :]
    A = np.zeros((3, 128, 128), np.float32)
    for bi, sh in enumerate((-128, 0, 128)):
        rel = kk + sh - qq
        A[bi] = np.where(np.abs(rel) <= 128, -slopes[hd] * np.abs(rel) * 8.0, NEG * 8)
    C = np.zeros((12, 128, 128), np.float32)
    for ci, d in enumerate((1, 4, 16)):
        for base, sh in ((0, -64), (1, 64)):
            rel = kk + sh - qq
            m = np.where(np.abs(rel) <= 64, -slopes[4 + hd] * d * np.abs(rel) * 8.0, NEG * 8)
            C[4 * ci + base] = m
        first = C[4 * ci + 0].copy()
        first[:64, :] = NEG * 8
        C[4 * ci + 2] = first
        last = C[4 * ci + 1].copy()
        last[64:, :] = NEG * 8
        C[4 * ci + 3] = last
    ri = np.zeros((96, 128, 128), np.int64)
    cidx = np.zeros((96, 128, 128), np.int64)
    Dm = np.zeros((96, 128, 128), np.float32)
    rows = S // GRID_W
    for icls, i in enumerate((0, 5, 31)):
        rs = min(max(8 * i - 4, 0), rows - 16)
        for j in range(4):
            for m in range(8):
                idx = (icls * 4 + j) * 8 + m
                krow = (rs + 2 * m + np.arange(128) // 64)[:, None]
                kcol = (np.arange(128) % 64)[:, None]
                qrow = (8 * i + np.arange(128) // 16)[None, :]
                qcol = (16 * j + np.arange(128) % 16)[None, :]
                wr = np.clip(qrow - 4, 0, rows - 8)
                wc = np.clip(qcol - 8, 0, GRID_W - 16)
                valid = (krow >= wr) & (krow < wr + 8) & (kcol >= wc) & (kcol < wc + 16)
                ri[idx] = np.clip(krow - qrow + 7, 0, 14) + 0 * qcol
                cidx[idx] = np.clip(kcol - qcol + 15, 0, 30) + 0 * qrow
                Dm[idx] = np.where(valid, 0.0, NEG * 8)
    return A.astype(NPBF), C.astype(NPBF), (ri, cidx), Dm.astype(NPBF)


def vt_indices(ph):
    vts, _ = attn_plan()[ph]
    idx = np.zeros((len(vts), 128), np.int64)
    for n, (off, pat) in enumerate(vts):
        st_, cnt = pat[0]
        assert len(pat) == 1 and cnt == 128
        idx[n] = off - PAD + st_ * np.arange(128)
    idx[(idx < 0) | (idx >= S)] = -1
    return idx


def build_attn(phases=(0, 1, 2)):
    from contextlib import ExitStack
    nc = bass.Bass("TRN2", target_bir_lowering=False)
    qk_d = dram_in(nc, "qk", [6, 64, S], BF16)
    vt_d = dram_in(nc, "vt", [128, 128 + NVT + 128, 64], BF16)
    bA_d = dram_in(nc, "bA", [128, 3, 128], BF16)
    bC_d = dram_in(nc, "bC", [128, 12, 128], BF16)
    bDg_d = dram_in(nc, "bDg", [128, 96, 128], BF16)
    bDm_d = dram_in(nc, "bDm", [128, 96, 128], BF16)
    sink_d = dram_in(nc, "sink", [128, 1], F32)
    idb_d = dram_in(nc, "idb", [128, 128], BF16)
    idf_d = dram_in(nc, "idf", [128, 128], F32)
    o_d = dram_out(nc, "o_out", [3, 64, S], BF16)
    plans = attn_plan()
    vbase = [0, 128, 128 + NVT]
    with ExitStack() as st:
        env = Env(nc, st)
        p = env.p
        qT = env.sb("qT", [64, S], BF16)
        kT = env.sb("kT", [64, SP_], BF16)
        Vt = env.sb("Vt", [128, NVT, 65], BF16)
        bias = env.sb("bias", [128, 96, 128], BF16)
        idb = env.sb("idbs", [128, 128], BF16)
        idf = env.sb("idfs", [128, 128], F32)
        es = env.sb("es", [128, 1], F32)
        Oacc = env.sb("Oacc", [64, 2, SBLK], F32)
        ost = env.sb("ost", [64, SBLK], BF16)
        Pt = [env.sb(f"Pt{i}", [128, 8, 128], BF16) for i in range(2)]
        Osb = [env.sb(f"Osb{i}", [128, 2, 64], F32) for i in range(2)]
        Sps = [env.ps(f"Sps{i}", [128, 8, 128]) for i in range(2)]
        Ops_t = env.ps("Ops", [128, 2, 65])
        Ops = [Ops_t[:, 0, :], Ops_t[:, 1, :]]
        OTps = [env.ps(f"OTps{i}", [64, 2, 128]) for i in range(2)]
        b_q, b_k, b_vt, b_bias, b_c = Buf(), Buf(), Buf(), Buf(), Buf()
        b_oacc, b_ost = Buf(), Buf()
        b_pt = [Buf(), Buf()]
        b_osb = [Buf(), Buf()]
        b_sps = [Buf(), Buf()]
        b_ops = [Buf(), Buf()]
        b_otps = [Buf(), Buf()]
        ds_c = env.dsem("dc")
        ds_q = env.dsem("dq")
        ds_b = env.dsem("db")
        ds_v = env.dsem("dv")
        ds_o = env.dsem("do")

        p.dma(p.pool, ds_c, idb[:], idb_d[:, :], writes=[b_c])
        p.dma(p.pool, ds_c, idf[:], idf_d[:, :], writes=[b_c])
        p.dma(p.pool, ds_c, es[:], sink_d[:, :], writes=[b_c])
        p.op(p.act, lambda h: h.activation(out=es[:], in_=es[:], func=AF.Exp), reads=[b_c], writes=[b_c])
        p.op(p.dve, lambda h: h.memset(kT[:, 0:PAD], 0.0), writes=[b_k])
        p.op(p.dve, lambda h: h.memset(kT[:, PAD + S:SP_], 0.0), writes=[b_k])

        for ph in phases:
            vts, qbs = plans[ph]
            lt = p.dma(p.sp, ds_q, qT[:], qk_d[2 * ph + 0, :, :], writes=[b_q])
            lt = p.dma(p.sp, ds_q, kT[:, PAD:PAD + S], qk_d[2 * ph + 1, :, :], writes=[b_k])
            b_q.w = lt
            b_k.w = lt
            if ph == 0:
                p.dma(p.pool, ds_b, bias[:, 0:3, :], bA_d[:, :, :], writes=[b_bias])
            elif ph == 1:
                p.dma(p.pool, ds_b, bias[:, 0:12, :], bC_d[:, :, :], writes=[b_bias])
            else:
                p.dma(p.pool, ds_b, bias[:], bDg_d[:, :, :], writes=[b_bias])
                btv = Vt[:].rearrange("p a b -> p (a b)")[:, 0:96 * 128].rearrange("p (a c) -> p a c", a=96)
                p.dma(p.pool, ds_b, btv, bDm_d[:, :, :], writes=[b_vt])
                for c8 in range(4):
                    p.op(p.dve, lambda h, c8=c8, btv=btv: h.scalar_tensor_tensor(
                        out=bias[:, 24 * c8:24 * c8 + 24, :], in0=bias[:, 24 * c8:24 * c8 + 24, :], scalar=8.0,
                        in1=btv[:, 24 * c8:24 * c8 + 24, :], op0=ALU.mult, op1=ALU.add),
                        reads=[b_bias, b_vt], writes=[b_bias])
            p.op(p.dve, lambda h: h.memset(Vt[:], 1.0), writes=[b_vt])
            nv = len(vts)
            lv = None
            for g0 in range(0, nv, 64):
                n = min(64, nv - g0)
                lv = p.dma(p.sp, ds_v, Vt[:, g0:g0 + n, 0:64], vt_d[:, vbase[ph] + g0:vbase[ph] + g0 + n, :], writes=[b_vt])
            b_vt.w = lv
            nq = len(qbs)

            def s1(i):
                qb = qbs[i]
                bi_ = i % 2
                qo, qp = qb["q"]
                for kt, (ko, kp, vi, bidx) in enumerate(qb["kts"]):
                    p.op(p.pe, lambda h, bi_=bi_, kt=kt, ko=ko, kp=kp, qo=qo, qp=qp: h.matmul(
                        Sps[bi_][:, kt, :], lhsT=fap(kT[:], ko, kp), rhs=fap(qT[:], qo, qp), start=True, stop=False),
                        reads=[b_k, b_q], writes=[b_sps[bi_]])
                    p.op(p.pe, lambda h, bi_=bi_, kt=kt, bidx=bidx: h.matmul(
                        Sps[bi_][:, kt, :], lhsT=idb[:], rhs=bias[:, bidx, :], start=False, stop=True),
                        reads=[b_bias, b_c], writes=[b_sps[bi_]])
                nk = len(qb["kts"])
                for k0 in range(0, nk, 4):
                    k1 = min(nk, k0 + 4)
                    p.op(p.act, lambda h, bi_=bi_, k0=k0, k1=k1: h.activation(
                        out=Pt[bi_][:, k0:k1, :], in_=Sps[bi_][:, k0:k1, :], func=AF.Exp, scale=0.125),
                        reads=[b_sps[bi_]], writes=[b_pt[bi_]])

            def s2(i):
                qb = qbs[i]
                bi_ = i % 2
                nk = len(qb["kts"])
                for kt, (ko, kp, vi, bidx) in enumerate(qb["kts"]):
                    p.op(p.pe, lambda h, bi_=bi_, kt=kt, vi=vi, nk=nk: h.matmul(
                        Ops[bi_], lhsT=Pt[bi_][:, kt, :], rhs=Vt[:, vi, :], start=(kt == 0), stop=(kt == nk - 1)),
                        reads=[b_pt[bi_], b_vt], writes=[b_ops[bi_]])
                p.op(p.dve, lambda h, bi_=bi_: h.tensor_copy(out=Osb[bi_][:, 0, :], in_=Ops[bi_][:, 0:64]),
                     reads=[b_ops[bi_]], writes=[b_osb[bi_]])
                p.op(p.dve, lambda h, bi_=bi_: h.tensor_scalar(out=Osb[bi_][:, 1, :], in0=idf[:, 0:64], scalar1=0.0,
                                                               scalar2=Ops[bi_][:, 64:65], op0=ALU.mult, op1=ALU.add),
                     reads=[b_ops[bi_], b_c], writes=[b_osb[bi_]])

            def s3(i):
                qb = qbs[i]
                bi_ = i % 2
                for hh in range(2):
                    p.op(p.pe, lambda h, bi_=bi_, hh=hh: h.transpose(OTps[bi_][:, hh, :], Osb[bi_][:, hh, :], idf[:]),
                         reads=[b_osb[bi_], b_c], writes=[b_otps[bi_]])
                do, dp = qb["dst"]
                for hh in range(2):
                    dst = fap(Oacc[:, hh, :], do, dp)
                    src = OTps[bi_][:, hh, :]
                    if len(dp) == 2:
                        src = src.rearrange("p (a b) -> p a b", a=dp[0][1])
                    if qb["add"]:
                        p.op(p.dve, lambda h, dst=dst, src=src: h.tensor_tensor(out=dst, in0=src, in1=dst, op=ALU.add),
                             reads=[b_otps[bi_], b_oacc], writes=[b_oacc])
                    else:
                        p.op(p.act, lambda h, dst=dst, src=src: h.activation(out=dst, in_=src, func=AF.Copy),
                             reads=[b_otps[bi_]], writes=[b_oacc])
                last = (i == nq - 1) or (qbs[i + 1]["sb"] != qb["sb"])
                if last:
                    finalize(qb["sb"])

            def finalize(sb):
                if ph == 0:
                    p.op(p.dve, lambda h: h.tensor_scalar(out=Oacc[:, 1, :], in0=Oacc[:, 1, :], scalar1=es[0:64, 0:1],
                                                          scalar2=None, op0=ALU.add),
                         reads=[b_oacc, b_c], writes=[b_oacc])
                p.op(p.dve, lambda h: h.reciprocal(out=Oacc[:, 1, :], in_=Oacc[:, 1, :]), reads=[b_oacc], writes=[b_oacc])
                p.op(p.dve, lambda h: h.tensor_tensor(out=ost[:], in0=Oacc[:, 0, :], in1=Oacc[:, 1, :], op=ALU.mult),
                     reads=[b_oacc], writes=[b_ost])
                p.dma(p.pool, ds_o, o_d[ph, :, sb * SBLK:(sb + 1) * SBLK], ost[:], reads=[b_ost])

            for step in range(nq + 2):
                if step < nq:
                    s1(step)
                if 0 <= step - 1 < nq:
                    s2(step - 1)
                if 0 <= step - 2 < nq:
                    s3(step - 2)
        p.finish(p.pool, [(ds_o.sem, ds_o.cnt)])
        with nc.Block() as block:
            p.replay(block)
    return nc


def attn_host_inputs(P, hd, acons_hd, rel_bias_hd, sink_val, idb, idf, vidx):
    bA, bC, (ri, ci), bDm = acons_hd
    bDg = rel_bias_hd[ri, ci].astype(NPBF)
    qk_rows = [0 + hd * 64, 256 + (hd // 2) * 64, 768 + hd * 64, 1024 + hd * 64, 1536 + hd * 64, 1792 + hd * 64]
    v_rows = [384 + (hd // 2) * 64, 1280 + hd * 64, 2048 + hd * 64]
    qk = np.stack([P[r0:r0 + 64] for r0 in qk_rows])
    qk[4] = qk[4].reshape(64, 32, 8, 4, 16).transpose(0, 1, 3, 2, 4).reshape(64, S)
    vts = []
    for ph in range(3):
        v = P[v_rows[ph]:v_rows[ph] + 64]
        vpad = np.concatenate([v, np.zeros((64, 1), v.dtype)], axis=1)
        g = vpad[:, vidx[ph]]
        vts.append(g.transpose(2, 1, 0))
    return {"qk": np.ascontiguousarray(qk), "vt": np.ascontiguousarray(np.concatenate(vts, axis=1)),
            "bA": np.ascontiguousarray(bA.transpose(1, 0, 2)), "bC": np.ascontiguousarray(bC.transpose(1, 0, 2)),
            "bDg": np.ascontiguousarray(bDg.transpose(1, 0, 2)), "bDm": np.ascontiguousarray(bDm.transpose(1, 0, 2)),
            "sink": np.full((128, 1), sink_val, np.float32), "idb": idb, "idf": idf}


ST = 512
NST = S // ST
C1_2PI = 6.28125
C2_2PI = 2.0 * np.pi - 6.28125


def build_s5():
    from contextlib import ExitStack
    nc = bass.Bass("TRN2", target_bir_lowering=False)
    u_d = dram_in(nc, "u", [2, 64, S], BF16)
    prm_d = dram_in(nc, "prm", [128, 8, 4], F32)
    bm_d = dram_in(nc, "bm", [128, 8, 2, 16], F32)
    cm_d = dram_in(nc, "cm", [128, 8, 2, 16], F32)
    iota_d = dram_in(nc, "iota", [128, ST + 1], F32)
    sgn_d = dram_in(nc, "sgn", [128, 2], F32)
    idf_d = dram_in(nc, "idf", [128, 128], F32)
    swp_d = dram_in(nc, "swp", [128, 128], F32)
    y_d = dram_out(nc, "y_out", [2, 64, S], F32)
    with ExitStack() as st:
        env = Env(nc, st)
        p = env.p
        u_sb = env.sb("u_s", [64, 2, S], BF16)
        prm = env.sb("prm_s", [128, 8, 4], F32)
        bm = env.sb("bm_s", [128, 8, 2, 16], F32)
        cm = env.sb("cm_s", [128, 8, 2, 16], F32)
        iota = env.sb("iota_s", [128, ST + 1], F32)
        sgn = env.sb("sgn_s", [128, 2], F32)
        idf = env.sb("idf_s", [128, 128], F32)
        swp = env.sb("swp_s", [128, 128], F32)
        COS = env.sb("COS", [128, 8, ST + 1], F32)
        SIN = env.sb("SIN", [128, 8, ST + 1], F32)
        Rk = env.sb("Rk", [128, 8, ST], F32)
        Rot = env.sb("Rot", [128, 8, 128], F32)
        Bl = env.sb("Bl", [64, 8, 2, 128], BF16)
        Cl = env.sb("Cl", [128, 8, 2, 64], BF16)
        sc = env.sb("sc", [128, 8, 16], F32)
        ang = env.sb("ang", [128, ST + 1], F32)
        kq = env.sb("kq", [128, ST + 1], I32)
        kf = env.sb("kf", [128, ST + 1], F32)
        bpad = env.sb("bpad", [128, 2, 64], F32)
        tmp16 = env.sb("tmp16", [128, 4, 16], F32)
        init = env.sb("init", [128, 8], F32)
        t1 = [env.sb(f"t1_{i}", [128, ST], F32) for i in range(2)]
        t2 = [env.sb(f"t2_{i}", [128, ST], F32) for i in range(2)]
        v_sb = [env.sb(f"v_{i}", [128, ST], F32) for i in range(2)]
        w_sb = [env.sb(f"w_{i}", [128, ST], F32) for i in range(2)]
        Wc = [env.sb(f"Wc_{i}", [128, ST], BF16) for i in range(2)]
        Ws = [env.sb(f"Ws_{i}", [128, ST], BF16) for i in range(2)]
        yst = [env.sb(f"yst_{i}", [64, ST], F32) for i in range(2)]
        bu_ps = [env.ps(f"bu{i}", [128, ST]) for i in range(2)]
        bs_ps = [env.ps(f"bs{i}", [128, ST]) for i in range(2)]
        y_ps = [env.ps(f"yps{i}", [64, ST]) for i in range(2)]
        i_ps = env.ps("ips", [128, 8])
        s_ps = env.ps("sps", [64, 128])
        b_c, b_u, b_tab, b_set = Buf(), Buf(), Buf(), Buf()
        b_ang, b_kq, b_kf, b_bpad, b_sps, b_tmp = Buf(), Buf(), Buf(), Buf(), Buf(), Buf()
        b_init = [Buf() for _ in range(8)]
        b_ips = [Buf() for _ in range(8)]
        b_t1 = [Buf(), Buf()]
        b_t2 = [Buf(), Buf()]
        b_v = [Buf(), Buf()]
        b_w = [Buf(), Buf()]
        b_wc = [Buf(), Buf()]
        b_ws = [Buf(), Buf()]
        b_yst = [Buf(), Buf()]
        b_bu = [Buf(), Buf()]
        b_bs = [Buf(), Buf()]
        b_yps = [Buf(), Buf()]
        ds_c = env.dsem("dc")
        ds_u = env.dsem("du")
        ds_o = env.dsem("do")

        for dst, src in ((prm, prm_d), (bm, bm_d), (cm, cm_d)):
            p.dma(p.pool, ds_c, dst[:], src[:, :, :] if len(src.shape) == 3 else src[:, :, :, :], writes=[b_c])
        for dst, src in ((iota, iota_d), (sgn, sgn_d), (idf, idf_d), (swp, swp_d)):
            lt = p.dma(p.pool, ds_c, dst[:], src[:, :], writes=[b_c])
        lu = p.dma(p.sp, ds_u, u_sb[:, 0, :], u_d[0, :, :], writes=[b_u])
        lu = p.dma(p.sp, ds_u, u_sb[:, 1, :], u_d[1, :, :], writes=[b_u])
        b_u.w = lu
        p.op(p.dve, lambda h: h.memset(init[:], 0.0), writes=b_init)
        p.op(p.dve, lambda h: h.memset(Bl[:], 0.0), writes=[b_set])

        def col(k, i):
            return sc[:, k, i:i + 1]

        def dv(fn, reads, writes):
            return p.op(p.dve, fn, reads=reads, writes=writes)

        for k in range(8):
            g = k % 4
            lr, li, ls = prm[:, k, 0:1], prm[:, k, 1:2], prm[:, k, 2:3]
            p.op(p.act, lambda h, k=k, ls=ls: h.activation(out=col(k, 0), in_=ls, func=AF.Exp), reads=[b_c], writes=[b_set])
            dv(lambda h, k=k, li=li: h.tensor_tensor(out=col(k, 1), in0=li, in1=col(k, 0), op=ALU.mult), [b_set, b_c], [b_set])
            dv(lambda h, k=k, lr=lr: h.tensor_tensor(out=col(k, 2), in0=lr, in1=col(k, 0), op=ALU.mult), [b_set, b_c], [b_set])
            p.op(p.act, lambda h, k=k: h.activation(out=col(k, 2), in_=col(k, 2), func=AF.Exp), reads=[b_set], writes=[b_set])
            for which in range(2):
                tab = SIN if which == 0 else COS
                shift = 0.0 if which == 0 else float(np.pi / 2)
                dv(lambda h, k=k, shift=shift: h.tensor_scalar(out=ang[:], in0=iota[:], scalar1=col(k, 1), scalar2=shift,
                                                               op0=ALU.mult, op1=ALU.add), [b_set, b_c], [b_ang])
                dv(lambda h: h.tensor_scalar(out=kq[:], in0=ang[:], scalar1=float(1.0 / (2 * np.pi)), scalar2=None,
                                             op0=ALU.mult), [b_ang], [b_kq])
                dv(lambda h: h.tensor_copy(out=kf[:], in_=kq[:]), [b_kq], [b_kf])
                dv(lambda h: h.scalar_tensor_tensor(out=ang[:], in0=kf[:], scalar=-C1_2PI, in1=ang[:],
                                                    op0=ALU.mult, op1=ALU.add), [b_kf, b_ang], [b_ang])
                dv(lambda h: h.scalar_tensor_tensor(out=ang[:], in0=kf[:], scalar=-C2_2PI, in1=ang[:],
                                                    op0=ALU.mult, op1=ALU.add), [b_kf, b_ang], [b_ang])
                p.op(p.act, lambda h, k=k, tab=tab: h.activation(out=tab[:, k, :], in_=ang[:], func=AF.Sin),
                     reads=[b_ang], writes=[b_tab])
            c1, s1 = COS[:, k, 1:2], SIN[:, k, 1:2]
            c5, s5 = COS[:, k, ST:ST + 1], SIN[:, k, ST:ST + 1]
            dv(lambda h, k=k, c1=c1: h.tensor_tensor(out=col(k, 3), in0=col(k, 2), in1=c1, op=ALU.mult), [b_set, b_tab], [b_set])
            dv(lambda h, k=k, s1=s1: h.tensor_tensor(out=col(k, 4), in0=col(k, 2), in1=s1, op=ALU.mult), [b_set, b_tab], [b_set])
            dv(lambda h, k=k, lr=lr: h.tensor_tensor(out=col(k, 5), in0=lr, in1=lr, op=ALU.mult), [b_c, b_set], [b_set])
            dv(lambda h, k=k, li=li: h.scalar_tensor_tensor(out=col(k, 5), in0=li, scalar=li, in1=col(k, 5),
                                                            op0=ALU.mult, op1=ALU.add), [b_c, b_set], [b_set])
            dv(lambda h, k=k: h.reciprocal(out=col(k, 13), in_=col(k, 5)), [b_set], [b_set])
            dv(lambda h, k=k: h.tensor_scalar(out=col(k, 6), in0=col(k, 3), scalar1=-1.0, scalar2=None, op0=ALU.add),
               [b_set], [b_set])
            dv(lambda h, k=k, li=li: h.tensor_tensor(out=col(k, 9), in0=col(k, 4), in1=li, op=ALU.mult), [b_set, b_c], [b_set])
            dv(lambda h, k=k, lr=lr: h.scalar_tensor_tensor(out=col(k, 7), in0=col(k, 6), scalar=lr, in1=col(k, 9),
                                                            op0=ALU.mult, op1=ALU.add), [b_set, b_c], [b_set])
            dv(lambda h, k=k: h.tensor_tensor(out=col(k, 7), in0=col(k, 7), in1=col(k, 13), op=ALU.mult), [b_set], [b_set])
            dv(lambda h, k=k, li=li: h.tensor_tensor(out=col(k, 9), in0=col(k, 6), in1=li, op=ALU.mult), [b_set, b_c], [b_set])
            dv(lambda h, k=k, lr=lr: h.scalar_tensor_tensor(out=col(k, 8), in0=col(k, 4), scalar=lr, in1=col(k, 9),
                                                            op0=ALU.mult, op1=ALU.subtract), [b_set, b_c], [b_set])
            dv(lambda h, k=k: h.tensor_tensor(out=col(k, 8), in0=col(k, 8), in1=col(k, 13), op=ALU.mult), [b_set], [b_set])
            dv(lambda h, k=k: h.tensor_tensor(out=col(k, 10), in0=col(k, 8), in1=sgn[:, 0:1], op=ALU.mult), [b_set, b_c], [b_set])
            dv(lambda h, k=k: h.tensor_tensor(out=col(k, 11), in0=col(k, 7), in1=sgn[:, 1:2], op=ALU.mult), [b_set, b_c], [b_set])
            dv(lambda h, k=k, s5=s5: h.tensor_tensor(out=col(k, 12), in0=s5, in1=sgn[:, 1:2], op=ALU.mult),
               [b_tab, b_c], [b_set])
            dv(lambda h: h.memset(bpad[:], 0.0), [], [b_bpad])
            bA, bB = bm[:, k, 0, :], bm[:, k, 1, :]
            dv(lambda h, k=k, bB=bB: h.tensor_scalar(out=tmp16[:, 0, :], in0=bB, scalar1=col(k, 10), scalar2=None, op0=ALU.mult),
               [b_set, b_c], [b_tmp])
            dv(lambda h, k=k, bA=bA, g=g: h.scalar_tensor_tensor(out=bpad[:, 0, 16 * g:16 * g + 16], in0=bA, scalar=col(k, 7),
                                                                 in1=tmp16[:, 0, :], op0=ALU.mult, op1=ALU.add),
               [b_set, b_c, b_tmp], [b_bpad])
            dv(lambda h, k=k, bB=bB: h.tensor_scalar(out=tmp16[:, 1, :], in0=bB, scalar1=col(k, 11), scalar2=None, op0=ALU.mult),
               [b_set, b_c], [b_tmp])
            dv(lambda h, k=k, bA=bA, g=g: h.scalar_tensor_tensor(out=bpad[:, 1, 16 * g:16 * g + 16], in0=bA, scalar=col(k, 8),
                                                                 in1=tmp16[:, 1, :], op0=ALU.mult, op1=ALU.add),
               [b_set, b_c, b_tmp], [b_bpad])
            for which in range(2):
                p.op(p.pe, lambda h, which=which: h.matmul(s_ps[:], lhsT=bpad[:, which, :], rhs=idf[:], start=True, stop=True),
                     reads=[b_bpad, b_c], writes=[b_sps])
                p.op(p.act, lambda h, k=k, which=which: h.activation(out=Bl[:, k, which, :], in_=s_ps[:], func=AF.Copy),
                     reads=[b_sps], writes=[b_set])
            dv(lambda h, k=k: h.memset(Cl[:, k, :, :], 0.0), [], [b_set])
            cA, cB = cm[:, k, 0, :], cm[:, k, 1, :]
            dv(lambda h, k=k, cA=cA, g=g: h.tensor_scalar(out=Cl[:, k, 0, 16 * g:16 * g + 16], in0=cA, scalar1=sgn[:, 1:2],
                                                          scalar2=None, op0=ALU.mult), [b_c], [b_set])
            dv(lambda h, k=k, cB=cB, g=g: h.tensor_scalar(out=Cl[:, k, 1, 16 * g:16 * g + 16], in0=cB, scalar1=-1.0,
                                                          scalar2=None, op0=ALU.mult), [b_c], [b_set])
            dv(lambda h, k=k: h.tensor_scalar(out=Rk[:, k, :], in0=iota[:, 0:ST], scalar1=0.0, scalar2=col(k, 2),
                                              op0=ALU.mult, op1=ALU.add), [b_c, b_set], [b_set])
            dv(lambda h, k=k: h.tensor_scalar(out=Rot[:, k, :], in0=swp[:], scalar1=col(k, 12), scalar2=None, op0=ALU.mult),
               [b_c, b_set], [b_set])
            dv(lambda h, k=k, c5=c5: h.scalar_tensor_tensor(out=Rot[:, k, :], in0=idf[:], scalar=c5, in1=Rot[:, k, :],
                                                            op0=ALU.mult, op1=ALU.add), [b_c, b_tab, b_set], [b_set])

        it = 0
        for t in range(NST):
            for d in range(2):
                for g in range(4):
                    k = d * 4 + g
                    bi = it % 2
                    it += 1
                    usl = u_sb[:, d, t * ST:(t + 1) * ST]
                    p.op(p.pe, lambda h, k=k, bi=bi, usl=usl: h.matmul(bu_ps[bi][:], lhsT=Bl[:, k, 0, :], rhs=usl,
                                                                       start=True, stop=True),
                         reads=[b_set, b_u], writes=[b_bu[bi]])
                    p.op(p.pe, lambda h, k=k, bi=bi, usl=usl: h.matmul(bs_ps[bi][:], lhsT=Bl[:, k, 1, :], rhs=usl,
                                                                       start=True, stop=True),
                         reads=[b_set, b_u], writes=[b_bs[bi]])
                    dv(lambda h, k=k, bi=bi: h.tensor_tensor(out=t1[bi][:], in0=bu_ps[bi][:], in1=COS[:, k, 0:ST], op=ALU.mult),
                       [b_bu[bi], b_tab], [b_t1[bi]])
                    dv(lambda h, k=k, bi=bi: h.tensor_tensor(out=t2[bi][:], in0=bs_ps[bi][:], in1=SIN[:, k, 0:ST], op=ALU.mult),
                       [b_bs[bi], b_tab], [b_t2[bi]])
                    p.op(p.pool, lambda h, bi=bi: h.tensor_tensor(out=v_sb[bi][:], in0=t1[bi][:], in1=t2[bi][:], op=ALU.add),
                         reads=[b_t1[bi], b_t2[bi]], writes=[b_v[bi]])
                    dv(lambda h, k=k, bi=bi: h.tensor_tensor_scan(out=w_sb[bi][:], data0=Rk[:, k, :], data1=v_sb[bi][:],
                                                                  initial=init[:, k:k + 1], op0=ALU.mult, op1=ALU.add),
                       [b_v[bi], b_set, b_init[k]], [b_w[bi]])
                    p.op(p.pe, lambda h, k=k, bi=bi: h.matmul(i_ps[:, k:k + 1], lhsT=Rot[:, k, :], rhs=w_sb[bi][:, ST - 1:ST],
                                                              start=True, stop=True),
                         reads=[b_w[bi], b_set], writes=[b_ips[k]])
                    p.op(p.act, lambda h, k=k: h.activation(out=init[:, k:k + 1], in_=i_ps[:, k:k + 1], func=AF.Copy),
                         reads=[b_ips[k]], writes=[b_init[k]])
                    p.op(p.pool, lambda h, k=k, bi=bi: h.tensor_tensor(out=Wc[bi][:], in0=w_sb[bi][:], in1=COS[:, k, 0:ST], op=ALU.mult),
                         reads=[b_w[bi], b_tab], writes=[b_wc[bi]])
                    p.op(p.pool, lambda h, k=k, bi=bi: h.tensor_tensor(out=Ws[bi][:], in0=w_sb[bi][:], in1=SIN[:, k, 0:ST], op=ALU.mult),
                         reads=[b_w[bi], b_tab], writes=[b_ws[bi]])
                    p.op(p.pe, lambda h, k=k, bi=bi, d=d, g=g: h.matmul(y_ps[d][:], lhsT=Cl[:, k, 0, :], rhs=Wc[bi][:],
                                                                        start=(g == 0), stop=False),
                         reads=[b_wc[bi], b_set], writes=[b_yps[d]])
                    p.op(p.pe, lambda h, k=k, bi=bi, d=d, g=g: h.matmul(y_ps[d][:], lhsT=Cl[:, k, 1, :], rhs=Ws[bi][:],
                                                                        start=False, stop=(g == 3)),
                         reads=[b_ws[bi], b_set], writes=[b_yps[d]])
                p.op(p.act, lambda h, d=d: h.activation(out=yst[d][:], in_=y_ps[d][:], func=AF.Copy),
                     reads=[b_yps[d]], writes=[b_yst[d]])
                p.dma(p.sp, ds_o, y_d[d, :, t * ST:(t + 1) * ST], yst[d][:], reads=[b_yst[d]])
        p.finish(p.sp, [(ds_o.sem, ds_o.cnt)])
        with nc.Block() as block:
            p.replay(block)
    return nc


_CACHE = {}


def _run(nc, maps):
    res = run_bass_kernel_spmd(nc, maps, core_ids=list(range(NCORES)))
    return res.results


def s5_host_params(inp, l, q):
    prm = np.zeros((128, 8, 4), np.float32)
    bm = np.zeros((128, 8, 2, 16), np.float32)
    cm = np.zeros((128, 8, 2, 16), np.float32)
    for d in range(2):
        for gl in range(4):
            k = d * 4 + gl
            g = 4 * q + gl
            prm[:, k, 0] = np.tile(inp["s5_lam_re"][l, d, g], 2)
            prm[:, k, 1] = np.tile(inp["s5_lam_im"][l, d, g], 2)
            prm[:, k, 2] = inp["s5_log_step"][l, d, g]
            bre, bim = inp["s5_b_re"][l, d, g], inp["s5_b_im"][l, d, g]
            bm[:, k, 0] = np.concatenate([bre, bim], 0)
            bm[:, k, 1] = np.concatenate([bim, bre], 0)
            cre, cim = inp["s5_c_re"][l, d, g].T, inp["s5_c_im"][l, d, g].T
            cm[:, k, 0] = np.concatenate([cre, cim], 0)
            cm[:, k, 1] = np.concatenate([cim, cre], 0)
    return prm, bm, cm


def s5_consts():
    iota = np.broadcast_to(np.arange(ST + 1, dtype=np.float32), (128, ST + 1)).copy()
    sgn = np.zeros((128, 2), np.float32)
    sgn[:64, 0], sgn[64:, 0] = -1.0, 1.0
    sgn[:64, 1], sgn[64:, 1] = 1.0, -1.0
    idf = np.eye(128, dtype=np.float32)
    swp = np.zeros((128, 128), np.float32)
    for p_ in range(128):
        swp[p_, (p_ + 64) % 128] = 1.0
    return iota, sgn, idf, swp


def kernel(**inputs):
    inp = {k: np.asarray(v, dtype=np.float32) for k, v in inputs.items()}
    streams = [np.stack(stream_for_launch(j, inp)) for j in range(DEPTH + 1)]
    sizes = [s.shape[0] for s in streams]
    allb = np.concatenate(streams)
    per = allb.shape[0] // NCORES
    assert per * NCORES == allb.shape[0]
    ncw = build_w(per)
    res = _run(ncw, [{"wf": allb[c * per:(c + 1) * per].reshape(per * 256, 2048)} for c in range(NCORES)])
    wb = np.concatenate([np.asarray(r["wb"]).reshape(per, 128, WSLOT) for r in res])
    del allb, streams
    wbs = np.split(wb, np.cumsum(sizes)[:-1])
    cst = np.zeros((128, 256), np.float32)
    cst[:, :128] = 1.0 / 1024
    for hh in range(2):
        cst[hh * 64:(hh + 1) * 64, 128 + hh * 64:128 + (hh + 1) * 64] = 1.0 / 64
    cst = cst.astype(NPBF)
    idb = np.eye(128, dtype=np.float32).astype(NPBF)
    iota, sgn, idf, swp = s5_consts()
    acons = [attn_bias_consts(hd) for hd in range(4)]
    vidx = [vt_indices(ph) for ph in range(3)]

    x2 = inp["x"].reshape(B * S, D)
    xT = [np.ascontiguousarray(x2[c * TPC:(c + 1) * TPC].T) for c in range(NCORES)]
    tail_in = None
    nct = {}
    nca = None
    ncs = None
    for j in range(DEPTH + 1):
        key = (j >= 1, j <= DEPTH - 1)
        if key not in nct:
            nct[key] = build_t(*key)
        cols = cols_for_launch(j, inp)
        maps = []
        for c in range(NCORES):
            m = {"xin": xT[c], "wst": wbs[j], "cols": cols, "cst": cst}
            if j >= 1:
                m.update(tail_in[c])
            maps.append(m)
        res = _run(nct[key], maps)
        xT = [np.asarray(r["xout"]) for r in res]
        if j == DEPTH:
            break
        l = j
        cpb = NCORES // B
        pT = [np.concatenate([np.asarray(res[b * cpb + i]["pout"]) for i in range(cpb)], axis=1) for b in range(B)]
        if nca is None:
            nca = build_attn()
        maps = []
        for c in range(NCORES):
            b, hd = c // 4, c % 4
            P = pT[b]
            maps.append(attn_host_inputs(P, hd, acons[hd], inp["na_rel_bias"][l][hd], inp["a_sink"][l][hd], idb, idf, vidx))
        ares = _run(nca, maps)
        if ncs is None:
            ncs = build_s5()
        maps = []
        for c in range(NCORES):
            b, q = c // 4, c % 4
            uu = pT[b][512 + 64 * q:512 + 64 * q + 64]
            prm, bm, cm = s5_host_params(inp, l, q)
            maps.append({"u": np.ascontiguousarray(np.stack([uu, uu[:, ::-1]])), "prm": prm, "bm": bm, "cm": cm,
                         "iota": iota, "sgn": sgn, "idf": idf, "swp": swp})
        sres = _run(ncs, maps)
        tail_in = []
        oT, yf, yb = [], [], []
        for b in range(B):
            o = np.zeros((768, S), NPBF)
            f = np.zeros((256, S), np.float32)
            r_ = np.zeros((256, S), np.float32)
            for hd in range(4):
                oo = np.asarray(ares[b * 4 + hd]["o_out"])
                for ph in range(3):
                    o[ph * 256 + hd * 64:ph * 256 + hd * 64 + 64] = oo[ph]
                yy = np.asarray(sres[b * 4 + hd]["y_out"])
                f[hd * 64:hd * 64 + 64] = yy[0]
                r_[hd * 64:hd * 64 + 64] = yy[1][:, ::-1]
            oT.append(o)
            yf.append(f)
            yb.append(r_)
        for c in range(NCORES):
            b, ch = c // cpb, c % cpb
            sl = slice(ch * TPC, (ch + 1) * TPC)
            tail_in.append({"o_in": np.ascontiguousarray(oT[b][:, sl]), "yf_in": np.ascontiguousarray(yf[b][:, sl]),
                            "yb_in": np.ascontiguousarray(yb[b][:, sl]),
                            "u_in": np.ascontiguousarray(pT[b][512:768, sl])})
    out = np.concatenate([x.T for x in xT], axis=0).reshape(B, S, D).astype(np.float32)
    return out
```

```python
import numpy as np
import ml_dtypes
import concourse.bass as bass
import concourse.mybir as mybir
from concourse.bass_utils import run_bass_kernel_spmd

F32 = mybir.dt.float32
BF16 = mybir.dt.bfloat16
I32 = mybir.dt.int32
AF = mybir.ActivationFunctionType
ALU = mybir.AluOpType
NPBF = ml_dtypes.bfloat16

NCORES = 8
D = 1024
DFF = 2816
NFC = DFF // 128
B = 2
S = 16384
DEPTH = 4
TPC = B * S // NCORES
TT = 512
NT = TPC // TT
WSLOT = 4096
NSLOT = 5
EPS = 1e-6
NEG = -30000.0


class Buf:
    __slots__ = ("w", "r")

    def __init__(self):
        self.w = None
        self.r = {}


class Eng:
    def __init__(self, name, sem, selfsync):
        self.name = name
        self.sem = sem
        self.cnt = 0
        self.ops = []
        self.waited = {}
        self.selfsync = selfsync


class DSem:
    def __init__(self, sem):
        self.sem = sem
        self.cnt = 0


class Prog:
    def __init__(self, nc, sems):
        self.nc = nc
        self.pe = Eng("tensor", sems["tensor"], False)
        self.act = Eng("scalar", sems["scalar"], True)
        self.dve = Eng("vector", sems["vector"], True)
        self.pool = Eng("gpsimd", sems["gpsimd"], True)
        self.sp = Eng("sync", sems["sync"], False)
        self.engs = [self.pe, self.act, self.dve, self.pool, self.sp]

    def _wait(self, eng, tok):
        sem, val = tok
        k = id(sem)
        if eng.waited.get(k, 0) >= val:
            return
        eng.waited[k] = val
        eng.ops.append(lambda h, sm=sem, v=val: h.wait_ge(sm, v))

    def _deps(self, eng, reads, writes, nosync=False):
        skip_self = nosync or not eng.selfsync
        for b in reads:
            if b.w is not None:
                if b.w[0] is eng.sem and skip_self:
                    continue
                self._wait(eng, b.w)
        for b in writes:
            for sem, val in b.r.values():
                if sem is eng.sem and skip_self:
                    continue
                self._wait(eng, (sem, val))
            if b.w is not None:
                if b.w[0] is eng.sem and skip_self:
                    continue
                self._wait(eng, b.w)

    def _mark(self, tok, reads, writes):
        for b in reads:
            k = id(tok[0])
            if k not in b.r or b.r[k][1] < tok[1]:
                b.r[k] = tok
        for b in writes:
            b.w = tok
            b.r = {}

    def op(self, eng, fn, reads=(), writes=(), nosync=False):
        self._deps(eng, reads, writes, nosync)
        eng.cnt += 1
        tok = (eng.sem, eng.cnt)
        eng.ops.append(lambda h, f=fn, sm=eng.sem: f(h).then_inc(sm, 1))
        self._mark(tok, reads, writes)
        return tok

    def dma(self, eng, dsem, out, in_, reads=(), writes=()):
        self._deps(eng, reads, writes)
        dsem.cnt += 16
        tok = (dsem.sem, dsem.cnt)
        eng.ops.append(lambda h, o=out, i=in_, sm=dsem.sem: h.dma_start(out=o, in_=i).then_inc(sm, 16))
        self._mark(tok, reads, writes)
        return tok

    def finish(self, eng, toks):
        for t in toks:
            self._wait(eng, t)

    def replay(self, block):
        for e in self.engs:
            def body(h, e=e):
                for f in e.ops:
                    f(h)
            getattr(block, e.name)(body)


class Env:
    def __init__(self, nc, stack):
        self.nc = nc
        self.stack = stack
        self.nsem = 0
        sems = {n: self.sem("p_" + n) for n in ("tensor", "scalar", "vector", "gpsimd", "sync")}
        self.p = Prog(nc, sems)

    def sem(self, name):
        self.nsem += 1
        return self.stack.enter_context(self.nc.semaphore(name))

    def dsem(self, name):
        return DSem(self.sem(name))

    def sb(self, name, shape, dt):
        return self.stack.enter_context(self.nc.sbuf_tensor(name, list(shape), dt))

    def ps(self, name, shape, dt=F32):
        return self.stack.enter_context(self.nc.psum_tensor(name, list(shape), dt))


def dram_in(nc, name, shape, dt):
    return nc.dram_tensor(name, list(shape), dt, kind="ExternalInput").ap()


def dram_out(nc, name, shape, dt):
    return nc.dram_tensor(name, list(shape), dt, kind="ExternalOutput").ap()


def build_w(nblk):
    from contextlib import ExitStack
    nc = bass.Bass("TRN2", target_bir_lowering=False)
    rows = nblk * 256
    wf = dram_in(nc, "wf", [rows, 2048], F32)
    wb = dram_out(nc, "wb", [rows, 2048], BF16)
    with ExitStack() as st:
        env = Env(nc, st)
        p = env.p
        ds = env.dsem("d")
        toks = []
        for b in range(nblk):
            toks.append(p.dma(p.pool, ds, wb[b * 256:(b + 1) * 256, :], wf[b * 256:(b + 1) * 256, :]))
        p.finish(p.pool, toks[-1:])
        with nc.Block() as block:
            p.replay(block)
    return nc


def _pad_block(a):
    a = a.reshape(128, -1)
    out = np.zeros((128, WSLOT), np.float32)
    out[:, : a.shape[1]] = a
    return out


def ffn_blocks(w_in, w_out):
    blocks = []
    wi = w_in.reshape(8, 128, 2, NFC, 128)
    for blk in range(NFC // 2):
        sub = wi[:, :, :, 2 * blk:2 * blk + 2, :]
        blocks.append(_pad_block(sub.transpose(1, 3, 2, 0, 4)))
    wo = w_out.reshape(NFC, 128, 8, 128)
    for dc in range(8):
        blocks.append(_pad_block(wo[:, :, dc, :].transpose(1, 0, 2)))
    return blocks


def proj_blocks(w, nmc):
    nk = w.shape[0] // 128
    wr = w.reshape(nk, 128, nmc, 128)
    blocks = []
    for b0 in range(0, nmc, 4):
        sub = wr[:, :, b0:b0 + 4, :]
        blocks.append(_pad_block(sub.transpose(1, 2, 0, 3)))
    return blocks


def stream_for_launch(j, inp):
    blocks = []
    if j >= 1:
        l = j - 1
        blocks += proj_blocks(inp["s5_w_glu"][l], 2)
        blocks += proj_blocks(inp["w_out"][l], 8)
        blocks += ffn_blocks(inp["ffn2_w_in"][l], inp["ffn2_w_out"][l])
    if j <= DEPTH - 1:
        l = j
        blocks += ffn_blocks(inp["ffn1_w_in"][l], inp["ffn1_w_out"][l])
        blocks += proj_blocks(inp["w_in"][l], 18)
    return blocks


QK_CHUNK_GAIN = {0: 0, 1: 0, 2: 1, 6: 2, 7: 2, 8: 3, 9: 3, 12: 4, 13: 4, 14: 5, 15: 5}


def cols_for_launch(j, inp):
    c = np.ones((128, 44), np.float32)
    if j >= 1:
        l = j - 1
        c[:, 0:8] = inp["ffn2_norm"][l].reshape(8, 128).T
        c[:, 42:44] = inp["s5_d"][l].reshape(2, 128).T
    if j <= DEPTH - 1:
        l = j
        c[:, 8:16] = inp["ffn1_norm"][l].reshape(8, 128).T
        c[:, 16:24] = inp["mix_norm"][l].reshape(8, 128).T
        for mc, gi in QK_CHUNK_GAIN.items():
            c[:, 24 + mc] = np.tile(inp["qk_gain"][l, gi], 2)
    return c


def build_t(has_tail, has_head):
    from contextlib import ExitStack
    nc = bass.Bass("TRN2", target_bir_lowering=False)
    nblk = (22 if has_tail else 0) + (24 if has_head else 0)
    xin = dram_in(nc, "xin", [D, TPC], F32)
    wst = dram_in(nc, "wst", [nblk, 128, WSLOT], BF16)
    cols_d = dram_in(nc, "cols", [128, 44], F32)
    cst_d = dram_in(nc, "cst", [128, 256], BF16)
    if has_tail:
        o_d = dram_in(nc, "o_in", [768, TPC], BF16)
        yf_d = dram_in(nc, "yf_in", [256, TPC], F32)
        yb_d = dram_in(nc, "yb_in", [256, TPC], F32)
        u_d = dram_in(nc, "u_in", [256, TPC], BF16)
    xout = dram_out(nc, "xout", [D, TPC], F32)
    if has_head:
        pout_d = dram_out(nc, "pout", [2304, TPC], BF16)

    with ExitStack() as st:
        env = Env(nc, st)
        p = env.p
        x_sb = [env.sb(f"x{i}", [128, 8, TT], F32) for i in range(2)]
        h_sb = env.sb("h", [128, 8, TT], BF16)
        sq_sb = env.sb("sq", [128, 8, TT], BF16)
        a_sb = env.sb("a", [128, NFC, TT], BF16)
        rs_sb = [env.sb(f"rs{i}", [128, TT], F32) for i in range(2)]
        s_sb = [env.sb(f"s{i}", [128, TT], F32) for i in range(2)]
        ring = env.sb("ring", [128, NSLOT, WSLOT], BF16)
        cols = env.sb("colsb", [128, 44], F32)
        cst = env.sb("cstb", [128, 256], BF16)
        epsc = env.sb("epsc", [128, 1], F32)
        if has_head:
            po_sb = env.sb("po", [128, 18, TT], BF16)
        if has_tail:
            mx_sb = env.sb("mx", [128, 8, TT], BF16)
            yf_sb = env.sb("yf", [128, 2, TT], F32)
            yb_sb = env.sb("yb", [128, 2, TT], F32)
            u_sb = env.sb("u", [128, 2, TT], BF16)
            g_sb = env.sb("g", [128, 2, TT], F32)
            t_sb = env.sb("tg", [128, 2, TT], F32)
            gb_sb = env.sb("gb", [128, 2, TT], BF16)
        bank = [env.ps(f"bk{i}", [128, TT]) for i in range(8)]
        b_x = [[Buf() for _ in range(8)] for _ in range(2)]
        b_h = [Buf() for _ in range(8)]
        b_sq = [Buf() for _ in range(8)]
        b_a = [Buf() for _ in range(NFC)]
        b_rs = [Buf(), Buf()]
        b_s = [Buf(), Buf()]
        b_ring = [Buf() for _ in range(NSLOT)]
        b_bank = [Buf() for _ in range(8)]
        b_const = Buf()
        b_po = [Buf() for _ in range(18)]
        b_mx = [Buf() for _ in range(8)]
        b_y = Buf()
        b_g = Buf()
        b_tg = Buf()
        b_gb = Buf()
        ds_ring = [env.dsem(f"dr{i}") for i in range(NSLOT)]
        ds_x = [env.dsem(f"dx{i}") for i in range(2)]
        ds_c = env.dsem("dc")
        ds_out = env.dsem("dout")
        ds_pout = env.dsem("dpo")
        ds_tail = env.dsem("dtl")

        ones_mean = cst[:, 0:128]
        blk_ones = cst[:, 128:256]

        p.dma(p.pool, ds_c, cols[:], cols_d[:, :], writes=[b_const])
        p.dma(p.pool, ds_c, cst[:], cst_d[:, :], writes=[b_const])
        p.op(p.dve, lambda h: h.memset(epsc[:], EPS), writes=[b_const])

        wstate = {"n": 0}

        def next_block(tile_idx, bidx):
            n = wstate["n"]
            wstate["n"] = n + 1
            s = n % NSLOT
            p.dma(p.sp, ds_ring[s], ring[:, s, :], wst[bidx, :, :], writes=[b_ring[s]])
            return s

        order = []
        per_tile = list(range(nblk))
        PREF = NSLOT - 1
        issued = {"k": 0}
        total_blocks = NT * nblk

        def ensure_issued(upto):
            while issued["k"] < min(upto, total_blocks):
                k = issued["k"]
                next_block(k // nblk, k % nblk)
                issued["k"] = k + 1

        used = {"k": 0}

        def take_block():
            k = used["k"]
            used["k"] = k + 1
            ensure_issued(k + 1)
            s = k % NSLOT
            return s

        def release_prefetch():
            ensure_issued(used["k"] + PREF)

        def load_x(t, xb):
            for c in range(8):
                pass
            src = xin[:, t * TT:(t + 1) * TT].rearrange("(c p) t -> p c t", p=128)
            p.dma(p.pool, ds_x[xb], x_sb[xb][:], src, writes=b_x[xb])

        def rmsnorm_to_h(xb, gcol0):
            xs = x_sb[xb]
            for c in range(8):
                p.op(p.act, lambda h, c=c: h.activation(out=sq_sb[:, c, :], in_=xs[:, c, :], func=AF.Square),
                     reads=[b_x[xb][c]], writes=[b_sq[c]])
            for c in range(8):
                p.op(p.pe, lambda h, c=c: h.matmul(bank[6][:], lhsT=ones_mean, rhs=sq_sb[:, c, :],
                                                   start=(c == 0), stop=(c == 7)),
                     reads=[b_sq[c], b_const], writes=[b_bank[6]])
            p.op(p.act, lambda h: h.activation(out=rs_sb[0][:], in_=bank[6][:], func=AF.Sqrt, bias=epsc[:, 0:1]),
                 reads=[b_bank[6], b_const], writes=[b_rs[0]])
            p.op(p.dve, lambda h: h.reciprocal(out=rs_sb[1][:], in_=rs_sb[0][:]), reads=[b_rs[0]], writes=[b_rs[1]])
            for c in range(8):
                p.op(p.dve, lambda h, c=c: h.scalar_tensor_tensor(
                    out=h_sb[:, c, :], in0=xs[:, c, :], scalar=cols[:, gcol0 + c:gcol0 + c + 1],
                    in1=rs_sb[1][:], op0=ALU.mult, op1=ALU.mult),
                    reads=[b_x[xb][c], b_rs[1], b_const], writes=[b_h[c]])

        def ffn(xb, gcol0):
            xs = x_sb[xb]
            rmsnorm_to_h(xb, gcol0)
            for blk in range(NFC // 2):
                s = take_block()
                wv = ring[:, s, :].rearrange("p (f g k m) -> p f g k m", f=2, g=2, k=8)
                for fcl in range(2):
                    fc = 2 * blk + fcl
                    pb = fc % 2
                    G, U = bank[2 * pb], bank[2 * pb + 1]
                    bG, bU = b_bank[2 * pb], b_bank[2 * pb + 1]
                    for kc in range(8):
                        p.op(p.pe, lambda h, wv=wv, kc=kc, fcl=fcl, G=G: h.matmul(
                            G[:], lhsT=wv[:, fcl, 0, kc, :], rhs=h_sb[:, kc, :], start=(kc == 0), stop=(kc == 7)),
                            reads=[b_ring[s], b_h[kc]], writes=[bG])
                    for kc in range(8):
                        p.op(p.pe, lambda h, wv=wv, kc=kc, fcl=fcl, U=U: h.matmul(
                            U[:], lhsT=wv[:, fcl, 1, kc, :], rhs=h_sb[:, kc, :], start=(kc == 0), stop=(kc == 7)),
                            reads=[b_ring[s], b_h[kc]], writes=[bU])
                    p.op(p.act, lambda h, G=G, pb=pb: h.activation(out=s_sb[pb][:], in_=G[:], func=AF.Silu),
                         reads=[bG], writes=[b_s[pb]])
                    p.op(p.dve, lambda h, U=U, pb=pb, fc=fc: h.tensor_tensor(
                        out=a_sb[:, fc, :], in0=U[:], in1=s_sb[pb][:], op=ALU.mult),
                        reads=[bU, b_s[pb]], writes=[b_a[fc]])
                release_prefetch()
            for dc in range(8):
                s = take_block()
                wv = ring[:, s, 0:NFC * 128].rearrange("p (f m) -> p f m", f=NFC)
                O = bank[4 + dc % 2]
                bO = b_bank[4 + dc % 2]
                for fc in range(NFC):
                    p.op(p.pe, lambda h, wv=wv, fc=fc, O=O: h.matmul(
                        O[:], lhsT=wv[:, fc, :], rhs=a_sb[:, fc, :], start=(fc == 0), stop=(fc == NFC - 1)),
                        reads=[b_ring[s], b_a[fc]], writes=[bO])
                p.op(p.dve, lambda h, dc=dc, O=O: h.scalar_tensor_tensor(
                    out=xs[:, dc, :], in0=O[:], scalar=0.5, in1=xs[:, dc, :], op0=ALU.mult, op1=ALU.add),
                    reads=[bO, b_x[xb][dc]], writes=[b_x[xb][dc]])
                release_prefetch()

        def head_proj(xb, t):
            xs = x_sb[xb]
            rmsnorm_to_h(xb, 16)
            for b0 in range(5):
                s = take_block()
                wv = ring[:, s, :].rearrange("p (c k m) -> p c k m", c=4, k=8)
                for mcl in range(4):
                    mc = 4 * b0 + mcl
                    if mc >= 18:
                        break
                    P = bank[mc % 4]
                    bP = b_bank[mc % 4]
                    for kc in range(8):
                        p.op(p.pe, lambda h, wv=wv, kc=kc, mcl=mcl, P=P: h.matmul(
                            P[:], lhsT=wv[:, mcl, kc, :], rhs=h_sb[:, kc, :], start=(kc == 0), stop=(kc == 7)),
                            reads=[b_ring[s], b_h[kc]], writes=[bP])
                    if mc in QK_CHUNK_GAIN:
                        q = mc % 2
                        sqb = s_sb[q]
                        p.op(p.act, lambda h, P=P, q=q: h.activation(out=sq_sb[:, q, :], in_=P[:], func=AF.Square),
                             reads=[bP], writes=[b_sq[q]])
                        p.op(p.pe, lambda h, q=q: h.matmul(bank[7][:], lhsT=blk_ones, rhs=sq_sb[:, q, :],
                                                           start=True, stop=True),
                             reads=[b_sq[q], b_const], writes=[b_bank[7]])
                        p.op(p.act, lambda h: h.activation(out=rs_sb[0][:], in_=bank[7][:], func=AF.Sqrt,
                                                           bias=epsc[:, 0:1]),
                             reads=[b_bank[7], b_const], writes=[b_rs[0]])
                        p.op(p.dve, lambda h: h.reciprocal(out=rs_sb[1][:], in_=rs_sb[0][:]),
                             reads=[b_rs[0]], writes=[b_rs[1]])
                        p.op(p.dve, lambda h, P=P, mc=mc: h.scalar_tensor_tensor(
                            out=po_sb[:, mc, :], in0=P[:], scalar=cols[:, 24 + mc:25 + mc], in1=rs_sb[1][:],
                            op0=ALU.mult, op1=ALU.mult),
                            reads=[bP, b_rs[1], b_const], writes=[b_po[mc]])
                    else:
                        p.op(p.act, lambda h, P=P, mc=mc: h.activation(out=po_sb[:, mc, :], in_=P[:], func=AF.Copy),
                             reads=[bP], writes=[b_po[mc]])
                release_prefetch()
            dst = pout_d[:, t * TT:(t + 1) * TT].rearrange("(c p) t -> p c t", p=128)
            return p.dma(p.pool, ds_pout, dst, po_sb[:], reads=b_po)

        def tail(xb, t):
            xs = x_sb[xb]
            tsl = slice(t * TT, (t + 1) * TT)
            for (r0, c0) in ((0, 0), (256, 4), (512, 6)):
                src = o_d[r0:r0 + 256, tsl].rearrange("(c p) t -> p c t", p=128)
                p.dma(p.pool, ds_tail, mx_sb[:, c0:c0 + 2, :], src, writes=[b_mx[c0], b_mx[c0 + 1]])
            p.dma(p.pool, ds_tail, yf_sb[:], yf_d[:, tsl].rearrange("(c p) t -> p c t", p=128), writes=[b_y])
            p.dma(p.pool, ds_tail, yb_sb[:], yb_d[:, tsl].rearrange("(c p) t -> p c t", p=128), writes=[b_y])
            ltok = p.dma(p.pool, ds_tail, u_sb[:], u_d[:, tsl].rearrange("(c p) t -> p c t", p=128), writes=[b_y])
            for bb in (b_mx[0], b_mx[1], b_mx[4], b_mx[5], b_mx[6], b_mx[7], b_y):
                bb.w = ltok
            p.op(p.dve, lambda h: h.tensor_tensor(out=yf_sb[:], in0=yf_sb[:], in1=yb_sb[:], op=ALU.add),
                 reads=[b_y], writes=[b_y])
            for c in range(2):
                p.op(p.dve, lambda h, c=c: h.scalar_tensor_tensor(
                    out=yf_sb[:, c, :], in0=u_sb[:, c, :], scalar=cols[:, 42 + c:43 + c], in1=yf_sb[:, c, :],
                    op0=ALU.mult, op1=ALU.add), reads=[b_y, b_const], writes=[b_y])
            p.op(p.dve, lambda h: h.tensor_tensor(out=t_sb[:], in0=yf_sb[:], in1=yf_sb[:], op=ALU.mult),
                 reads=[b_y], writes=[b_tg])
            p.op(p.dve, lambda h: h.tensor_scalar(out=t_sb[:], in0=t_sb[:], scalar1=0.044715, scalar2=1.0,
                                                  op0=ALU.mult, op1=ALU.add), reads=[b_tg], writes=[b_tg])
            p.op(p.dve, lambda h: h.tensor_tensor(out=t_sb[:], in0=t_sb[:], in1=yf_sb[:], op=ALU.mult),
                 reads=[b_tg, b_y], writes=[b_tg])
            p.op(p.act, lambda h: h.activation(out=t_sb[:], in_=t_sb[:], func=AF.Sigmoid, scale=1.5957691216057308),
                 reads=[b_tg], writes=[b_tg])
            p.op(p.dve, lambda h: h.tensor_tensor(out=g_sb[:], in0=t_sb[:], in1=yf_sb[:], op=ALU.mult),
                 reads=[b_tg, b_y], writes=[b_g])
            p.op(p.act, lambda h: h.activation(out=gb_sb[:], in_=g_sb[:], func=AF.Copy), reads=[b_g], writes=[b_gb])
            s = take_block()
            wv = ring[:, s, 0:512].rearrange("p (c k m) -> p c k m", c=2, k=2)
            for mc in range(2):
                Z = bank[mc]
                for kc in range(2):
                    p.op(p.pe, lambda h, wv=wv, mc=mc, kc=kc, Z=Z: h.matmul(
                        Z[:], lhsT=wv[:, mc, kc, :], rhs=gb_sb[:, kc, :], start=(kc == 0), stop=(kc == 1)),
                        reads=[b_ring[s], b_gb], writes=[b_bank[mc]])
                p.op(p.act, lambda h, mc=mc, Z=Z: h.activation(out=s_sb[mc][:], in_=Z[:], func=AF.Sigmoid),
                     reads=[b_bank[mc]], writes=[b_s[mc]])
                p.op(p.dve, lambda h, mc=mc: h.tensor_tensor(out=mx_sb[:, 2 + mc, :], in0=g_sb[:, mc, :],
                                                             in1=s_sb[mc][:], op=ALU.mult),
                     reads=[b_g, b_s[mc]], writes=[b_mx[2 + mc]])
            release_prefetch()
            for b0 in range(2):
                s = take_block()
                wv = ring[:, s, :].rearrange("p (c k m) -> p c k m", c=4, k=8)
                for mcl in range(4):
                    dc = 4 * b0 + mcl
                    O = bank[4 + dc % 2]
                    bO = b_bank[4 + dc % 2]
                    for kc in range(8):
                        p.op(p.pe, lambda h, wv=wv, kc=kc, mcl=mcl, O=O: h.matmul(
                            O[:], lhsT=wv[:, mcl, kc, :], rhs=mx_sb[:, kc, :], start=(kc == 0), stop=(kc == 7)),
                            reads=[b_ring[s], b_mx[kc]], writes=[bO])
                    p.op(p.dve, lambda h, dc=dc, O=O: h.tensor_tensor(
                        out=xs[:, dc, :], in0=O[:], in1=xs[:, dc, :], op=ALU.add),
                        reads=[bO, b_x[xb][dc]], writes=[b_x[xb][dc]])
                release_prefetch()

        out_toks = []
        load_x(0, 0)
        ensure_issued(PREF)
        for t in range(NT):
            xb = t % 2
            if t + 1 < NT:
                load_x(t + 1, 1 - xb)
            if has_tail:
                tail(xb, t)
                ffn(xb, 0)
            if has_head:
                ffn(xb, 8)
                out_toks.append(head_proj(xb, t))
            dst = xout[:, t * TT:(t + 1) * TT].rearrange("(c p) t -> p c t", p=128)
            out_toks.append(p.dma(p.pool, ds_out, dst, x_sb[xb][:], reads=b_x[xb]))
        p.finish(p.pool, [(ds_out.sem, ds_out.cnt), (ds_pout.sem, ds_pout.cnt)] if has_head
                 else [(ds_out.sem, ds_out.cnt)])
        with nc.Block() as block:
            p.replay(block)
    return nc


PAD = 1024
SP_ = S + 2 * PAD
SBLK = 2048
NVT = 405
GRID_W = 64


def fap(ap2d, off, pat):
    return bass.AP(ap2d.tensor, ap2d.offset + off, [list(ap2d.ap[0])] + [list(x) for x in pat])


def attn_plan():
    plans = []
    vt = [(PAD + 128 * t, [[1, 128]]) for t in range(128)]
    qb = []
    for b in range(128):
        kts = []
        for dt_, bi in ((-1, 0), (0, 1), (1, 2)):
            t = b + dt_
            if 0 <= t < 128:
                kts.append((PAD + 128 * t, [[1, 128]], t, bi))
        sb = (128 * b) // SBLK
        qb.append(dict(q=(128 * b, [[1, 128]]), kts=kts, dst=(128 * b - sb * SBLK, [[1, 128]]), add=False, sb=sb))
    plans.append((vt, qb))
    vt = []
    vidx = {}
    for ci, d in enumerate((1, 4, 16)):
        L = S // d
        for rho in range(d):
            for bp in range(L // 128 + 1):
                vidx[(ci, rho, bp)] = len(vt)
                vt.append((PAD + d * 64 * (2 * bp - 1) + rho, [[d, 128]]))
    assert len(vt) == NVT
    qb = []
    for sb in range(S // SBLK):
        for ci, d in enumerate((1, 4, 16)):
            L = S // d
            nb = L // 128
            for rho in range(d):
                for b in range(nb):
                    t0 = d * 128 * b + rho
                    if t0 // SBLK != sb:
                        continue
                    kts = []
                    for bp, base in ((b, 0), (b + 1, 1)):
                        bi = 4 * ci + base
                        if base == 0 and b == 0:
                            bi = 4 * ci + 2
                        if base == 1 and b == nb - 1:
                            bi = 4 * ci + 3
                        off, pat = vt[vidx[(ci, rho, bp)]]
                        kts.append((off, pat, vidx[(ci, rho, bp)], bi))
                    qb.append(dict(q=(t0, [[d, 128]]), kts=kts, dst=(t0 - sb * SBLK, [[d, 128]]), add=(ci > 0), sb=sb))
    plans.append((vt, qb))
    vt = [(PAD + 128 * t, [[1, 128]]) for t in range(128)]
    qb = []
    for i in range(32):
        rs = min(max(8 * i - 4, 0), 240)
        icls = 0 if i == 0 else (2 if i == 31 else 1)
        for j in range(4):
            kts = []
            for m in range(8):
                t = rs // 2 + m
                kts.append((PAD + 128 * t, [[1, 128]], t, (icls * 4 + j) * 8 + m))
            t0 = 8 * i * GRID_W + 16 * j
            sb = t0 // SBLK
            qb.append(dict(q=((i * 4 + j) * 128, [[1, 128]]), kts=kts, dst=(t0 - sb * SBLK, [[GRID_W, 8], [1, 16]]),
                           add=False, sb=sb))
    plans.append((vt, qb))
    return plans


def attn_bias_consts(hd):
    slopes = 2.0 ** (-np.arange(1, 9, dtype=np.float64))
    kk = np.arange(128)[:, None]
    qq = np.arange(128)[None, :]
    A = np.zeros((3, 128, 128), np.float32)
    for bi, sh in enumerate((-128, 0, 128)):
        rel = kk + sh - qq
        A[bi] = np.where(np.abs(rel) <= 128, -slopes[hd] * np.abs(rel) * 8.0, NEG * 8)
    C = np.zeros((12, 128, 128), np.float32)
    for ci, d in enumerate((1, 4, 16)):
        for base, sh in ((0, -64), (1, 64)):
            rel = kk + sh - qq
            m = np.where(np.abs(rel) <= 64, -slopes[4 + hd] * d * np.abs(rel) * 8.0, NEG * 8)
            C[4 * ci + base] = m
        first = C[4 * ci + 0].copy()
        first[:64, :] = NEG * 8
        C[4 * ci + 2] = first
        last = C[4 * ci + 1].copy()
        last[64:, :] = NEG * 8
        C[4 * ci + 3] = last
    ri = np.zeros((96, 128, 128), np.int64)
    cidx = np.zeros((96, 128, 128), np.int64)
    Dm = np.zeros((96, 128, 128), np.float32)
    rows = S // GRID_W
    for icls, i in enumerate((0, 5, 31)):
        rs = min(max(8 * i - 4, 0), rows - 16)
        for j in range(4):
            for m in range(8):
                idx = (icls * 4 + j) * 8 + m
                krow = (rs + 2 * m + np.arange(128) // 64)[:, None]
                kcol = (np.arange(128) % 64)[:, None]
                qrow = (8 * i + np.arange(128) // 16)[None, :]
                qcol = (16 * j + np.arange(128) % 16)[None, :]
                wr = np.clip(qrow - 4, 0, rows - 8)
                wc = np.clip(qcol - 8, 0, GRID_W - 16)
                valid = (krow >= wr) & (krow < wr + 8) & (kcol >= wc) & (kcol < wc + 16)
                ri[idx] = np.clip(krow - qrow + 7, 0, 14) + 0 * qcol
                cidx[idx] = np.clip(kcol - qcol + 15, 0, 30) + 0 * qrow
                Dm[idx] = np.where(valid, 0.0, NEG * 8)
    return A.astype(NPBF), C.astype(NPBF), (ri, cidx), Dm.astype(NPBF)


def vt_indices(ph):
    vts, _ = attn_plan()[ph]
    idx = np.zeros((len(vts), 128), np.int64)
    for n, (off, pat) in enumerate(vts):
        st_, cnt = pat[0]
        assert len(pat) == 1 and cnt == 128
        idx[n] = off - PAD + st_ * np.arange(128)
    idx[(idx < 0) | (idx >= S)] = -1
    return idx


def build_attn(phases=(0, 1, 2)):
    from contextlib import ExitStack
    nc = bass.Bass("TRN2", target_bir_lowering=False)
    qk_d = dram_in(nc, "qk", [6, 64, S], BF16)
    vt_d = dram_in(nc, "vt", [128, 128 + NVT + 128, 64], BF16)
    bA_d = dram_in(nc, "bA", [128, 3, 128], BF16)
    bC_d = dram_in(nc, "bC", [128, 12, 128], BF16)
    bDg_d = dram_in(nc, "bDg", [128, 96, 128], BF16)
    bDm_d = dram_in(nc, "bDm", [128, 96, 128], BF16)
    sink_d = dram_in(nc, "sink", [128, 1], F32)
    idb_d = dram_in(nc, "idb", [128, 128], BF16)
    idf_d = dram_in(nc, "idf", [128, 128], F32)
    o_d = dram_out(nc, "o_out", [3, 64, S], BF16)
    plans = attn_plan()
    vbase = [0, 128, 128 + NVT]
    with ExitStack() as st:
        env = Env(nc, st)
        p = env.p
        qT = env.sb("qT", [64, S], BF16)
        kT = env.sb("kT", [64, SP_], BF16)
        Vt = env.sb("Vt", [128, NVT, 65], BF16)
        bias = env.sb("bias", [128, 96, 128], BF16)
        idb = env.sb("idbs", [128, 128], BF16)
        idf = env.sb("idfs", [128, 128], F32)
        es = env.sb("es", [128, 1], F32)
        Oacc = env.sb("Oacc", [64, 2, SBLK], F32)
        ost = env.sb("ost", [64, SBLK], BF16)
        Pt = [env.sb(f"Pt{i}", [128, 8, 128], BF16) for i in range(2)]
        Osb = [env.sb(f"Osb{i}", [128, 2, 64], F32) for i in range(2)]
        Sps = [env.ps(f"Sps{i}", [128, 8, 128]) for i in range(2)]
        Ops_t = env.ps("Ops", [128, 2, 65])
        Ops = [Ops_t[:, 0, :], Ops_t[:, 1, :]]
        OTps = [env.ps(f"OTps{i}", [64, 2, 128]) for i in range(2)]
        b_q, b_k, b_vt, b_bias, b_c = Buf(), Buf(), Buf(), Buf(), Buf()
        b_oacc, b_ost = Buf(), Buf()
        b_pt = [Buf(), Buf()]
        b_osb = [Buf(), Buf()]
        b_sps = [Buf(), Buf()]
        b_ops = [Buf(), Buf()]
        b_otps = [Buf(), Buf()]
        ds_c = env.dsem("dc")
        ds_q = env.dsem("dq")
        ds_b = env.dsem("db")
        ds_v = env.dsem("dv")
        ds_o = env.dsem("do")

        p.dma(p.pool, ds_c, idb[:], idb_d[:, :], writes=[b_c])
        p.dma(p.pool, ds_c, idf[:], idf_d[:, :], writes=[b_c])
        p.dma(p.pool, ds_c, es[:], sink_d[:, :], writes=[b_c])
        p.op(p.act, lambda h: h.activation(out=es[:], in_=es[:], func=AF.Exp), reads=[b_c], writes=[b_c])
        p.op(p.dve, lambda h: h.memset(kT[:, 0:PAD], 0.0), writes=[b_k])
        p.op(p.dve, lambda h: h.memset(kT[:, PAD + S:SP_], 0.0), writes=[b_k])

        for ph in phases:
            vts, qbs = plans[ph]
            lt = p.dma(p.sp, ds_q, qT[:], qk_d[2 * ph + 0, :, :], writes=[b_q])
            lt = p.dma(p.sp, ds_q, kT[:, PAD:PAD + S], qk_d[2 * ph + 1, :, :], writes=[b_k])
            b_q.w = lt
            b_k.w = lt
            if ph == 0:
                p.dma(p.pool, ds_b, bias[:, 0:3, :], bA_d[:, :, :], writes=[b_bias])
            elif ph == 1:
                p.dma(p.pool, ds_b, bias[:, 0:12, :], bC_d[:, :, :], writes=[b_bias])
            else:
                p.dma(p.pool, ds_b, bias[:], bDg_d[:, :, :], writes=[b_bias])
                btv = Vt[:].rearrange("p a b -> p (a b)")[:, 0:96 * 128].rearrange("p (a c) -> p a c", a=96)
                p.dma(p.pool, ds_b, btv, bDm_d[:, :, :], writes=[b_vt])
                for c8 in range(4):
                    p.op(p.dve, lambda h, c8=c8, btv=btv: h.scalar_tensor_tensor(
                        out=bias[:, 24 * c8:24 * c8 + 24, :], in0=bias[:, 24 * c8:24 * c8 + 24, :], scalar=8.0,
                        in1=btv[:, 24 * c8:24 * c8 + 24, :], op0=ALU.mult, op1=ALU.add),
                        reads=[b_bias, b_vt], writes=[b_bias])
            p.op(p.dve, lambda h: h.memset(Vt[:], 1.0), writes=[b_vt])
            nv = len(vts)
            lv = None
            for g0 in range(0, nv, 64):
                n = min(64, nv - g0)
                lv = p.dma(p.sp, ds_v, Vt[:, g0:g0 + n, 0:64], vt_d[:, vbase[ph] + g0:vbase[ph] + g0 + n, :], writes=[b_vt])
            b_vt.w = lv
            nq = len(qbs)

            def s1(i):
                qb = qbs[i]
                bi_ = i % 2
                qo, qp = qb["q"]
                for kt, (ko, kp, vi, bidx) in enumerate(qb["kts"]):
                    p.op(p.pe, lambda h, bi_=bi_, kt=kt, ko=ko, kp=kp, qo=qo, qp=qp: h.matmul(
                        Sps[bi_][:, kt, :], lhsT=fap(kT[:], ko, kp), rhs=fap(qT[:], qo, qp), start=True, stop=False),
                        reads=[b_k, b_q], writes=[b_sps[bi_]])
                    p.op(p.pe, lambda h, bi_=bi_, kt=kt, bidx=bidx: h.matmul(
                        Sps[bi_][:, kt, :], lhsT=idb[:], rhs=bias[:, bidx, :], start=False, stop=True),
                        reads=[b_bias, b_c], writes=[b_sps[bi_]])
                nk = len(qb["kts"])
                for k0 in range(0, nk, 4):
                    k1 = min(nk, k0 + 4)
                    p.op(p.act, lambda h, bi_=bi_, k0=k0, k1=k1: h.activation(
                        out=Pt[bi_][:, k0:k1, :], in_=Sps[bi_][:, k0:k1, :], func=AF.Exp, scale=0.125),
                        reads=[b_sps[bi_]], writes=[b_pt[bi_]])

            def s2(i):
                qb = qbs[i]
                bi_ = i % 2
                nk = len(qb["kts"])
                for kt, (ko, kp, vi, bidx) in enumerate(qb["kts"]):
                    p.op(p.pe, lambda h, bi_=bi_, kt=kt, vi=vi, nk=nk: h.matmul(
                        Ops[bi_], lhsT=Pt[bi_][:, kt, :], rhs=Vt[:, vi, :], start=(kt == 0), stop=(kt == nk - 1)),
                        reads=[b_pt[bi_], b_vt], writes=[b_ops[bi_]])
                p.op(p.dve, lambda h, bi_=bi_: h.tensor_copy(out=Osb[bi_][:, 0, :], in_=Ops[bi_][:, 0:64]),
                     reads=[b_ops[bi_]], writes=[b_osb[bi_]])
                p.op(p.dve, lambda h, bi_=bi_: h.tensor_scalar(out=Osb[bi_][:, 1, :], in0=idf[:, 0:64], scalar1=0.0,
                                                               scalar2=Ops[bi_][:, 64:65], op0=ALU.mult, op1=ALU.add),
                     reads=[b_ops[bi_], b_c], writes=[b_osb[bi_]])

            def s3(i):
                qb = qbs[i]
                bi_ = i % 2
                for hh in range(2):
                    p.op(p.pe, lambda h, bi_=bi_, hh=hh: h.transpose(OTps[bi_][:, hh, :], Osb[bi_][:, hh, :], idf[:]),
                         reads=[b_osb[bi_], b_c], writes=[b_otps[bi_]])
                do, dp = qb["dst"]
                for hh in range(2):
                    dst = fap(Oacc[:, hh, :], do, dp)
                    src = OTps[bi_][:, hh, :]
                    if len(dp) == 2:
                        src = src.rearrange("p (a b) -> p a b", a=dp[0][1])
                    if qb["add"]:
                        p.op(p.dve, lambda h, dst=dst, src=src: h.tensor_tensor(out=dst, in0=src, in1=dst, op=ALU.add),
                             reads=[b_otps[bi_], b_oacc], writes=[b_oacc])
                    else:
                        p.op(p.act, lambda h, dst=dst, src=src: h.activation(out=dst, in_=src, func=AF.Copy),
                             reads=[b_otps[bi_]], writes=[b_oacc])
                last = (i == nq - 1) or (qbs[i + 1]["sb"] != qb["sb"])
                if last:
                    finalize(qb["sb"])

            def finalize(sb):
                if ph == 0:
                    p.op(p.dve, lambda h: h.tensor_scalar(out=Oacc[:, 1, :], in0=Oacc[:, 1, :], scalar1=es[0:64, 0:1],
                                                          scalar2=None, op0=ALU.add),
                         reads=[b_oacc, b_c], writes=[b_oacc])
                p.op(p.dve, lambda h: h.reciprocal(out=Oacc[:, 1, :], in_=Oacc[:, 1, :]), reads=[b_oacc], writes=[b_oacc])
                p.op(p.dve, lambda h: h.tensor_tensor(out=ost[:], in0=Oacc[:, 0, :], in1=Oacc[:, 1, :], op=ALU.mult),
                     reads=[b_oacc], writes=[b_ost])
                p.dma(p.pool, ds_o, o_d[ph, :, sb * SBLK:(sb + 1) * SBLK], ost[:], reads=[b_ost])

            for step in range(nq + 2):
                if step < nq:
                    s1(step)
                if 0 <= step - 1 < nq:
                    s2(step - 1)
                if 0 <= step - 2 < nq:
                    s3(step - 2)
        p.finish(p.pool, [(ds_o.sem, ds_o.cnt)])
        with nc.Block() as block:
            p.replay(block)
    return nc


def attn_host_inputs(P, hd, acons_hd, rel_bias_hd, sink_val, idb, idf, vidx):
    bA, bC, (ri, ci), bDm = acons_hd
    bDg = rel_bias_hd[ri, ci].astype(NPBF)
    qk_rows = [0 + hd * 64, 256 + (hd // 2) * 64, 768 + hd * 64, 1024 + hd * 64, 1536 + hd * 64, 1792 + hd * 64]
    v_rows = [384 + (hd // 2) * 64, 1280 + hd * 64, 2048 + hd * 64]
    qk = np.stack([P[r0:r0 + 64] for r0 in qk_rows])
    qk[4] = qk[4].reshape(64, 32, 8, 4, 16).transpose(0, 1, 3, 2, 4).reshape(64, S)
    vts = []
    for ph in range(3):
        v = P[v_rows[ph]:v_rows[ph] + 64]
        vpad = np.concatenate([v, np.zeros((64, 1), v.dtype)], axis=1)
        g = vpad[:, vidx[ph]]
        vts.append(g.transpose(2, 1, 0))
    return {"qk": np.ascontiguousarray(qk), "vt": np.ascontiguousarray(np.concatenate(vts, axis=1)),
            "bA": np.ascontiguousarray(bA.transpose(1, 0, 2)), "bC": np.ascontiguousarray(bC.transpose(1, 0, 2)),
            "bDg": np.ascontiguousarray(bDg.transpose(1, 0, 2)), "bDm": np.ascontiguousarray(bDm.transpose(1, 0, 2)),
            "sink": np.full((128, 1), sink_val, np.float32), "idb": idb, "idf": idf}


ST = 512
NST = S // ST
C1_2PI = 6.28125
C2_2PI = 2.0 * np.pi - 6.28125


def build_s5():
    from contextlib import ExitStack
    nc = bass.Bass("TRN2", target_bir_lowering=False)
    u_d = dram_in(nc, "u", [2, 64, S], BF16)
    prm_d = dram_in(nc, "prm", [128, 8, 4], F32)
    bm_d = dram_in(nc, "bm", [128, 8, 2, 16], F32)
    cm_d = dram_in(nc, "cm", [128, 8, 2, 16], F32)
    iota_d = dram_in(nc, "iota", [128, ST + 1], F32)
    sgn_d = dram_in(nc, "sgn", [128, 2], F32)
    idf_d = dram_in(nc, "idf", [128, 128], F32)
    swp_d = dram_in(nc, "swp", [128, 128], F32)
    y_d = dram_out(nc, "y_out", [2, 64, S], F32)
    with ExitStack() as st:
        env = Env(nc, st)
        p = env.p
        u_sb = env.sb("u_s", [64, 2, S], BF16)
        prm = env.sb("prm_s", [128, 8, 4], F32)
        bm = env.sb("bm_s", [128, 8, 2, 16], F32)
        cm = env.sb("cm_s", [128, 8, 2, 16], F32)
        iota = env.sb("iota_s", [128, ST + 1], F32)
        sgn = env.sb("sgn_s", [128, 2], F32)
        idf = env.sb("idf_s", [128, 128], F32)
        swp = env.sb("swp_s", [128, 128], F32)
        COS = env.sb("COS", [128, 8, ST + 1], F32)
        SIN = env.sb("SIN", [128, 8, ST + 1], F32)
        Rk = env.sb("Rk", [128, 8, ST], F32)
        Rot = env.sb("Rot", [128, 8, 128], F32)
        Bl = env.sb("Bl", [64, 8, 2, 128], BF16)
        Cl = env.sb("Cl", [128, 8, 2, 64], BF16)
        sc = env.sb("sc", [128, 8, 16], F32)
        ang = env.sb("ang", [128, ST + 1], F32)
        kq = env.sb("kq", [128, ST + 1], I32)
        kf = env.sb("kf", [128, ST + 1], F32)
        bpad = env.sb("bpad", [128, 2, 64], F32)
        tmp16 = env.sb("tmp16", [128, 4, 16], F32)
        init = env.sb("init", [128, 8], F32)
        t1 = [env.sb(f"t1_{i}", [128, ST], F32) for i in range(2)]
        t2 = [env.sb(f"t2_{i}", [128, ST], F32) for i in range(2)]
        v_sb = [env.sb(f"v_{i}", [128, ST], F32) for i in range(2)]
        w_sb = [env.sb(f"w_{i}", [128, ST], F32) for i in range(2)]
        Wc = [env.sb(f"Wc_{i}", [128, ST], BF16) for i in range(2)]
        Ws = [env.sb(f"Ws_{i}", [128, ST], BF16) for i in range(2)]
        yst = [env.sb(f"yst_{i}", [64, ST], F32) for i in range(2)]
        bu_ps = [env.ps(f"bu{i}", [128, ST]) for i in range(2)]
        bs_ps = [env.ps(f"bs{i}", [128, ST]) for i in range(2)]
        y_ps = [env.ps(f"yps{i}", [64, ST]) for i in range(2)]
        i_ps = env.ps("ips", [128, 8])
        s_ps = env.ps("sps", [64, 128])
        b_c, b_u, b_tab, b_set = Buf(), Buf(), Buf(), Buf()
        b_ang, b_kq, b_kf, b_bpad, b_sps, b_tmp = Buf(), Buf(), Buf(), Buf(), Buf(), Buf()
        b_init = [Buf() for _ in range(8)]
        b_ips = [Buf() for _ in range(8)]
        b_t1 = [Buf(), Buf()]
        b_t2 = [Buf(), Buf()]
        b_v = [Buf(), Buf()]
        b_w = [Buf(), Buf()]
        b_wc = [Buf(), Buf()]
        b_ws = [Buf(), Buf()]
        b_yst = [Buf(), Buf()]
        b_bu = [Buf(), Buf()]
        b_bs = [Buf(), Buf()]
        b_yps = [Buf(), Buf()]
        ds_c = env.dsem("dc")
        ds_u = env.dsem("du")
        ds_o = env.dsem("do")

        for dst, src in ((prm, prm_d), (bm, bm_d), (cm, cm_d)):
            p.dma(p.pool, ds_c, dst[:], src[:, :, :] if len(src.shape) == 3 else src[:, :, :, :], writes=[b_c])
        for dst, src in ((iota, iota_d), (sgn, sgn_d), (idf, idf_d), (swp, swp_d)):
            lt = p.dma(p.pool, ds_c, dst[:], src[:, :], writes=[b_c])
        lu = p.dma(p.sp, ds_u, u_sb[:, 0, :], u_d[0, :, :], writes=[b_u])
        lu = p.dma(p.sp, ds_u, u_sb[:, 1, :], u_d[1, :, :], writes=[b_u])
        b_u.w = lu
        p.op(p.dve, lambda h: h.memset(init[:], 0.0), writes=b_init)
        p.op(p.dve, lambda h: h.memset(Bl[:], 0.0), writes=[b_set])

        def col(k, i):
            return sc[:, k, i:i + 1]

        def dv(fn, reads, writes):
            return p.op(p.dve, fn, reads=reads, writes=writes)

        def C_(i):
            return sc[:, :, i]
        LR, LI, LS = prm[:, :, 0], prm[:, :, 1], prm[:, :, 2]
        p.op(p.act, lambda h: h.activation(out=C_(0), in_=LS, func=AF.Exp), reads=[b_c], writes=[b_set])
        dv(lambda h: h.tensor_tensor(out=C_(1), in0=LI, in1=C_(0), op=ALU.mult), [b_set, b_c], [b_set])
        dv(lambda h: h.tensor_tensor(out=C_(2), in0=LR, in1=C_(0), op=ALU.mult), [b_set, b_c], [b_set])
        p.op(p.act, lambda h: h.activation(out=C_(2), in_=C_(2), func=AF.Exp), reads=[b_set], writes=[b_set])
        for k in range(8):
            for which in range(2):
                tab = SIN if which == 0 else COS
                shift = 0.0 if which == 0 else float(np.pi / 2)
                dv(lambda h, k=k, shift=shift: h.tensor_scalar(out=ang[:], in0=iota[:], scalar1=col(k, 1), scalar2=shift,
                                                               op0=ALU.mult, op1=ALU.add), [b_set, b_c], [b_ang])
                dv(lambda h: h.tensor_scalar(out=kq[:], in0=ang[:], scalar1=float(1.0 / (2 * np.pi)), scalar2=None,
                                             op0=ALU.mult), [b_ang], [b_kq])
                dv(lambda h: h.tensor_copy(out=kf[:], in_=kq[:]), [b_kq], [b_kf])
                dv(lambda h: h.scalar_tensor_tensor(out=ang[:], in0=kf[:], scalar=-C1_2PI, in1=ang[:],
                                                    op0=ALU.mult, op1=ALU.add), [b_kf, b_ang], [b_ang])
                dv(lambda h: h.scalar_tensor_tensor(out=ang[:], in0=kf[:], scalar=-C2_2PI, in1=ang[:],
                                                    op0=ALU.mult, op1=ALU.add), [b_kf, b_ang], [b_ang])
                p.op(p.act, lambda h, k=k, tab=tab: h.activation(out=tab[:, k, :], in_=ang[:], func=AF.Sin),
                     reads=[b_ang], writes=[b_tab])
        c1v, s1v = COS[:, :, 1], SIN[:, :, 1]
        s5v = SIN[:, :, ST]
        sg1 = sgn[:, 0:1].to_broadcast([128, 8])
        sg2 = sgn[:, 1:2].to_broadcast([128, 8])
        dv(lambda h: h.tensor_tensor(out=C_(3), in0=C_(2), in1=c1v, op=ALU.mult), [b_set, b_tab], [b_set])
        dv(lambda h: h.tensor_tensor(out=C_(4), in0=C_(2), in1=s1v, op=ALU.mult), [b_set, b_tab], [b_set])
        dv(lambda h: h.tensor_tensor(out=C_(5), in0=LR, in1=LR, op=ALU.mult), [b_c, b_set], [b_set])
        dv(lambda h: h.tensor_tensor(out=C_(9), in0=LI, in1=LI, op=ALU.mult), [b_c, b_set], [b_set])
        dv(lambda h: h.tensor_tensor(out=C_(5), in0=C_(5), in1=C_(9), op=ALU.add), [b_set], [b_set])
        dv(lambda h: h.reciprocal(out=C_(13), in_=C_(5)), [b_set], [b_set])
        dv(lambda h: h.tensor_scalar(out=C_(6), in0=C_(3), scalar1=-1.0, scalar2=None, op0=ALU.add), [b_set], [b_set])
        dv(lambda h: h.tensor_tensor(out=C_(9), in0=C_(4), in1=LI, op=ALU.mult), [b_set, b_c], [b_set])
        dv(lambda h: h.tensor_tensor(out=C_(7), in0=C_(6), in1=LR, op=ALU.mult), [b_set, b_c], [b_set])
        dv(lambda h: h.tensor_tensor(out=C_(7), in0=C_(7), in1=C_(9), op=ALU.add), [b_set], [b_set])
        dv(lambda h: h.tensor_tensor(out=C_(7), in0=C_(7), in1=C_(13), op=ALU.mult), [b_set], [b_set])
        dv(lambda h: h.tensor_tensor(out=C_(9), in0=C_(6), in1=LI, op=ALU.mult), [b_set, b_c], [b_set])
        dv(lambda h: h.tensor_tensor(out=C_(8), in0=C_(4), in1=LR, op=ALU.mult), [b_set, b_c], [b_set])
        dv(lambda h: h.tensor_tensor(out=C_(8), in0=C_(8), in1=C_(9), op=ALU.subtract), [b_set], [b_set])
        dv(lambda h: h.tensor_tensor(out=C_(8), in0=C_(8), in1=C_(13), op=ALU.mult), [b_set], [b_set])
        dv(lambda h: h.tensor_tensor(out=C_(10), in0=C_(8), in1=sg1, op=ALU.mult), [b_set, b_c], [b_set])
        dv(lambda h: h.tensor_tensor(out=C_(11), in0=C_(7), in1=sg2, op=ALU.mult), [b_set, b_c], [b_set])
        dv(lambda h: h.tensor_tensor(out=C_(12), in0=s5v, in1=sg2, op=ALU.mult), [b_tab, b_c], [b_set])
        for k in range(8):
            g = k % 4
            c5 = COS[:, k, ST:ST + 1]
            dv(lambda h: h.memset(bpad[:], 0.0), [], [b_bpad])
            bA, bB = bm[:, k, 0, :], bm[:, k, 1, :]
            dv(lambda h, k=k, bB=bB: h.tensor_scalar(out=tmp16[:, 0, :], in0=bB, scalar1=col(k, 10), scalar2=None, op0=ALU.mult),
               [b_set, b_c], [b_tmp])
            dv(lambda h, k=k, bA=bA, g=g: h.scalar_tensor_tensor(out=bpad[:, 0, 16 * g:16 * g + 16], in0=bA, scalar=col(k, 7),
                                                                 in1=tmp16[:, 0, :], op0=ALU.mult, op1=ALU.add),
               [b_set, b_c, b_tmp], [b_bpad])
            dv(lambda h, k=k, bB=bB: h.tensor_scalar(out=tmp16[:, 1, :], in0=bB, scalar1=col(k, 11), scalar2=None, op0=ALU.mult),
               [b_set, b_c], [b_tmp])
            dv(lambda h, k=k, bA=bA, g=g: h.scalar_tensor_tensor(out=bpad[:, 1, 16 * g:16 * g + 16], in0=bA, scalar=col(k, 8),
                                                                 in1=tmp16[:, 1, :], op0=ALU.mult, op1=ALU.add),
               [b_set, b_c, b_tmp], [b_bpad])
            for which in range(2):
                p.op(p.pe, lambda h, which=which: h.matmul(s_ps[:], lhsT=bpad[:, which, :], rhs=idf[:], start=True, stop=True),
                     reads=[b_bpad, b_c], writes=[b_sps])
                p.op(p.act, lambda h, k=k, which=which: h.activation(out=Bl[:, k, which, :], in_=s_ps[:], func=AF.Copy),
                     reads=[b_sps], writes=[b_set])
            dv(lambda h, k=k: h.memset(Cl[:, k, :, :], 0.0), [], [b_set])
            cA, cB = cm[:, k, 0, :], cm[:, k, 1, :]
            dv(lambda h, k=k, cA=cA, g=g: h.tensor_scalar(out=Cl[:, k, 0, 16 * g:16 * g + 16], in0=cA, scalar1=sgn[:, 1:2],
                                                          scalar2=None, op0=ALU.mult), [b_c], [b_set])
            dv(lambda h, k=k, cB=cB, g=g: h.tensor_scalar(out=Cl[:, k, 1, 16 * g:16 * g + 16], in0=cB, scalar1=-1.0,
                                                          scalar2=None, op0=ALU.mult), [b_c], [b_set])
            dv(lambda h, k=k: h.tensor_scalar(out=Rk[:, k, :], in0=iota[:, 0:ST], scalar1=0.0, scalar2=col(k, 2),
                                              op0=ALU.mult, op1=ALU.add), [b_c, b_set], [b_set])
            dv(lambda h, k=k: h.tensor_scalar(out=Rot[:, k, :], in0=swp[:], scalar1=col(k, 12), scalar2=None, op0=ALU.mult),
               [b_c, b_set], [b_set])
            dv(lambda h, k=k, c5=c5: h.scalar_tensor_tensor(out=Rot[:, k, :], in0=idf[:], scalar=c5, in1=Rot[:, k, :],
                                                            op0=ALU.mult, op1=ALU.add), [b_c, b_tab, b_set], [b_set])

        steps = [(t, d, g) for t in range(NST) for d in range(2) for g in range(4)]

        def dvs(fn, reads, writes):
            return p.op(p.dve, fn, reads=reads, writes=writes, nosync=True)

        def stA(i):
            t, d, g = steps[i]
            k = d * 4 + g
            bi = i % 2
            usl = u_sb[:, d, t * ST:(t + 1) * ST]
            p.op(p.pe, lambda h, k=k, bi=bi, usl=usl: h.matmul(bu_ps[bi][:], lhsT=Bl[:, k, 0, :], rhs=usl, start=True, stop=True),
                 reads=[b_set, b_u], writes=[b_bu[bi]])
            p.op(p.pe, lambda h, k=k, bi=bi, usl=usl: h.matmul(bs_ps[bi][:], lhsT=Bl[:, k, 1, :], rhs=usl, start=True, stop=True),
                 reads=[b_set, b_u], writes=[b_bs[bi]])
            dvs(lambda h, k=k, bi=bi: h.tensor_tensor(out=t1[bi][:], in0=bu_ps[bi][:], in1=COS[:, k, 0:ST], op=ALU.mult),
               [b_bu[bi], b_tab], [b_t1[bi]])
            dvs(lambda h, k=k, bi=bi: h.tensor_tensor(out=t2[bi][:], in0=bs_ps[bi][:], in1=SIN[:, k, 0:ST], op=ALU.mult),
               [b_bs[bi], b_tab], [b_t2[bi]])
            dvs(lambda h, bi=bi: h.tensor_tensor(out=v_sb[bi][:], in0=t1[bi][:], in1=t2[bi][:], op=ALU.add),
               [b_t1[bi], b_t2[bi]], [b_v[bi]])

        def stB(i):
            t, d, g = steps[i]
            k = d * 4 + g
            bi = i % 2
            dvs(lambda h, k=k, bi=bi: h.tensor_tensor_scan(out=w_sb[bi][:], data0=Rk[:, k, :], data1=v_sb[bi][:],
                                                          initial=init[:, k:k + 1], op0=ALU.mult, op1=ALU.add),
               [b_v[bi], b_set, b_init[k]], [b_w[bi]])
            p.op(p.pe, lambda h, k=k, bi=bi: h.matmul(i_ps[:, k:k + 1], lhsT=Rot[:, k, :], rhs=w_sb[bi][:, ST - 1:ST],
                                                      start=True, stop=True),
                 reads=[b_w[bi], b_set], writes=[b_ips[k]])
            p.op(p.act, lambda h, k=k: h.activation(out=init[:, k:k + 1], in_=i_ps[:, k:k + 1], func=AF.Copy),
                 reads=[b_ips[k]], writes=[b_init[k]])

        def stC(i):
            t, d, g = steps[i]
            k = d * 4 + g
            bi = i % 2
            p.op(p.pool, lambda h, k=k, bi=bi: h.tensor_tensor(out=Wc[bi][:], in0=w_sb[bi][:], in1=COS[:, k, 0:ST], op=ALU.mult),
                 reads=[b_w[bi], b_tab], writes=[b_wc[bi]])
            dvs(lambda h, k=k, bi=bi: h.tensor_tensor(out=Ws[bi][:], in0=w_sb[bi][:], in1=SIN[:, k, 0:ST], op=ALU.mult),
               [b_w[bi], b_tab], [b_ws[bi]])
            p.op(p.pe, lambda h, k=k, bi=bi, d=d, g=g: h.matmul(y_ps[d][:], lhsT=Cl[:, k, 0, :], rhs=Wc[bi][:],
                                                                start=(g == 0), stop=False),
                 reads=[b_wc[bi], b_set], writes=[b_yps[d]])
            p.op(p.pe, lambda h, k=k, bi=bi, d=d, g=g: h.matmul(y_ps[d][:], lhsT=Cl[:, k, 1, :], rhs=Ws[bi][:],
                                                                start=False, stop=(g == 3)),
                 reads=[b_ws[bi], b_set], writes=[b_yps[d]])
            if g == 3:
                p.op(p.act, lambda h, d=d: h.activation(out=yst[d][:], in_=y_ps[d][:], func=AF.Copy),
                     reads=[b_yps[d]], writes=[b_yst[d]])
                p.dma(p.sp, ds_o, y_d[d, :, t * ST:(t + 1) * ST], yst[d][:], reads=[b_yst[d]])

        n = len(steps)
        for i in range(n + 1):
            if i < n:
                stA(i)
                stB(i)
            if i >= 1:
                stC(i - 1)
        p.finish(p.sp, [(ds_o.sem, ds_o.cnt)])
        with nc.Block() as block:
            p.replay(block)
    return nc


_CACHE = {}


def _run(nc, maps):
    res = run_bass_kernel_spmd(nc, maps, core_ids=list(range(NCORES)))
    return res.results


def s5_host_params(inp, l, q):
    prm = np.zeros((128, 8, 4), np.float32)
    bm = np.zeros((128, 8, 2, 16), np.float32)
    cm = np.zeros((128, 8, 2, 16), np.float32)
    for d in range(2):
        for gl in range(4):
            k = d * 4 + gl
            g = 4 * q + gl
            prm[:, k, 0] = np.tile(inp["s5_lam_re"][l, d, g], 2)
            prm[:, k, 1] = np.tile(inp["s5_lam_im"][l, d, g], 2)
            prm[:, k, 2] = inp["s5_log_step"][l, d, g]
            bre, bim = inp["s5_b_re"][l, d, g], inp["s5_b_im"][l, d, g]
            bm[:, k, 0] = np.concatenate([bre, bim], 0)
            bm[:, k, 1] = np.concatenate([bim, bre], 0)
            cre, cim = inp["s5_c_re"][l, d, g].T, inp["s5_c_im"][l, d, g].T
            cm[:, k, 0] = np.concatenate([cre, cim], 0)
            cm[:, k, 1] = np.concatenate([cim, cre], 0)
    return prm, bm, cm


def s5_consts():
    iota = np.broadcast_to(np.arange(ST + 1, dtype=np.float32), (128, ST + 1)).copy()
    sgn = np.zeros((128, 2), np.float32)
    sgn[:64, 0], sgn[64:, 0] = -1.0, 1.0
    sgn[:64, 1], sgn[64:, 1] = 1.0, -1.0
    idf = np.eye(128, dtype=np.float32)
    swp = np.zeros((128, 128), np.float32)
    for p_ in range(128):
        swp[p_, (p_ + 64) % 128] = 1.0
    return iota, sgn, idf, swp


def kernel(**inputs):
    inp = {k: np.asarray(v, dtype=np.float32) for k, v in inputs.items()}
    streams = [np.stack(stream_for_launch(j, inp)) for j in range(DEPTH + 1)]
    sizes = [s.shape[0] for s in streams]
    allb = np.concatenate(streams)
    per = allb.shape[0] // NCORES
    assert per * NCORES == allb.shape[0]
    ncw = build_w(per)
    res = _run(ncw, [{"wf": allb[c * per:(c + 1) * per].reshape(per * 256, 2048)} for c in range(NCORES)])
    wb = np.concatenate([np.asarray(r["wb"]).reshape(per, 128, WSLOT) for r in res])
    del allb, streams
    wbs = np.split(wb, np.cumsum(sizes)[:-1])
    cst = np.zeros((128, 256), np.float32)
    cst[:, :128] = 1.0 / 1024
    for hh in range(2):
        cst[hh * 64:(hh + 1) * 64, 128 + hh * 64:128 + (hh + 1) * 64] = 1.0 / 64
    cst = cst.astype(NPBF)
    idb = np.eye(128, dtype=np.float32).astype(NPBF)
    iota, sgn, idf, swp = s5_consts()
    acons = [attn_bias_consts(hd) for hd in range(4)]
    vidx = [vt_indices(ph) for ph in range(3)]

    x2 = inp["x"].reshape(B * S, D)
    xT = [np.ascontiguousarray(x2[c * TPC:(c + 1) * TPC].T) for c in range(NCORES)]
    tail_in = None
    nct = {}
    nca = None
    ncs = None
    for j in range(DEPTH + 1):
        key = (j >= 1, j <= DEPTH - 1)
        if key not in nct:
            nct[key] = build_t(*key)
        cols = cols_for_launch(j, inp)
        maps = []
        for c in range(NCORES):
            m = {"xin": xT[c], "wst": wbs[j], "cols": cols, "cst": cst}
            if j >= 1:
                m.update(tail_in[c])
            maps.append(m)
        res = _run(nct[key], maps)
        xT = [np.asarray(r["xout"]) for r in res]
        if j == DEPTH:
            break
        l = j
        cpb = NCORES // B
        pT = [np.concatenate([np.asarray(res[b * cpb + i]["pout"]) for i in range(cpb)], axis=1) for b in range(B)]
        if nca is None:
            nca = build_attn()
        maps = []
        for c in range(NCORES):
            b, hd = c // 4, c % 4
            P = pT[b]
            maps.append(attn_host_inputs(P, hd, acons[hd], inp["na_rel_bias"][l][hd], inp["a_sink"][l][hd], idb, idf, vidx))
        ares = _run(nca, maps)
        if ncs is None:
            ncs = build_s5()
        maps = []
        for c in range(NCORES):
            b, q = c // 4, c % 4
            uu = pT[b][512 + 64 * q:512 + 64 * q + 64]
            prm, bm, cm = s5_host_params(inp, l, q)
            maps.append({"u": np.ascontiguousarray(np.stack([uu, uu[:, ::-1]])), "prm": prm, "bm": bm, "cm": cm,
                         "iota": iota, "sgn": sgn, "idf": idf, "swp": swp})
        sres = _run(ncs, maps)
        tail_in = []
        oT, yf, yb = [], [], []
        for b in range(B):
            o = np.zeros((768, S), NPBF)
            f = np.zeros((256, S), np.float32)
            r_ = np.zeros((256, S), np.float32)
            for hd in range(4):
                oo = np.asarray(ares[b * 4 + hd]["o_out"])
                for ph in range(3):
                    o[ph * 256 + hd * 64:ph * 256 + hd * 64 + 64] = oo[ph]
                yy = np.asarray(sres[b * 4 + hd]["y_out"])
                f[hd * 64:hd * 64 + 64] = yy[0]
                r_[hd * 64:hd * 64 + 64] = yy[1][:, ::-1]
            oT.append(o)
            yf.append(f)
            yb.append(r_)
        for c in range(NCORES):
            b, ch = c // cpb, c % cpb
            sl = slice(ch * TPC, (ch + 1) * TPC)
            tail_in.append({"o_in": np.ascontiguousarray(oT[b][:, sl]), "yf_in": np.ascontiguousarray(yf[b][:, sl]),
                            "yb_in": np.ascontiguousarray(yb[b][:, sl]),
                            "u_in": np.ascontiguousarray(pT[b][512:768, sl])})
    out = np.concatenate([x.T for x in xT], axis=0).reshape(B, S, D).astype(np.float32)
    return out
```

```python
import numpy as np
import ml_dtypes
import concourse.bass as bass
import concourse.mybir as mybir
from concourse.bass_utils import run_bass_kernel_spmd

F32 = mybir.dt.float32
BF16 = mybir.dt.bfloat16
I32 = mybir.dt.int32
AF = mybir.ActivationFunctionType
ALU = mybir.AluOpType
NPBF = ml_dtypes.bfloat16

NCORES = 8
D = 1024
DFF = 2816
NFC = DFF // 128
B = 2
S = 16384
DEPTH = 4
TPC = B * S // NCORES
TT = 512
NT = TPC // TT
WSLOT = 4096
NSLOT = 5
EPS = 1e-6
NEG = -30000.0


class Buf:
    __slots__ = ("w", "r")

    def __init__(self):
        self.w = None
        self.r = {}


class Eng:
    def __init__(self, name, sem, selfsync):
        self.name = name
        self.sem = sem
        self.cnt = 0
        self.ops = []
        self.waited = {}
        self.selfsync = selfsync


class DSem:
    def __init__(self, sem):
        self.sem = sem
        self.cnt = 0


class Prog:
    def __init__(self, nc, sems):
        self.nc = nc
        self.pe = Eng("tensor", sems["tensor"], False)
        self.act = Eng("scalar", sems["scalar"], True)
        self.dve = Eng("vector", sems["vector"], True)
        self.pool = Eng("gpsimd", sems["gpsimd"], True)
        self.sp = Eng("sync", sems["sync"], False)
        self.engs = [self.pe, self.act, self.dve, self.pool, self.sp]

    def _wait(self, eng, tok):
        sem, val = tok
        k = id(sem)
        if eng.waited.get(k, 0) >= val:
            return
        eng.waited[k] = val
        eng.ops.append(lambda h, sm=sem, v=val: h.wait_ge(sm, v))

    def _deps(self, eng, reads, writes, nosync=False):
        skip_self = nosync or not eng.selfsync
        for b in reads:
            if b.w is not None:
                if b.w[0] is eng.sem and skip_self:
                    continue
                self._wait(eng, b.w)
        for b in writes:
            for sem, val in b.r.values():
                if sem is eng.sem and skip_self:
                    continue
                self._wait(eng, (sem, val))
            if b.w is not None:
                if b.w[0] is eng.sem and skip_self:
                    continue
                self._wait(eng, b.w)

    def _mark(self, tok, reads, writes):
        for b in reads:
            k = id(tok[0])
            if k not in b.r or b.r[k][1] < tok[1]:
                b.r[k] = tok
        for b in writes:
            b.w = tok
            b.r = {}

    def op(self, eng, fn, reads=(), writes=(), nosync=False):
        self._deps(eng, reads, writes, nosync)
        eng.cnt += 1
        tok = (eng.sem, eng.cnt)
        eng.ops.append(lambda h, f=fn, sm=eng.sem: f(h).then_inc(sm, 1))
        self._mark(tok, reads, writes)
        return tok

    def dma(self, eng, dsem, out, in_, reads=(), writes=()):
        self._deps(eng, reads, writes)
        dsem.cnt += 16
        tok = (dsem.sem, dsem.cnt)
        eng.ops.append(lambda h, o=out, i=in_, sm=dsem.sem: h.dma_start(out=o, in_=i).then_inc(sm, 16))
        self._mark(tok, reads, writes)
        return tok

    def finish(self, eng, toks):
        for t in toks:
            self._wait(eng, t)

    def replay(self, block):
        for e in self.engs:
            def body(h, e=e):
                for f in e.ops:
                    f(h)
            getattr(block, e.name)(body)


class Env:
    def __init__(self, nc, stack):
        self.nc = nc
        self.stack = stack
        self.nsem = 0
        sems = {n: self.sem("p_" + n) for n in ("tensor", "scalar", "vector", "gpsimd", "sync")}
        self.p = Prog(nc, sems)

    def sem(self, name):
        self.nsem += 1
        return self.stack.enter_context(self.nc.semaphore(name))

    def dsem(self, name):
        return DSem(self.sem(name))

    def sb(self, name, shape, dt):
        return self.stack.enter_context(self.nc.sbuf_tensor(name, list(shape), dt))

    def ps(self, name, shape, dt=F32):
        return self.stack.enter_context(self.nc.psum_tensor(name, list(shape), dt))


def dram_in(nc, name, shape, dt):
    return nc.dram_tensor(name, list(shape), dt, kind="ExternalInput").ap()


def dram_out(nc, name, shape, dt):
    return nc.dram_tensor(name, list(shape), dt, kind="ExternalOutput").ap()


def build_w(nblk):
    from contextlib import ExitStack
    nc = bass.Bass("TRN2", target_bir_lowering=False)
    rows = nblk * 256
    wf = dram_in(nc, "wf", [rows, 2048], F32)
    wb = dram_out(nc, "wb", [rows, 2048], BF16)
    with ExitStack() as st:
        env = Env(nc, st)
        p = env.p
        ds = env.dsem("d")
        toks = []
        for b in range(nblk):
            toks.append(p.dma(p.pool, ds, wb[b * 256:(b + 1) * 256, :], wf[b * 256:(b + 1) * 256, :]))
        p.finish(p.pool, toks[-1:])
        with nc.Block() as block:
            p.replay(block)
    return nc


def _pad_block(a):
    a = a.reshape(128, -1)
    out = np.zeros((128, WSLOT), np.float32)
    out[:, : a.shape[1]] = a
    return out


def ffn_blocks(w_in, w_out):
    blocks = []
    wi = w_in.reshape(8, 128, 2, NFC, 128)
    for blk in range(NFC // 2):
        sub = wi[:, :, :, 2 * blk:2 * blk + 2, :]
        blocks.append(_pad_block(sub.transpose(1, 3, 2, 0, 4)))
    wo = w_out.reshape(NFC, 128, 8, 128)
    for dc in range(8):
        blocks.append(_pad_block(wo[:, :, dc, :].transpose(1, 0, 2)))
    return blocks


def proj_blocks(w, nmc):
    nk = w.shape[0] // 128
    wr = w.reshape(nk, 128, nmc, 128)
    blocks = []
    for b0 in range(0, nmc, 4):
        sub = wr[:, :, b0:b0 + 4, :]
        blocks.append(_pad_block(sub.transpose(1, 2, 0, 3)))
    return blocks


def stream_for_launch(j, inp):
    blocks = []
    if j >= 1:
        l = j - 1
        blocks += proj_blocks(inp["s5_w_glu"][l], 2)
        blocks += proj_blocks(inp["w_out"][l], 8)
        blocks += ffn_blocks(inp["ffn2_w_in"][l], inp["ffn2_w_out"][l])
    if j <= DEPTH - 1:
        l = j
        blocks += ffn_blocks(inp["ffn1_w_in"][l], inp["ffn1_w_out"][l])
        blocks += proj_blocks(inp["w_in"][l], 18)
    return blocks


QK_CHUNK_GAIN = {0: 0, 1: 0, 2: 1, 6: 2, 7: 2, 8: 3, 9: 3, 12: 4, 13: 4, 14: 5, 15: 5}


def cols_for_launch(j, inp):
    c = np.ones((128, 44), np.float32)
    if j >= 1:
        l = j - 1
        c[:, 0:8] = inp["ffn2_norm"][l].reshape(8, 128).T
        c[:, 42:44] = inp["s5_d"][l].reshape(2, 128).T
    if j <= DEPTH - 1:
        l = j
        c[:, 8:16] = inp["ffn1_norm"][l].reshape(8, 128).T
        c[:, 16:24] = inp["mix_norm"][l].reshape(8, 128).T
        for mc, gi in QK_CHUNK_GAIN.items():
            c[:, 24 + mc] = np.tile(inp["qk_gain"][l, gi], 2)
    return c


def build_t(has_tail, has_head):
    from contextlib import ExitStack
    nc = bass.Bass("TRN2", target_bir_lowering=False)
    nblk = (22 if has_tail else 0) + (24 if has_head else 0)
    xin = dram_in(nc, "xin", [D, TPC], F32)
    wst = dram_in(nc, "wst", [nblk, 128, WSLOT], BF16)
    cols_d = dram_in(nc, "cols", [128, 44], F32)
    cst_d = dram_in(nc, "cst", [128, 256], BF16)
    if has_tail:
        o_d = dram_in(nc, "o_in", [768, TPC], BF16)
        yf_d = dram_in(nc, "yf_in", [256, TPC], F32)
        yb_d = dram_in(nc, "yb_in", [256, TPC], F32)
        u_d = dram_in(nc, "u_in", [256, TPC], BF16)
    xout = dram_out(nc, "xout", [D, TPC], F32)
    if has_head:
        pout_d = dram_out(nc, "pout", [2304, TPC], BF16)

    with ExitStack() as st:
        env = Env(nc, st)
        p = env.p
        x_sb = [env.sb(f"x{i}", [128, 8, TT], F32) for i in range(2)]
        h_sb = env.sb("h", [128, 8, TT], BF16)
        sq_sb = env.sb("sq", [128, 8, TT], BF16)
        a_sb = env.sb("a", [128, NFC, TT], BF16)
        rs_sb = [env.sb(f"rs{i}", [128, TT], F32) for i in range(2)]
        s_sb = [env.sb(f"s{i}", [128, TT], F32) for i in range(2)]
        ring = env.sb("ring", [128, NSLOT, WSLOT], BF16)
        cols = env.sb("colsb", [128, 44], F32)
        cst = env.sb("cstb", [128, 256], BF16)
        epsc = env.sb("epsc", [128, 1], F32)
        if has_head:
            po_sb = env.sb("po", [128, 18, TT], BF16)
        if has_tail:
            mx_sb = env.sb("mx", [128, 8, TT], BF16)
            yf_sb = env.sb("yf", [128, 2, TT], F32)
            yb_sb = env.sb("yb", [128, 2, TT], F32)
            u_sb = env.sb("u", [128, 2, TT], BF16)
            g_sb = env.sb("g", [128, 2, TT], F32)
            t_sb = env.sb("tg", [128, 2, TT], F32)
            gb_sb = env.sb("gb", [128, 2, TT], BF16)
        bank = [env.ps(f"bk{i}", [128, TT]) for i in range(8)]
        b_x = [[Buf() for _ in range(8)] for _ in range(2)]
        b_h = [Buf() for _ in range(8)]
        b_sq = [Buf() for _ in range(8)]
        b_a = [Buf() for _ in range(NFC)]
        b_rs = [Buf(), Buf()]
        b_s = [Buf(), Buf()]
        b_ring = [Buf() for _ in range(NSLOT)]
        b_bank = [Buf() for _ in range(8)]
        b_const = Buf()
        b_po = [Buf() for _ in range(18)]
        b_mx = [Buf() for _ in range(8)]
        b_y = Buf()
        b_g = Buf()
        b_tg = Buf()
        b_gb = Buf()
        ds_ring = [env.dsem(f"dr{i}") for i in range(NSLOT)]
        ds_x = [env.dsem(f"dx{i}") for i in range(2)]
        ds_c = env.dsem("dc")
        ds_out = env.dsem("dout")
        ds_pout = env.dsem("dpo")
        ds_tail = env.dsem("dtl")

        ones_mean = cst[:, 0:128]
        blk_ones = cst[:, 128:256]

        p.dma(p.pool, ds_c, cols[:], cols_d[:, :], writes=[b_const])
        p.dma(p.pool, ds_c, cst[:], cst_d[:, :], writes=[b_const])
        p.op(p.dve, lambda h: h.memset(epsc[:], EPS), writes=[b_const])

        wstate = {"n": 0}

        def next_block(tile_idx, bidx):
            n = wstate["n"]
            wstate["n"] = n + 1
            s = n % NSLOT
            p.dma(p.sp, ds_ring[s], ring[:, s, :], wst[bidx, :, :], writes=[b_ring[s]])
            return s

        order = []
        per_tile = list(range(nblk))
        PREF = NSLOT - 1
        issued = {"k": 0}
        total_blocks = NT * nblk

        def ensure_issued(upto):
            while issued["k"] < min(upto, total_blocks):
                k = issued["k"]
                next_block(k // nblk, k % nblk)
                issued["k"] = k + 1

        used = {"k": 0}

        def take_block():
            k = used["k"]
            used["k"] = k + 1
            ensure_issued(k + 1)
            s = k % NSLOT
            return s

        def release_prefetch():
            ensure_issued(used["k"] + PREF)

        def load_x(t, xb):
            for c in range(8):
                pass
            src = xin[:, t * TT:(t + 1) * TT].rearrange("(c p) t -> p c t", p=128)
            p.dma(p.pool, ds_x[xb], x_sb[xb][:], src, writes=b_x[xb])

        def rmsnorm_to_h(xb, gcol0):
            xs = x_sb[xb]
            for c in range(8):
                p.op(p.act, lambda h, c=c: h.activation(out=sq_sb[:, c, :], in_=xs[:, c, :], func=AF.Square),
                     reads=[b_x[xb][c]], writes=[b_sq[c]])
            for c in range(8):
                p.op(p.pe, lambda h, c=c: h.matmul(bank[6][:], lhsT=ones_mean, rhs=sq_sb[:, c, :],
                                                   start=(c == 0), stop=(c == 7)),
                     reads=[b_sq[c], b_const], writes=[b_bank[6]])
            p.op(p.act, lambda h: h.activation(out=rs_sb[0][:], in_=bank[6][:], func=AF.Sqrt, bias=epsc[:, 0:1]),
                 reads=[b_bank[6], b_const], writes=[b_rs[0]])
            p.op(p.dve, lambda h: h.reciprocal(out=rs_sb[1][:], in_=rs_sb[0][:]), reads=[b_rs[0]], writes=[b_rs[1]])
            for c in range(8):
                p.op(p.dve, lambda h, c=c: h.scalar_tensor_tensor(
                    out=h_sb[:, c, :], in0=xs[:, c, :], scalar=cols[:, gcol0 + c:gcol0 + c + 1],
                    in1=rs_sb[1][:], op0=ALU.mult, op1=ALU.mult),
                    reads=[b_x[xb][c], b_rs[1], b_const], writes=[b_h[c]])

        def ffn(xb, gcol0):
            xs = x_sb[xb]
            rmsnorm_to_h(xb, gcol0)
            for blk in range(NFC // 2):
                s = take_block()
                wv = ring[:, s, :].rearrange("p (f g k m) -> p f g k m", f=2, g=2, k=8)
                for fcl in range(2):
                    fc = 2 * blk + fcl
                    pb = fc % 2
                    G, U = bank[2 * pb], bank[2 * pb + 1]
                    bG, bU = b_bank[2 * pb], b_bank[2 * pb + 1]
                    for kc in range(8):
                        p.op(p.pe, lambda h, wv=wv, kc=kc, fcl=fcl, G=G: h.matmul(
                            G[:], lhsT=wv[:, fcl, 0, kc, :], rhs=h_sb[:, kc, :], start=(kc == 0), stop=(kc == 7)),
                            reads=[b_ring[s], b_h[kc]], writes=[bG])
                    for kc in range(8):
                        p.op(p.pe, lambda h, wv=wv, kc=kc, fcl=fcl, U=U: h.matmul(
                            U[:], lhsT=wv[:, fcl, 1, kc, :], rhs=h_sb[:, kc, :], start=(kc == 0), stop=(kc == 7)),
                            reads=[b_ring[s], b_h[kc]], writes=[bU])
                    p.op(p.act, lambda h, G=G, pb=pb: h.activation(out=s_sb[pb][:], in_=G[:], func=AF.Silu),
                         reads=[bG], writes=[b_s[pb]])
                    p.op(p.dve, lambda h, U=U, pb=pb, fc=fc: h.tensor_tensor(
                        out=a_sb[:, fc, :], in0=U[:], in1=s_sb[pb][:], op=ALU.mult),
                        reads=[bU, b_s[pb]], writes=[b_a[fc]])
                release_prefetch()
            for dc in range(8):
                s = take_block()
                wv = ring[:, s, 0:NFC * 128].rearrange("p (f m) -> p f m", f=NFC)
                O = bank[4 + dc % 2]
                bO = b_bank[4 + dc % 2]
                for fc in range(NFC):
                    p.op(p.pe, lambda h, wv=wv, fc=fc, O=O: h.matmul(
                        O[:], lhsT=wv[:, fc, :], rhs=a_sb[:, fc, :], start=(fc == 0), stop=(fc == NFC - 1)),
                        reads=[b_ring[s], b_a[fc]], writes=[bO])
                p.op(p.dve, lambda h, dc=dc, O=O: h.scalar_tensor_tensor(
                    out=xs[:, dc, :], in0=O[:], scalar=0.5, in1=xs[:, dc, :], op0=ALU.mult, op1=ALU.add),
                    reads=[bO, b_x[xb][dc]], writes=[b_x[xb][dc]])
                release_prefetch()

        def head_proj(xb, t):
            xs = x_sb[xb]
            rmsnorm_to_h(xb, 16)
            for b0 in range(5):
                s = take_block()
                wv = ring[:, s, :].rearrange("p (c k m) -> p c k m", c=4, k=8)
                for mcl in range(4):
                    mc = 4 * b0 + mcl
                    if mc >= 18:
                        break
                    P = bank[mc % 4]
                    bP = b_bank[mc % 4]
                    for kc in range(8):
                        p.op(p.pe, lambda h, wv=wv, kc=kc, mcl=mcl, P=P: h.matmul(
                            P[:], lhsT=wv[:, mcl, kc, :], rhs=h_sb[:, kc, :], start=(kc == 0), stop=(kc == 7)),
                            reads=[b_ring[s], b_h[kc]], writes=[bP])
                    if mc in QK_CHUNK_GAIN:
                        q = mc % 2
                        sqb = s_sb[q]
                        p.op(p.act, lambda h, P=P, q=q: h.activation(out=sq_sb[:, q, :], in_=P[:], func=AF.Square),
                             reads=[bP], writes=[b_sq[q]])
                        p.op(p.pe, lambda h, q=q: h.matmul(bank[7][:], lhsT=blk_ones, rhs=sq_sb[:, q, :],
                                                           start=True, stop=True),
                             reads=[b_sq[q], b_const], writes=[b_bank[7]])
                        p.op(p.act, lambda h: h.activation(out=rs_sb[0][:], in_=bank[7][:], func=AF.Sqrt,
                                                           bias=epsc[:, 0:1]),
                             reads=[b_bank[7], b_const], writes=[b_rs[0]])
                        p.op(p.dve, lambda h: h.reciprocal(out=rs_sb[1][:], in_=rs_sb[0][:]),
                             reads=[b_rs[0]], writes=[b_rs[1]])
                        p.op(p.dve, lambda h, P=P, mc=mc: h.scalar_tensor_tensor(
                            out=po_sb[:, mc, :], in0=P[:], scalar=cols[:, 24 + mc:25 + mc], in1=rs_sb[1][:],
                            op0=ALU.mult, op1=ALU.mult),
                            reads=[bP, b_rs[1], b_const], writes=[b_po[mc]])
                    else:
                        p.op(p.act, lambda h, P=P, mc=mc: h.activation(out=po_sb[:, mc, :], in_=P[:], func=AF.Copy),
                             reads=[bP], writes=[b_po[mc]])
                release_prefetch()
            dst = pout_d[:, t * TT:(t + 1) * TT].rearrange("(c p) t -> p c t", p=128)
            return p.dma(p.pool, ds_pout, dst, po_sb[:], reads=b_po)

        def tail(xb, t):
            xs = x_sb[xb]
            tsl = slice(t * TT, (t + 1) * TT)
            for (r0, c0) in ((0, 0), (256, 4), (512, 6)):
                src = o_d[r0:r0 + 256, tsl].rearrange("(c p) t -> p c t", p=128)
                p.dma(p.pool, ds_tail, mx_sb[:, c0:c0 + 2, :], src, writes=[b_mx[c0], b_mx[c0 + 1]])
            p.dma(p.pool, ds_tail, yf_sb[:], yf_d[:, tsl].rearrange("(c p) t -> p c t", p=128), writes=[b_y])
            p.dma(p.pool, ds_tail, yb_sb[:], yb_d[:, tsl].rearrange("(c p) t -> p c t", p=128), writes=[b_y])
            ltok = p.dma(p.pool, ds_tail, u_sb[:], u_d[:, tsl].rearrange("(c p) t -> p c t", p=128), writes=[b_y])
            for bb in (b_mx[0], b_mx[1], b_mx[4], b_mx[5], b_mx[6], b_mx[7], b_y):
                bb.w = ltok
            p.op(p.dve, lambda h: h.tensor_tensor(out=yf_sb[:], in0=yf_sb[:], in1=yb_sb[:], op=ALU.add),
                 reads=[b_y], writes=[b_y])
            for c in range(2):
                p.op(p.dve, lambda h, c=c: h.scalar_tensor_tensor(
                    out=yf_sb[:, c, :], in0=u_sb[:, c, :], scalar=cols[:, 42 + c:43 + c], in1=yf_sb[:, c, :],
                    op0=ALU.mult, op1=ALU.add), reads=[b_y, b_const], writes=[b_y])
            p.op(p.dve, lambda h: h.tensor_tensor(out=t_sb[:], in0=yf_sb[:], in1=yf_sb[:], op=ALU.mult),
                 reads=[b_y], writes=[b_tg])
            p.op(p.dve, lambda h: h.tensor_scalar(out=t_sb[:], in0=t_sb[:], scalar1=0.044715, scalar2=1.0,
                                                  op0=ALU.mult, op1=ALU.add), reads=[b_tg], writes=[b_tg])
            p.op(p.dve, lambda h: h.tensor_tensor(out=t_sb[:], in0=t_sb[:], in1=yf_sb[:], op=ALU.mult),
                 reads=[b_tg, b_y], writes=[b_tg])
            p.op(p.act, lambda h: h.activation(out=t_sb[:], in_=t_sb[:], func=AF.Sigmoid, scale=1.5957691216057308),
                 reads=[b_tg], writes=[b_tg])
            p.op(p.dve, lambda h: h.tensor_tensor(out=g_sb[:], in0=t_sb[:], in1=yf_sb[:], op=ALU.mult),
                 reads=[b_tg, b_y], writes=[b_g])
            p.op(p.act, lambda h: h.activation(out=gb_sb[:], in_=g_sb[:], func=AF.Copy), reads=[b_g], writes=[b_gb])
            s = take_block()
            wv = ring[:, s, 0:512].rearrange("p (c k m) -> p c k m", c=2, k=2)
            for mc in range(2):
                Z = bank[mc]
                for kc in range(2):
                    p.op(p.pe, lambda h, wv=wv, mc=mc, kc=kc, Z=Z: h.matmul(
                        Z[:], lhsT=wv[:, mc, kc, :], rhs=gb_sb[:, kc, :], start=(kc == 0), stop=(kc == 1)),
                        reads=[b_ring[s], b_gb], writes=[b_bank[mc]])
                p.op(p.act, lambda h, mc=mc, Z=Z: h.activation(out=s_sb[mc][:], in_=Z[:], func=AF.Sigmoid),
                     reads=[b_bank[mc]], writes=[b_s[mc]])
                p.op(p.dve, lambda h, mc=mc: h.tensor_tensor(out=mx_sb[:, 2 + mc, :], in0=g_sb[:, mc, :],
                                                             in1=s_sb[mc][:], op=ALU.mult),
                     reads=[b_g, b_s[mc]], writes=[b_mx[2 + mc]])
            release_prefetch()
            for b0 in range(2):
                s = take_block()
                wv = ring[:, s, :].rearrange("p (c k m) -> p c k m", c=4, k=8)
                for mcl in range(4):
                    dc = 4 * b0 + mcl
                    O = bank[4 + dc % 2]
                    bO = b_bank[4 + dc % 2]
                    for kc in range(8):
                        p.op(p.pe, lambda h, wv=wv, kc=kc, mcl=mcl, O=O: h.matmul(
                            O[:], lhsT=wv[:, mcl, kc, :], rhs=mx_sb[:, kc, :], start=(kc == 0), stop=(kc == 7)),
                            reads=[b_ring[s], b_mx[kc]], writes=[bO])
                    p.op(p.dve, lambda h, dc=dc, O=O: h.tensor_tensor(
                        out=xs[:, dc, :], in0=O[:], in1=xs[:, dc, :], op=ALU.add),
                        reads=[bO, b_x[xb][dc]], writes=[b_x[xb][dc]])
                release_prefetch()

        out_toks = []
        load_x(0, 0)
        ensure_issued(PREF)
        for t in range(NT):
            xb = t % 2
            if t + 1 < NT:
                load_x(t + 1, 1 - xb)
            if has_tail:
                tail(xb, t)
                ffn(xb, 0)
            if has_head:
                ffn(xb, 8)
                out_toks.append(head_proj(xb, t))
            dst = xout[:, t * TT:(t + 1) * TT].rearrange("(c p) t -> p c t", p=128)
            out_toks.append(p.dma(p.pool, ds_out, dst, x_sb[xb][:], reads=b_x[xb]))
        p.finish(p.pool, [(ds_out.sem, ds_out.cnt), (ds_pout.sem, ds_pout.cnt)] if has_head
                 else [(ds_out.sem, ds_out.cnt)])
        with nc.Block() as block:
            p.replay(block)
    return nc


PAD = 1024
SP_ = S + 2 * PAD
SBLK = 2048
NVT = 405
GRID_W = 64


def fap(ap2d, off, pat):
    return bass.AP(ap2d.tensor, ap2d.offset + off, [list(ap2d.ap[0])] + [list(x) for x in pat])


def attn_plan():
    plans = []
    vt = [(PAD + 128 * t, [[1, 128]]) for t in range(128)]
    qb = []
    for b in range(128):
        kts = []
        for dt_, bi in ((-1, 0), (0, 1), (1, 2)):
            t = b + dt_
            if 0 <= t < 128:
                kts.append((PAD + 128 * t, [[1, 128]], t, bi))
        sb = (128 * b) // SBLK
        qb.append(dict(q=(128 * b, [[1, 128]]), kts=kts, dst=(128 * b - sb * SBLK, [[1, 128]]), add=False, sb=sb))
    plans.append((vt, qb))
    vt = []
    vidx = {}
    for ci, d in enumerate((1, 4, 16)):
        L = S // d
        for rho in range(d):
            for bp in range(L // 128 + 1):
                vidx[(ci, rho, bp)] = len(vt)
                vt.append((PAD + d * 64 * (2 * bp - 1) + rho, [[d, 128]]))
    assert len(vt) == NVT
    qb = []
    for sb in range(S // SBLK):
        for ci, d in enumerate((1, 4, 16)):
            L = S // d
            nb = L // 128
            for rho in range(d):
                for b in range(nb):
                    t0 = d * 128 * b + rho
                    if t0 // SBLK != sb:
                        continue
                    kts = []
                    for bp, base in ((b, 0), (b + 1, 1)):
                        bi = 4 * ci + base
                        if base == 0 and b == 0:
                            bi = 4 * ci + 2
                        if base == 1 and b == nb - 1:
                            bi = 4 * ci + 3
                        off, pat = vt[vidx[(ci, rho, bp)]]
                        kts.append((off, pat, vidx[(ci, rho, bp)], bi))
                    qb.append(dict(q=(t0, [[d, 128]]), kts=kts, dst=(t0 - sb * SBLK, [[d, 128]]), add=(ci > 0), sb=sb))
    plans.append((vt, qb))
    vt = [(PAD + 128 * t, [[1, 128]]) for t in range(128)]
    qb = []
    for i in range(32):
        rs = min(max(8 * i - 4, 0), 240)
        icls = 0 if i == 0 else (2 if i == 31 else 1)
        for j in range(4):
            kts = []
            for m in range(8):
                t = rs // 2 + m
                kts.append((PAD + 128 * t, [[1, 128]], t, (icls * 4 + j) * 8 + m))
            t0 = 8 * i * GRID_W + 16 * j
            sb = t0 // SBLK
            qb.append(dict(q=((i * 4 + j) * 128, [[1, 128]]), kts=kts, dst=(t0 - sb * SBLK, [[GRID_W, 8], [1, 16]]),
                           add=False, sb=sb))
    plans.append((vt, qb))
    return plans


def attn_bias_consts(hd):
    slopes = 2.0 ** (-np.arange(1, 9, dtype=np.float64))
    kk = np.arange(128)[:, None]
    qq = np.arange(128)[None, :]
    A = np.zeros((3, 128, 128), np.float32)
    for bi, sh in enumerate((-128, 0, 128)):
        rel = kk + sh - qq
        A[bi] = np.where(np.abs(rel) <= 128, -slopes[hd] * np.abs(rel) * 8.0, NEG * 8)
    C = np.zeros((12, 128, 128), np.float32)
    for ci, d in enumerate((1, 4, 16)):
        for base, sh in ((0, -64), (1, 64)):
            rel = kk + sh - qq
            m = np.where(np.abs(rel) <= 64, -slopes[4 + hd] * d * np.abs(rel) * 8.0, NEG * 8)
            C[4 * ci + base] = m
        first = C[4 * ci + 0].copy()
        first[:64, :] = NEG * 8
        C[4 * ci + 2] = first
        last = C[4 * ci + 1].copy()
        last[64:, :] = NEG * 8
        C[4 * ci + 3] = last
    ri = np.zeros((96, 128, 128), np.int64)
    cidx = np.zeros((96, 128, 128), np.int64)
    Dm = np.zeros((96, 128, 128), np.float32)
    rows = S // GRID_W
    for icls, i in enumerate((0, 5, 31)):
        rs = min(max(8 * i - 4, 0), rows - 16)
        for j in range(4):
            for m in range(8):
                idx = (icls * 4 + j) * 8 + m
                krow = (rs + 2 * m + np.arange(128) // 64)[:, None]
                kcol = (np.arange(128) % 64)[:, None]
                qrow = (8 * i + np.arange(128) // 16)[None, :]
                qcol = (16 * j + np.arange(128) % 16)[None, :]
                wr = np.clip(qrow - 4, 0, rows - 8)
                wc = np.clip(qcol - 8, 0, GRID_W - 16)
                valid = (krow >= wr) & (krow < wr + 8) & (kcol >= wc) & (kcol < wc + 16)
                ri[idx] = np.clip(krow - qrow + 7, 0, 14) + 0 * qcol
                cidx[idx] = np.clip(kcol - qcol + 15, 0, 30) + 0 * qrow
                Dm[idx] = np.where(valid, 0.0, NEG * 8)
    return A.astype(NPBF), C.astype(NPBF), (ri, cidx), Dm.astype(NPBF)


def vt_indices(ph):
    vts, _ = attn_plan()[ph]
    idx = np.zeros((len(vts), 128), np.int64)
    for n, (off, pat) in enumerate(vts):
        st_, cnt = pat[0]
        assert len(pat) == 1 and cnt == 128
        idx[n] = off - PAD + st_ * np.arange(128)
    idx[(idx < 0) | (idx >= S)] = -1
    return idx


def build_attn(phases=(0, 1, 2)):
    from contextlib import ExitStack
    nc = bass.Bass("TRN2", target_bir_lowering=False)
    qk_d = dram_in(nc, "qk", [6, 64, S], BF16)
    vt_d = dram_in(nc, "vt", [128, 128 + NVT + 128, 65], BF16)
    bA_d = dram_in(nc, "bA", [128, 3, 128], BF16)
    bC_d = dram_in(nc, "bC", [128, 12, 128], BF16)
    bDg_d = dram_in(nc, "bDg", [128, 96, 128], BF16)
    bDm_d = dram_in(nc, "bDm", [128, 96, 128], BF16)
    sink_d = dram_in(nc, "sink", [128, 1], F32)
    idb_d = dram_in(nc, "idb", [128, 128], BF16)
    idf_d = dram_in(nc, "idf", [128, 128], F32)
    o_d = dram_out(nc, "o_out", [3, 64, S], BF16)
    plans = attn_plan()
    vbase = [0, 128, 128 + NVT]
    with ExitStack() as st:
        env = Env(nc, st)
        p = env.p
        qT = env.sb("qT", [64, S], BF16)
        kT = env.sb("kT", [64, SP_], BF16)
        Vt = env.sb("Vt", [128, NVT, 65], BF16)
        bias = env.sb("bias", [128, 96, 128], BF16)
        idb = env.sb("idbs", [128, 128], BF16)
        idf = env.sb("idfs", [128, 128], F32)
        es = env.sb("es", [128, 1], F32)
        Oacc = env.sb("Oacc", [64, 2, SBLK], F32)
        ost = env.sb("ost", [64, SBLK], BF16)
        Pt = [env.sb(f"Pt{i}", [128, 8, 128], BF16) for i in range(2)]
        Osb = [env.sb(f"Osb{i}", [128, 2, 64], F32) for i in range(2)]
        Sps = [env.ps(f"Sps{i}", [128, 8, 128]) for i in range(2)]
        Ops_t = env.ps("Ops", [128, 2, 65])
        Ops = [Ops_t[:, 0, :], Ops_t[:, 1, :]]
        OTps = [env.ps(f"OTps{i}", [64, 2, 128]) for i in range(2)]
        b_q, b_k, b_vt, b_bias, b_c = Buf(), Buf(), Buf(), Buf(), Buf()
        b_oacc, b_ost = Buf(), Buf()
        b_pt = [Buf(), Buf()]
        b_osb = [Buf(), Buf()]
        b_sps = [Buf(), Buf()]
        b_ops = [Buf(), Buf()]
        b_otps = [Buf(), Buf()]
        ds_c = env.dsem("dc")
        ds_q = env.dsem("dq")
        ds_b = env.dsem("db")
        ds_v = env.dsem("dv")
        ds_o = env.dsem("do")

        p.dma(p.pool, ds_c, idb[:], idb_d[:, :], writes=[b_c])
        p.dma(p.pool, ds_c, idf[:], idf_d[:, :], writes=[b_c])
        p.dma(p.pool, ds_c, es[:], sink_d[:, :], writes=[b_c])
        p.op(p.act, lambda h: h.activation(out=es[:], in_=es[:], func=AF.Exp), reads=[b_c], writes=[b_c])
        p.op(p.dve, lambda h: h.memset(kT[:, 0:PAD], 0.0), writes=[b_k])
        p.op(p.dve, lambda h: h.memset(kT[:, PAD + S:SP_], 0.0), writes=[b_k])

        for ph in phases:
            vts, qbs = plans[ph]
            lt = p.dma(p.sp, ds_q, qT[:], qk_d[2 * ph + 0, :, :], writes=[b_q])
            lt = p.dma(p.sp, ds_q, kT[:, PAD:PAD + S], qk_d[2 * ph + 1, :, :], writes=[b_k])
            b_q.w = lt
            b_k.w = lt
            if ph == 0:
                p.dma(p.pool, ds_b, bias[:, 0:3, :], bA_d[:, :, :], writes=[b_bias])
            elif ph == 1:
                p.dma(p.pool, ds_b, bias[:, 0:12, :], bC_d[:, :, :], writes=[b_bias])
            else:
                p.dma(p.pool, ds_b, bias[:], bDg_d[:, :, :], writes=[b_bias])
                btv = Vt[:].rearrange("p a b -> p (a b)")[:, 0:96 * 128].rearrange("p (a c) -> p a c", a=96)
                p.dma(p.pool, ds_b, btv, bDm_d[:, :, :], writes=[b_vt])
                for c8 in range(4):
                    p.op(p.dve, lambda h, c8=c8, btv=btv: h.scalar_tensor_tensor(
                        out=bias[:, 24 * c8:24 * c8 + 24, :], in0=bias[:, 24 * c8:24 * c8 + 24, :], scalar=8.0,
                        in1=btv[:, 24 * c8:24 * c8 + 24, :], op0=ALU.mult, op1=ALU.add),
                        reads=[b_bias, b_vt], writes=[b_bias])
            p.op(p.dve, lambda h: h.memset(Vt[:], 1.0), writes=[b_vt])
            nv = len(vts)
            lv = None
            for g0 in range(0, nv, 64):
                n = min(64, nv - g0)
                lv = p.dma(p.sp, ds_v, Vt[:, g0:g0 + n, :], vt_d[:, vbase[ph] + g0:vbase[ph] + g0 + n, :], writes=[b_vt])
            b_vt.w = lv
            nq = len(qbs)

            def s1(i):
                qb = qbs[i]
                bi_ = i % 2
                qo, qp = qb["q"]
                for kt, (ko, kp, vi, bidx) in enumerate(qb["kts"]):
                    p.op(p.pe, lambda h, bi_=bi_, kt=kt, ko=ko, kp=kp, qo=qo, qp=qp: h.matmul(
                        Sps[bi_][:, kt, :], lhsT=fap(kT[:], ko, kp), rhs=fap(qT[:], qo, qp), start=True, stop=False),
                        reads=[b_k, b_q], writes=[b_sps[bi_]])
                    p.op(p.pe, lambda h, bi_=bi_, kt=kt, bidx=bidx: h.matmul(
                        Sps[bi_][:, kt, :], lhsT=idb[:], rhs=bias[:, bidx, :], start=False, stop=True),
                        reads=[b_bias, b_c], writes=[b_sps[bi_]])
                nk = len(qb["kts"])
                for k0 in range(0, nk, 4):
                    k1 = min(nk, k0 + 4)
                    p.op(p.act, lambda h, bi_=bi_, k0=k0, k1=k1: h.activation(
                        out=Pt[bi_][:, k0:k1, :], in_=Sps[bi_][:, k0:k1, :], func=AF.Exp, scale=0.125),
                        reads=[b_sps[bi_]], writes=[b_pt[bi_]])

            def s2(i):
                qb = qbs[i]
                bi_ = i % 2
                nk = len(qb["kts"])
                for kt, (ko, kp, vi, bidx) in enumerate(qb["kts"]):
                    p.op(p.pe, lambda h, bi_=bi_, kt=kt, vi=vi, nk=nk: h.matmul(
                        Ops[bi_], lhsT=Pt[bi_][:, kt, :], rhs=Vt[:, vi, :], start=(kt == 0), stop=(kt == nk - 1)),
                        reads=[b_pt[bi_], b_vt], writes=[b_ops[bi_]])
                p.op(p.dve, lambda h, bi_=bi_: h.tensor_copy(out=Osb[bi_][:, 0, :], in_=Ops[bi_][:, 0:64]),
                     reads=[b_ops[bi_]], writes=[b_osb[bi_]])
                p.op(p.dve, lambda h, bi_=bi_: h.tensor_scalar(out=Osb[bi_][:, 1, :], in0=idf[:, 0:64], scalar1=0.0,
                                                               scalar2=Ops[bi_][:, 64:65], op0=ALU.mult, op1=ALU.add),
                     reads=[b_ops[bi_], b_c], writes=[b_osb[bi_]], nosync=True)

            def s3(i):
                qb = qbs[i]
                bi_ = i % 2
                for hh in range(2):
                    p.op(p.pe, lambda h, bi_=bi_, hh=hh: h.transpose(OTps[bi_][:, hh, :], Osb[bi_][:, hh, :], idf[:]),
                         reads=[b_osb[bi_], b_c], writes=[b_otps[bi_]])
                do, dp = qb["dst"]
                for hh in range(2):
                    dst = fap(Oacc[:, hh, :], do, dp)
                    src = OTps[bi_][:, hh, :]
                    if len(dp) == 2:
                        src = src.rearrange("p (a b) -> p a b", a=dp[0][1])
                    if qb["add"]:
                        p.op(p.dve, lambda h, dst=dst, src=src: h.tensor_tensor(out=dst, in0=src, in1=dst, op=ALU.add),
                             reads=[b_otps[bi_], b_oacc], writes=[b_oacc], nosync=True)
                    else:
                        p.op(p.act, lambda h, dst=dst, src=src: h.activation(out=dst, in_=src, func=AF.Copy),
                             reads=[b_otps[bi_]], writes=[b_oacc], nosync=True)
                last = (i == nq - 1) or (qbs[i + 1]["sb"] != qb["sb"])
                if last:
                    finalize(qb["sb"])

            def finalize(sb):
                if ph == 0:
                    p.op(p.dve, lambda h: h.tensor_scalar(out=Oacc[:, 1, :], in0=Oacc[:, 1, :], scalar1=es[0:64, 0:1],
                                                          scalar2=None, op0=ALU.add),
                         reads=[b_oacc, b_c], writes=[b_oacc])
                p.op(p.dve, lambda h: h.reciprocal(out=Oacc[:, 1, :], in_=Oacc[:, 1, :]), reads=[b_oacc], writes=[b_oacc])
                p.op(p.dve, lambda h: h.tensor_tensor(out=ost[:], in0=Oacc[:, 0, :], in1=Oacc[:, 1, :], op=ALU.mult),
                     reads=[b_oacc], writes=[b_ost])
                p.dma(p.pool, ds_o, o_d[ph, :, sb * SBLK:(sb + 1) * SBLK], ost[:], reads=[b_ost])

            for step in range(nq + 2):
                if step < nq:
                    s1(step)
                if 0 <= step - 1 < nq:
                    s2(step - 1)
                if 0 <= step - 2 < nq:
                    s3(step - 2)
        p.finish(p.pool, [(ds_o.sem, ds_o.cnt)])
        with nc.Block() as block:
            p.replay(block)
    return nc


def attn_host_inputs(P, hd, acons_hd, rel_bias_hd, sink_val, idb, idf, vidx):
    bA, bC, (ri, ci), bDm = acons_hd
    bDg = rel_bias_hd[ri, ci].astype(NPBF)
    qk_rows = [0 + hd * 64, 256 + (hd // 2) * 64, 768 + hd * 64, 1024 + hd * 64, 1536 + hd * 64, 1792 + hd * 64]
    v_rows = [384 + (hd // 2) * 64, 1280 + hd * 64, 2048 + hd * 64]
    qk = np.stack([P[r0:r0 + 64] for r0 in qk_rows])
    qk[4] = qk[4].reshape(64, 32, 8, 4, 16).transpose(0, 1, 3, 2, 4).reshape(64, S)
    vts = []
    for ph in range(3):
        v = P[v_rows[ph]:v_rows[ph] + 64]
        vpad = np.concatenate([v, np.zeros((64, 1), v.dtype)], axis=1)
        g = vpad[:, vidx[ph]].transpose(2, 1, 0)
        g1 = np.ones((128, g.shape[1], 65), g.dtype)
        g1[:, :, 0:64] = g
        vts.append(g1)
    return {"qk": np.ascontiguousarray(qk), "vt": np.ascontiguousarray(np.concatenate(vts, axis=1)),
            "bA": np.ascontiguousarray(bA.transpose(1, 0, 2)), "bC": np.ascontiguousarray(bC.transpose(1, 0, 2)),
            "bDg": np.ascontiguousarray(bDg.transpose(1, 0, 2)), "bDm": np.ascontiguousarray(bDm.transpose(1, 0, 2)),
            "sink": np.full((128, 1), sink_val, np.float32), "idb": idb, "idf": idf}


ST = 512
NST = S // ST
C1_2PI = 6.28125
C2_2PI = 2.0 * np.pi - 6.28125


def build_s5():
    from contextlib import ExitStack
    nc = bass.Bass("TRN2", target_bir_lowering=False)
    u_d = dram_in(nc, "u", [2, 64, S], BF16)
    prm_d = dram_in(nc, "prm", [128, 8, 4], F32)
    bm_d = dram_in(nc, "bm", [128, 8, 2, 16], F32)
    cm_d = dram_in(nc, "cm", [128, 8, 2, 16], F32)
    iota_d = dram_in(nc, "iota", [128, ST + 1], F32)
    sgn_d = dram_in(nc, "sgn", [128, 2], F32)
    idf_d = dram_in(nc, "idf", [128, 128], F32)
    swp_d = dram_in(nc, "swp", [128, 128], F32)
    y_d = dram_out(nc, "y_out", [2, 64, S], F32)
    with ExitStack() as st:
        env = Env(nc, st)
        p = env.p
        u_sb = env.sb("u_s", [64, 2, S], BF16)
        prm = env.sb("prm_s", [128, 8, 4], F32)
        bm = env.sb("bm_s", [128, 8, 2, 16], F32)
        cm = env.sb("cm_s", [128, 8, 2, 16], F32)
        iota = env.sb("iota_s", [128, ST + 1], F32)
        sgn = env.sb("sgn_s", [128, 2], F32)
        idf = env.sb("idf_s", [128, 128], F32)
        swp = env.sb("swp_s", [128, 128], F32)
        COS = env.sb("COS", [128, 8, ST + 1], F32)
        SIN = env.sb("SIN", [128, 8, ST + 1], F32)
        Rk = env.sb("Rk", [128, 8, ST], F32)
        Rot = env.sb("Rot", [128, 8, 128], F32)
        Bl = env.sb("Bl", [64, 8, 2, 128], BF16)
        Cl = env.sb("Cl", [128, 8, 2, 64], BF16)
        sc = env.sb("sc", [128, 8, 16], F32)
        ang = env.sb("ang", [128, ST + 1], F32)
        kq = env.sb("kq", [128, ST + 1], I32)
        kf = env.sb("kf", [128, ST + 1], F32)
        bpad = env.sb("bpad", [128, 2, 64], F32)
        tmp16 = env.sb("tmp16", [128, 4, 16], F32)
        init = env.sb("init", [128, 8], F32)
        t1 = [env.sb(f"t1_{i}", [128, ST], F32) for i in range(2)]
        t2 = [env.sb(f"t2_{i}", [128, ST], F32) for i in range(2)]
        v_sb = [env.sb(f"v_{i}", [128, ST], F32) for i in range(2)]
        w_sb = [env.sb(f"w_{i}", [128, ST], F32) for i in range(2)]
        Wc = [env.sb(f"Wc_{i}", [128, ST], BF16) for i in range(2)]
        Ws = [env.sb(f"Ws_{i}", [128, ST], BF16) for i in range(2)]
        yst = [env.sb(f"yst_{i}", [64, ST], F32) for i in range(2)]
        bu_ps = [env.ps(f"bu{i}", [128, ST]) for i in range(2)]
        bs_ps = [env.ps(f"bs{i}", [128, ST]) for i in range(2)]
        y_ps = [env.ps(f"yps{i}", [64, ST]) for i in range(2)]
        i_ps = env.ps("ips", [128, 8])
        s_ps = env.ps("sps", [64, 128])
        b_c, b_u, b_tab, b_set = Buf(), Buf(), Buf(), Buf()
        b_ang, b_kq, b_kf, b_bpad, b_sps, b_tmp = Buf(), Buf(), Buf(), Buf(), Buf(), Buf()
        b_init = [Buf() for _ in range(8)]
        b_ips = [Buf() for _ in range(8)]
        b_t1 = [Buf(), Buf()]
        b_t2 = [Buf(), Buf()]
        b_v = [Buf(), Buf()]
        b_w = [Buf(), Buf()]
        b_wc = [Buf(), Buf()]
        b_ws = [Buf(), Buf()]
        b_yst = [Buf(), Buf()]
        b_bu = [Buf(), Buf()]
        b_bs = [Buf(), Buf()]
        b_yps = [Buf(), Buf()]
        ds_c = env.dsem("dc")
        ds_u = env.dsem("du")
        ds_o = env.dsem("do")

        for dst, src in ((prm, prm_d), (bm, bm_d), (cm, cm_d)):
            p.dma(p.pool, ds_c, dst[:], src[:, :, :] if len(src.shape) == 3 else src[:, :, :, :], writes=[b_c])
        for dst, src in ((iota, iota_d), (sgn, sgn_d), (idf, idf_d), (swp, swp_d)):
            lt = p.dma(p.pool, ds_c, dst[:], src[:, :], writes=[b_c])
        lu = p.dma(p.sp, ds_u, u_sb[:, 0, :], u_d[0, :, :], writes=[b_u])
        lu = p.dma(p.sp, ds_u, u_sb[:, 1, :], u_d[1, :, :], writes=[b_u])
        b_u.w = lu
        p.op(p.dve, lambda h: h.memset(init[:], 0.0), writes=b_init)
        p.op(p.dve, lambda h: h.memset(Bl[:], 0.0), writes=[b_set])

        def col(k, i):
            return sc[:, k, i:i + 1]

        def dv(fn, reads, writes):
            return p.op(p.dve, fn, reads=reads, writes=writes)

        def C_(i):
            return sc[:, :, i]
        LR, LI, LS = prm[:, :, 0], prm[:, :, 1], prm[:, :, 2]
        p.op(p.act, lambda h: h.activation(out=C_(0), in_=LS, func=AF.Exp), reads=[b_c], writes=[b_set])
        dv(lambda h: h.tensor_tensor(out=C_(1), in0=LI, in1=C_(0), op=ALU.mult), [b_set, b_c], [b_set])
        dv(lambda h: h.tensor_tensor(out=C_(2), in0=LR, in1=C_(0), op=ALU.mult), [b_set, b_c], [b_set])
        p.op(p.act, lambda h: h.activation(out=C_(2), in_=C_(2), func=AF.Exp), reads=[b_set], writes=[b_set])
        for k in range(8):
            for which in range(2):
                tab = SIN if which == 0 else COS
                shift = 0.0 if which == 0 else float(np.pi / 2)
                dv(lambda h, k=k, shift=shift: h.tensor_scalar(out=ang[:], in0=iota[:], scalar1=col(k, 1), scalar2=shift,
                                                               op0=ALU.mult, op1=ALU.add), [b_set, b_c], [b_ang])
                dv(lambda h: h.tensor_scalar(out=kq[:], in0=ang[:], scalar1=float(1.0 / (2 * np.pi)), scalar2=None,
                                             op0=ALU.mult), [b_ang], [b_kq])
                dv(lambda h: h.tensor_copy(out=kf[:], in_=kq[:]), [b_kq], [b_kf])
                dv(lambda h: h.scalar_tensor_tensor(out=ang[:], in0=kf[:], scalar=-C1_2PI, in1=ang[:],
                                                    op0=ALU.mult, op1=ALU.add), [b_kf, b_ang], [b_ang])
                dv(lambda h: h.scalar_tensor_tensor(out=ang[:], in0=kf[:], scalar=-C2_2PI, in1=ang[:],
                                                    op0=ALU.mult, op1=ALU.add), [b_kf, b_ang], [b_ang])
                p.op(p.act, lambda h, k=k, tab=tab: h.activation(out=tab[:, k, :], in_=ang[:], func=AF.Sin),
                     reads=[b_ang], writes=[b_tab])
        c1v, s1v = COS[:, :, 1], SIN[:, :, 1]
        s5v = SIN[:, :, ST]
        sg1 = sgn[:, 0:1].to_broadcast([128, 8])
        sg2 = sgn[:, 1:2].to_broadcast([128, 8])
        dv(lambda h: h.tensor_tensor(out=C_(3), in0=C_(2), in1=c1v, op=ALU.mult), [b_set, b_tab], [b_set])
        dv(lambda h: h.tensor_tensor(out=C_(4), in0=C_(2), in1=s1v, op=ALU.mult), [b_set, b_tab], [b_set])
        dv(lambda h: h.tensor_tensor(out=C_(5), in0=LR, in1=LR, op=ALU.mult), [b_c, b_set], [b_set])
        dv(lambda h: h.tensor_tensor(out=C_(9), in0=LI, in1=LI, op=ALU.mult), [b_c, b_set], [b_set])
        dv(lambda h: h.tensor_tensor(out=C_(5), in0=C_(5), in1=C_(9), op=ALU.add), [b_set], [b_set])
        dv(lambda h: h.reciprocal(out=C_(13), in_=C_(5)), [b_set], [b_set])
        dv(lambda h: h.tensor_scalar(out=C_(6), in0=C_(3), scalar1=-1.0, scalar2=None, op0=ALU.add), [b_set], [b_set])
        dv(lambda h: h.tensor_tensor(out=C_(9), in0=C_(4), in1=LI, op=ALU.mult), [b_set, b_c], [b_set])
        dv(lambda h: h.tensor_tensor(out=C_(7), in0=C_(6), in1=LR, op=ALU.mult), [b_set, b_c], [b_set])
        dv(lambda h: h.tensor_tensor(out=C_(7), in0=C_(7), in1=C_(9), op=ALU.add), [b_set], [b_set])
        dv(lambda h: h.tensor_tensor(out=C_(7), in0=C_(7), in1=C_(13), op=ALU.mult), [b_set], [b_set])
        dv(lambda h: h.tensor_tensor(out=C_(9), in0=C_(6), in1=LI, op=ALU.mult), [b_set, b_c], [b_set])
        dv(lambda h: h.tensor_tensor(out=C_(8), in0=C_(4), in1=LR, op=ALU.mult), [b_set, b_c], [b_set])
        dv(lambda h: h.tensor_tensor(out=C_(8), in0=C_(8), in1=C_(9), op=ALU.subtract), [b_set], [b_set])
        dv(lambda h: h.tensor_tensor(out=C_(8), in0=C_(8), in1=C_(13), op=ALU.mult), [b_set], [b_set])
        dv(lambda h: h.tensor_tensor(out=C_(10), in0=C_(8), in1=sg1, op=ALU.mult), [b_set, b_c], [b_set])
        dv(lambda h: h.tensor_tensor(out=C_(11), in0=C_(7), in1=sg2, op=ALU.mult), [b_set, b_c], [b_set])
        dv(lambda h: h.tensor_tensor(out=C_(12), in0=s5v, in1=sg2, op=ALU.mult), [b_tab, b_c], [b_set])
        for k in range(8):
            g = k % 4
            c5 = COS[:, k, ST:ST + 1]
            dv(lambda h: h.memset(bpad[:], 0.0), [], [b_bpad])
            bA, bB = bm[:, k, 0, :], bm[:, k, 1, :]
            dv(lambda h, k=k, bB=bB: h.tensor_scalar(out=tmp16[:, 0, :], in0=bB, scalar1=col(k, 10), scalar2=None, op0=ALU.mult),
               [b_set, b_c], [b_tmp])
            dv(lambda h, k=k, bA=bA, g=g: h.scalar_tensor_tensor(out=bpad[:, 0, 16 * g:16 * g + 16], in0=bA, scalar=col(k, 7),
                                                                 in1=tmp16[:, 0, :], op0=ALU.mult, op1=ALU.add),
               [b_set, b_c, b_tmp], [b_bpad])
            dv(lambda h, k=k, bB=bB: h.tensor_scalar(out=tmp16[:, 1, :], in0=bB, scalar1=col(k, 11), scalar2=None, op0=ALU.mult),
               [b_set, b_c], [b_tmp])
            dv(lambda h, k=k, bA=bA, g=g: h.scalar_tensor_tensor(out=bpad[:, 1, 16 * g:16 * g + 16], in0=bA, scalar=col(k, 8),
                                                                 in1=tmp16[:, 1, :], op0=ALU.mult, op1=ALU.add),
               [b_set, b_c, b_tmp], [b_bpad])
            for which in range(2):
                p.op(p.pe, lambda h, which=which: h.matmul(s_ps[:], lhsT=bpad[:, which, :], rhs=idf[:], start=True, stop=True),
                     reads=[b_bpad, b_c], writes=[b_sps])
                p.op(p.act, lambda h, k=k, which=which: h.activation(out=Bl[:, k, which, :], in_=s_ps[:], func=AF.Copy),
                     reads=[b_sps], writes=[b_set])
            dv(lambda h, k=k: h.memset(Cl[:, k, :, :], 0.0), [], [b_set])
            cA, cB = cm[:, k, 0, :], cm[:, k, 1, :]
            dv(lambda h, k=k, cA=cA, g=g: h.tensor_scalar(out=Cl[:, k, 0, 16 * g:16 * g + 16], in0=cA, scalar1=sgn[:, 1:2],
                                                          scalar2=None, op0=ALU.mult), [b_c], [b_set])
            dv(lambda h, k=k, cB=cB, g=g: h.tensor_scalar(out=Cl[:, k, 1, 16 * g:16 * g + 16], in0=cB, scalar1=-1.0,
                                                          scalar2=None, op0=ALU.mult), [b_c], [b_set])
            dv(lambda h, k=k: h.tensor_scalar(out=Rk[:, k, :], in0=iota[:, 0:ST], scalar1=0.0, scalar2=col(k, 2),
                                              op0=ALU.mult, op1=ALU.add), [b_c, b_set], [b_set])
            dv(lambda h, k=k: h.tensor_scalar(out=Rot[:, k, :], in0=swp[:], scalar1=col(k, 12), scalar2=None, op0=ALU.mult),
               [b_c, b_set], [b_set])
            dv(lambda h, k=k, c5=c5: h.scalar_tensor_tensor(out=Rot[:, k, :], in0=idf[:], scalar=c5, in1=Rot[:, k, :],
                                                            op0=ALU.mult, op1=ALU.add), [b_c, b_tab, b_set], [b_set])

        steps = [(t, d, g) for t in range(NST) for d in range(2) for g in range(4)]

        def dvs(fn, reads, writes):
            return p.op(p.dve, fn, reads=reads, writes=writes, nosync=True)

        def stA(i):
            t, d, g = steps[i]
            k = d * 4 + g
            bi = i % 2
            usl = u_sb[:, d, t * ST:(t + 1) * ST]
            p.op(p.pe, lambda h, k=k, bi=bi, usl=usl: h.matmul(bu_ps[bi][:], lhsT=Bl[:, k, 0, :], rhs=usl, start=True, stop=True),
                 reads=[b_set, b_u], writes=[b_bu[bi]])
            p.op(p.pe, lambda h, k=k, bi=bi, usl=usl: h.matmul(bs_ps[bi][:], lhsT=Bl[:, k, 1, :], rhs=usl, start=True, stop=True),
                 reads=[b_set, b_u], writes=[b_bs[bi]])
            dvs(lambda h, k=k, bi=bi: h.tensor_tensor(out=t1[bi][:], in0=bu_ps[bi][:], in1=COS[:, k, 0:ST], op=ALU.mult),
               [b_bu[bi], b_tab], [b_t1[bi]])
            dvs(lambda h, k=k, bi=bi: h.tensor_tensor(out=t2[bi][:], in0=bs_ps[bi][:], in1=SIN[:, k, 0:ST], op=ALU.mult),
               [b_bs[bi], b_tab], [b_t2[bi]])
            dvs(lambda h, bi=bi: h.tensor_tensor(out=v_sb[bi][:], in0=t1[bi][:], in1=t2[bi][:], op=ALU.add),
               [b_t1[bi], b_t2[bi]], [b_v[bi]])

        def stB(i):
            t, d, g = steps[i]
            k = d * 4 + g
            bi = i % 2
            dvs(lambda h, k=k, bi=bi: h.tensor_tensor_scan(out=w_sb[bi][:], data0=Rk[:, k, :], data1=v_sb[bi][:],
                                                          initial=init[:, k:k + 1], op0=ALU.mult, op1=ALU.add),
               [b_v[bi], b_set, b_init[k]], [b_w[bi]])
            p.op(p.pe, lambda h, k=k, bi=bi: h.matmul(i_ps[:, k:k + 1], lhsT=Rot[:, k, :], rhs=w_sb[bi][:, ST - 1:ST],
                                                      start=True, stop=True),
                 reads=[b_w[bi], b_set], writes=[b_ips[k]])
            p.op(p.act, lambda h, k=k: h.activation(out=init[:, k:k + 1], in_=i_ps[:, k:k + 1], func=AF.Copy),
                 reads=[b_ips[k]], writes=[b_init[k]])

        def stC(i):
            t, d, g = steps[i]
            k = d * 4 + g
            bi = i % 2
            p.op(p.pool, lambda h, k=k, bi=bi: h.tensor_tensor(out=Wc[bi][:], in0=w_sb[bi][:], in1=COS[:, k, 0:ST], op=ALU.mult),
                 reads=[b_w[bi], b_tab], writes=[b_wc[bi]])
            dvs(lambda h, k=k, bi=bi: h.tensor_tensor(out=Ws[bi][:], in0=w_sb[bi][:], in1=SIN[:, k, 0:ST], op=ALU.mult),
               [b_w[bi], b_tab], [b_ws[bi]])
            p.op(p.pe, lambda h, k=k, bi=bi, d=d, g=g: h.matmul(y_ps[d][:], lhsT=Cl[:, k, 0, :], rhs=Wc[bi][:],
                                                                start=(g == 0), stop=False),
                 reads=[b_wc[bi], b_set], writes=[b_yps[d]])
            p.op(p.pe, lambda h, k=k, bi=bi, d=d, g=g: h.matmul(y_ps[d][:], lhsT=Cl[:, k, 1, :], rhs=Ws[bi][:],
                                                                start=False, stop=(g == 3)),
                 reads=[b_ws[bi], b_set], writes=[b_yps[d]])
            if g == 3:
                p.op(p.act, lambda h, d=d: h.activation(out=yst[d][:], in_=y_ps[d][:], func=AF.Copy),
                     reads=[b_yps[d]], writes=[b_yst[d]])
                p.dma(p.sp, ds_o, y_d[d, :, t * ST:(t + 1) * ST], yst[d][:], reads=[b_yst[d]])

        n = len(steps)
        for i in range(n + 1):
            if i < n:
                stA(i)
                stB(i)
            if i >= 1:
                stC(i - 1)
        p.finish(p.sp, [(ds_o.sem, ds_o.cnt)])
        with nc.Block() as block:
            p.replay(block)
    return nc


_CACHE = {}


def _run(nc, maps):
    res = run_bass_kernel_spmd(nc, maps, core_ids=list(range(NCORES)))
    return res.results


def s5_host_params(inp, l, q):
    prm = np.zeros((128, 8, 4), np.float32)
    bm = np.zeros((128, 8, 2, 16), np.float32)
    cm = np.zeros((128, 8, 2, 16), np.float32)
    for d in range(2):
        for gl in range(4):
            k = d * 4 + gl
            g = 4 * q + gl
            prm[:, k, 0] = np.tile(inp["s5_lam_re"][l, d, g], 2)
            prm[:, k, 1] = np.tile(inp["s5_lam_im"][l, d, g], 2)
            prm[:, k, 2] = inp["s5_log_step"][l, d, g]
            bre, bim = inp["s5_b_re"][l, d, g], inp["s5_b_im"][l, d, g]
            bm[:, k, 0] = np.concatenate([bre, bim], 0)
            bm[:, k, 1] = np.concatenate([bim, bre], 0)
            cre, cim = inp["s5_c_re"][l, d, g].T, inp["s5_c_im"][l, d, g].T
            cm[:, k, 0] = np.concatenate([cre, cim], 0)
            cm[:, k, 1] = np.concatenate([cim, cre], 0)
    return prm, bm, cm


def s5_consts():
    iota = np.broadcast_to(np.arange(ST + 1, dtype=np.float32), (128, ST + 1)).copy()
    sgn = np.zeros((128, 2), np.float32)
    sgn[:64, 0], sgn[64:, 0] = -1.0, 1.0
    sgn[:64, 1], sgn[64:, 1] = 1.0, -1.0
    idf = np.eye(128, dtype=np.float32)
    swp = np.zeros((128, 128), np.float32)
    for p_ in range(128):
        swp[p_, (p_ + 64) % 128] = 1.0
    return iota, sgn, idf, swp


def kernel(**inputs):
    inp = {k: np.asarray(v, dtype=np.float32) for k, v in inputs.items()}
    streams = [np.stack(stream_for_launch(j, inp)) for j in range(DEPTH + 1)]
    sizes = [s.shape[0] for s in streams]
    allb = np.concatenate(streams)
    per = allb.shape[0] // NCORES
    assert per * NCORES == allb.shape[0]
    ncw = build_w(per)
    res = _run(ncw, [{"wf": allb[c * per:(c + 1) * per].reshape(per * 256, 2048)} for c in range(NCORES)])
    wb = np.concatenate([np.asarray(r["wb"]).reshape(per, 128, WSLOT) for r in res])
    del allb, streams
    wbs = np.split(wb, np.cumsum(sizes)[:-1])
    cst = np.zeros((128, 256), np.float32)
    cst[:, :128] = 1.0 / 1024
    for hh in range(2):
        cst[hh * 64:(hh + 1) * 64, 128 + hh * 64:128 + (hh + 1) * 64] = 1.0 / 64
    cst = cst.astype(NPBF)
    idb = np.eye(128, dtype=np.float32).astype(NPBF)
    iota, sgn, idf, swp = s5_consts()
    acons = [attn_bias_consts(hd) for hd in range(4)]
    vidx = [vt_indices(ph) for ph in range(3)]

    x2 = inp["x"].reshape(B * S, D)
    xT = [np.ascontiguousarray(x2[c * TPC:(c + 1) * TPC].T) for c in range(NCORES)]
    tail_in = None
    nct = {}
    nca = None
    ncs = None
    for j in range(DEPTH + 1):
        key = (j >= 1, j <= DEPTH - 1)
        if key not in nct:
            nct[key] = build_t(*key)
        cols = cols_for_launch(j, inp)
        maps = []
        for c in range(NCORES):
            m = {"xin": xT[c], "wst": wbs[j], "cols": cols, "cst": cst}
            if j >= 1:
                m.update(tail_in[c])
            maps.append(m)
        res = _run(nct[key], maps)
        xT = [np.asarray(r["xout"]) for r in res]
        if j == DEPTH:
            break
        l = j
        cpb = NCORES // B
        pT = [np.concatenate([np.asarray(res[b * cpb + i]["pout"]) for i in range(cpb)], axis=1) for b in range(B)]
        if nca is None:
            nca = build_attn()
        maps = []
        for c in range(NCORES):
            b, hd = c // 4, c % 4
            P = pT[b]
            maps.append(attn_host_inputs(P, hd, acons[hd], inp["na_rel_bias"][l][hd], inp["a_sink"][l][hd], idb, idf, vidx))
        ares = _run(nca, maps)
        if ncs is None:
            ncs = build_s5()
        maps = []
        for c in range(NCORES):
            b, q = c // 4, c % 4
            uu = pT[b][512 + 64 * q:512 + 64 * q + 64]
            prm, bm, cm = s5_host_params(inp, l, q)
            maps.append({"u": np.ascontiguousarray(np.stack([uu, uu[:, ::-1]])), "prm": prm, "bm": bm, "cm": cm,
                         "iota": iota, "sgn": sgn, "idf": idf, "swp": swp})
        sres = _run(ncs, maps)
        tail_in = []
        oT, yf, yb = [], [], []
        for b in range(B):
            o = np.zeros((768, S), NPBF)
            f = np.zeros((256, S), np.float32)
            r_ = np.zeros((256, S), np.float32)
            for hd in range(4):
                oo = np.asarray(ares[b * 4 + hd]["o_out"])
                for ph in range(3):
                    o[ph * 256 + hd * 64:ph * 256 + hd * 64 + 64] = oo[ph]
                yy = np.asarray(sres[b * 4 + hd]["y_out"])
                f[hd * 64:hd * 64 + 64] = yy[0]
                r_[hd * 64:hd * 64 + 64] = yy[1][:, ::-1]
            oT.append(o)
            yf.append(f)
            yb.append(r_)
        for c in range(NCORES):
            b, ch = c // cpb, c % cpb
            sl = slice(ch * TPC, (ch + 1) * TPC)
            tail_in.append({"o_in": np.ascontiguousarray(oT[b][:, sl]), "yf_in": np.ascontiguousarray(yf[b][:, sl]),
                            "yb_in": np.ascontiguousarray(yb[b][:, sl]),
                            "u_in": np.ascontiguousarray(pT[b][512:768, sl])})
    out = np.concatenate([x.T for x in xT], axis=0).reshape(B, S, D).astype(np.float32)
    return out
```

```python
import numpy as np
import ml_dtypes
import concourse.bass as bass
import concourse.mybir as mybir
from concourse.bass_utils import run_bass_kernel_spmd

F32 = mybir.dt.float32
BF16 = mybir.dt.bfloat16
I32 = mybir.dt.int32
AF = mybir.ActivationFunctionType
ALU = mybir.AluOpType
NPBF = ml_dtypes.bfloat16

NCORES = 8
D = 1024
DFF = 2816
NFC = DFF // 128
B = 2
S = 16384
DEPTH = 4
TPC = B * S // NCORES
TT = 512
NT = TPC // TT
WSLOT = 4096
NSLOT = 5
EPS = 1e-6
NEG = -30000.0


class Buf:
    __slots__ = ("w", "r")

    def __init__(self):
        self.w = None
        self.r = {}


class Eng:
    def __init__(self, name, sem, selfsync):
        self.name = name
        self.sem = sem
        self.cnt = 0
        self.ops = []
        self.waited = {}
        self.selfsync = selfsync


class DSem:
    def __init__(self, sem):
        self.sem = sem
        self.cnt = 0


class Prog:
    def __init__(self, nc, sems):
        self.nc = nc
        self.pe = Eng("tensor", sems["tensor"], False)
        self.act = Eng("scalar", sems["scalar"], True)
        self.dve = Eng("vector", sems["vector"], True)
        self.pool = Eng("gpsimd", sems["gpsimd"], True)
        self.sp = Eng("sync", sems["sync"], False)
        self.engs = [self.pe, self.act, self.dve, self.pool, self.sp]

    def _wait(self, eng, tok):
        sem, val = tok
        k = id(sem)
        if eng.waited.get(k, 0) >= val:
            return
        eng.waited[k] = val
        eng.ops.append(lambda h, sm=sem, v=val: h.wait_ge(sm, v))

    def _deps(self, eng, reads, writes, nosync=False):
        skip_self = nosync or not eng.selfsync
        for b in reads:
            if b.w is not None:
                if b.w[0] is eng.sem and skip_self:
                    continue
                self._wait(eng, b.w)
        for b in writes:
            for sem, val in b.r.values():
                if sem is eng.sem and skip_self:
                    continue
                self._wait(eng, (sem, val))
            if b.w is not None:
                if b.w[0] is eng.sem and skip_self:
                    continue
                self._wait(eng, b.w)

    def _mark(self, tok, reads, writes):
        for b in reads:
            k = id(tok[0])
            if k not in b.r or b.r[k][1] < tok[1]:
                b.r[k] = tok
        for b in writes:
            b.w = tok
            b.r = {}

    def op(self, eng, fn, reads=(), writes=(), nosync=False):
        self._deps(eng, reads, writes, nosync)
        eng.cnt += 1
        tok = (eng.sem, eng.cnt)
        eng.ops.append(lambda h, f=fn, sm=eng.sem: f(h).then_inc(sm, 1))
        self._mark(tok, reads, writes)
        return tok

    def dma(self, eng, dsem, out, in_, reads=(), writes=()):
        self._deps(eng, reads, writes)
        dsem.cnt += 16
        tok = (dsem.sem, dsem.cnt)
        eng.ops.append(lambda h, o=out, i=in_, sm=dsem.sem: h.dma_start(out=o, in_=i).then_inc(sm, 16))
        self._mark(tok, reads, writes)
        return tok

    def finish(self, eng, toks):
        for t in toks:
            self._wait(eng, t)

    def replay(self, block):
        for e in self.engs:
            def body(h, e=e):
                for f in e.ops:
                    f(h)
            getattr(block, e.name)(body)


class Env:
    def __init__(self, nc, stack):
        self.nc = nc
        self.stack = stack
        self.nsem = 0
        sems = {n: self.sem("p_" + n) for n in ("tensor", "scalar", "vector", "gpsimd", "sync")}
        self.p = Prog(nc, sems)

    def sem(self, name):
        self.nsem += 1
        return self.stack.enter_context(self.nc.semaphore(name))

    def dsem(self, name):
        return DSem(self.sem(name))

    def sb(self, name, shape, dt):
        return self.stack.enter_context(self.nc.sbuf_tensor(name, list(shape), dt))

    def ps(self, name, shape, dt=F32):
        return self.stack.enter_context(self.nc.psum_tensor(name, list(shape), dt))


def dram_in(nc, name, shape, dt):
    return nc.dram_tensor(name, list(shape), dt, kind="ExternalInput").ap()


def dram_out(nc, name, shape, dt):
    return nc.dram_tensor(name, list(shape), dt, kind="ExternalOutput").ap()


def build_w(nblk):
    from contextlib import ExitStack
    nc = bass.Bass("TRN2", target_bir_lowering=False)
    rows = nblk * 256
    wf = dram_in(nc, "wf", [rows, 2048], F32)
    wb = dram_out(nc, "wb", [rows, 2048], BF16)
    with ExitStack() as st:
        env = Env(nc, st)
        p = env.p
        ds = env.dsem("d")
        toks = []
        for b in range(nblk):
            toks.append(p.dma(p.pool, ds, wb[b * 256:(b + 1) * 256, :], wf[b * 256:(b + 1) * 256, :]))
        p.finish(p.pool, toks[-1:])
        with nc.Block() as block:
            p.replay(block)
    return nc


def _pad_block(a):
    a = a.reshape(128, -1)
    out = np.zeros((128, WSLOT), np.float32)
    out[:, : a.shape[1]] = a
    return out


def ffn_blocks(w_in, w_out):
    blocks = []
    wi = w_in.reshape(8, 128, 2, NFC, 128)
    for blk in range(NFC // 2):
        sub = wi[:, :, :, 2 * blk:2 * blk + 2, :]
        blocks.append(_pad_block(sub.transpose(1, 3, 2, 0, 4)))
    wo = w_out.reshape(NFC, 128, 8, 128)
    for dc in range(8):
        blocks.append(_pad_block(wo[:, :, dc, :].transpose(1, 0, 2)))
    return blocks


def proj_blocks(w, nmc):
    nk = w.shape[0] // 128
    wr = w.reshape(nk, 128, nmc, 128)
    blocks = []
    for b0 in range(0, nmc, 4):
        sub = wr[:, :, b0:b0 + 4, :]
        blocks.append(_pad_block(sub.transpose(1, 2, 0, 3)))
    return blocks


def stream_for_launch(j, inp):
    blocks = []
    if j >= 1:
        l = j - 1
        blocks += proj_blocks(inp["s5_w_glu"][l], 2)
        blocks += proj_blocks(inp["w_out"][l], 8)
        blocks += ffn_blocks(inp["ffn2_w_in"][l], inp["ffn2_w_out"][l])
    if j <= DEPTH - 1:
        l = j
        blocks += ffn_blocks(inp["ffn1_w_in"][l], inp["ffn1_w_out"][l])
        blocks += proj_blocks(inp["w_in"][l], 18)
    return blocks


QK_CHUNK_GAIN = {0: 0, 1: 0, 2: 1, 6: 2, 7: 2, 8: 3, 9: 3, 12: 4, 13: 4, 14: 5, 15: 5}


def cols_for_launch(j, inp):
    c = np.ones((128, 44), np.float32)
    if j >= 1:
        l = j - 1
        c[:, 0:8] = inp["ffn2_norm"][l].reshape(8, 128).T
        c[:, 42:44] = inp["s5_d"][l].reshape(2, 128).T
    if j <= DEPTH - 1:
        l = j
        c[:, 8:16] = inp["ffn1_norm"][l].reshape(8, 128).T
        c[:, 16:24] = inp["mix_norm"][l].reshape(8, 128).T
        for mc, gi in QK_CHUNK_GAIN.items():
            c[:, 24 + mc] = np.tile(inp["qk_gain"][l, gi], 2)
    return c


def build_t(has_tail, has_head):
    from contextlib import ExitStack
    nc = bass.Bass("TRN2", target_bir_lowering=False)
    nblk = (22 if has_tail else 0) + (24 if has_head else 0)
    xin = dram_in(nc, "xin", [D, TPC], F32)
    wst = dram_in(nc, "wst", [nblk, 128, WSLOT], BF16)
    cols_d = dram_in(nc, "cols", [128, 44], F32)
    cst_d = dram_in(nc, "cst", [128, 256], BF16)
    if has_tail:
        o_d = dram_in(nc, "o_in", [768, TPC], BF16)
        yf_d = dram_in(nc, "yf_in", [256, TPC], F32)
        yb_d = dram_in(nc, "yb_in", [256, TPC], F32)
        u_d = dram_in(nc, "u_in", [256, TPC], BF16)
    xout = dram_out(nc, "xout", [D, TPC], F32)
    if has_head:
        pout_d = dram_out(nc, "pout", [2304, TPC], BF16)

    with ExitStack() as st:
        env = Env(nc, st)
        p = env.p
        x_sb = [env.sb(f"x{i}", [128, 8, TT], F32) for i in range(2)]
        h_sb = env.sb("h", [128, 8, TT], BF16)
        sq_sb = env.sb("sq", [128, 8, TT], BF16)
        a_sb = env.sb("a", [128, NFC, TT], BF16)
        rs_sb = [env.sb(f"rs{i}", [128, TT], F32) for i in range(2)]
        s_sb = [env.sb(f"s{i}", [128, TT], F32) for i in range(2)]
        ring = env.sb("ring", [128, NSLOT, WSLOT], BF16)
        cols = env.sb("colsb", [128, 44], F32)
        cst = env.sb("cstb", [128, 256], BF16)
        epsc = env.sb("epsc", [128, 1], F32)
        if has_head:
            po_sb = env.sb("po", [128, 18, TT], BF16)
        if has_tail:
            mx_sb = env.sb("mx", [128, 8, TT], BF16)
            yf_sb = env.sb("yf", [128, 2, TT], F32)
            yb_sb = env.sb("yb", [128, 2, TT], F32)
            u_sb = env.sb("u", [128, 2, TT], BF16)
            g_sb = env.sb("g", [128, 2, TT], F32)
            t_sb = env.sb("tg", [128, 2, TT], F32)
            gb_sb = env.sb("gb", [128, 2, TT], BF16)
        bank = [env.ps(f"bk{i}", [128, TT]) for i in range(8)]
        b_x = [[Buf() for _ in range(8)] for _ in range(2)]
        b_h = [Buf() for _ in range(8)]
        b_sq = [Buf() for _ in range(8)]
        b_a = [Buf() for _ in range(NFC)]
        b_rs = [Buf(), Buf()]
        b_s = [Buf(), Buf()]
        b_ring = [Buf() for _ in range(NSLOT)]
        b_bank = [Buf() for _ in range(8)]
        b_const = Buf()
        b_po = [Buf() for _ in range(18)]
        b_mx = [Buf() for _ in range(8)]
        b_y = Buf()
        b_g = Buf()
        b_tg = Buf()
        b_gb = Buf()
        ds_ring = [env.dsem(f"dr{i}") for i in range(NSLOT)]
        ds_x = [env.dsem(f"dx{i}") for i in range(2)]
        ds_c = env.dsem("dc")
        ds_out = env.dsem("dout")
        ds_pout = env.dsem("dpo")
        ds_tail = env.dsem("dtl")

        ones_mean = cst[:, 0:128]
        blk_ones = cst[:, 128:256]

        p.dma(p.pool, ds_c, cols[:], cols_d[:, :], writes=[b_const])
        p.dma(p.pool, ds_c, cst[:], cst_d[:, :], writes=[b_const])
        p.op(p.dve, lambda h: h.memset(epsc[:], EPS), writes=[b_const])

        wstate = {"n": 0}

        def next_block(tile_idx, bidx):
            n = wstate["n"]
            wstate["n"] = n + 1
            s = n % NSLOT
            p.dma(p.sp, ds_ring[s], ring[:, s, :], wst[bidx, :, :], writes=[b_ring[s]])
            return s

        order = []
        per_tile = list(range(nblk))
        PREF = NSLOT - 1
        issued = {"k": 0}
        total_blocks = NT * nblk

        def ensure_issued(upto):
            while issued["k"] < min(upto, total_blocks):
                k = issued["k"]
                next_block(k // nblk, k % nblk)
                issued["k"] = k + 1

        used = {"k": 0}

        def take_block():
            k = used["k"]
            used["k"] = k + 1
            ensure_issued(k + 1)
            s = k % NSLOT
            return s

        def release_prefetch():
            ensure_issued(used["k"] + PREF)

        def load_x(t, xb):
            for c in range(8):
                pass
            src = xin[:, t * TT:(t + 1) * TT].rearrange("(c p) t -> p c t", p=128)
            p.dma(p.pool, ds_x[xb], x_sb[xb][:], src, writes=b_x[xb])

        def rmsnorm_to_h(xb, gcol0):
            xs = x_sb[xb]
            for c in range(8):
                p.op(p.act, lambda h, c=c: h.activation(out=sq_sb[:, c, :], in_=xs[:, c, :], func=AF.Square),
                     reads=[b_x[xb][c]], writes=[b_sq[c]])
            for c in range(8):
                p.op(p.pe, lambda h, c=c: h.matmul(bank[6][:], lhsT=ones_mean, rhs=sq_sb[:, c, :],
                                                   start=(c == 0), stop=(c == 7)),
                     reads=[b_sq[c], b_const], writes=[b_bank[6]])
            p.op(p.act, lambda h: h.activation(out=rs_sb[0][:], in_=bank[6][:], func=AF.Sqrt, bias=epsc[:, 0:1]),
                 reads=[b_bank[6], b_const], writes=[b_rs[0]])
            p.op(p.dve, lambda h: h.reciprocal(out=rs_sb[1][:], in_=rs_sb[0][:]), reads=[b_rs[0]], writes=[b_rs[1]])
            for c in range(8):
                p.op(p.dve, lambda h, c=c: h.scalar_tensor_tensor(
                    out=h_sb[:, c, :], in0=xs[:, c, :], scalar=cols[:, gcol0 + c:gcol0 + c + 1],
                    in1=rs_sb[1][:], op0=ALU.mult, op1=ALU.mult),
                    reads=[b_x[xb][c], b_rs[1], b_const], writes=[b_h[c]])

        def ffn(xb, gcol0):
            xs = x_sb[xb]
            rmsnorm_to_h(xb, gcol0)
            for blk in range(NFC // 2):
                s = take_block()
                wv = ring[:, s, :].rearrange("p (f g k m) -> p f g k m", f=2, g=2, k=8)
                for fcl in range(2):
                    fc = 2 * blk + fcl
                    pb = fc % 2
                    G, U = bank[2 * pb], bank[2 * pb + 1]
                    bG, bU = b_bank[2 * pb], b_bank[2 * pb + 1]
                    for kc in range(8):
                        p.op(p.pe, lambda h, wv=wv, kc=kc, fcl=fcl, G=G: h.matmul(
                            G[:], lhsT=wv[:, fcl, 0, kc, :], rhs=h_sb[:, kc, :], start=(kc == 0), stop=(kc == 7)),
                            reads=[b_ring[s], b_h[kc]], writes=[bG])
                    for kc in range(8):
                        p.op(p.pe, lambda h, wv=wv, kc=kc, fcl=fcl, U=U: h.matmul(
                            U[:], lhsT=wv[:, fcl, 1, kc, :], rhs=h_sb[:, kc, :], start=(kc == 0), stop=(kc == 7)),
                            reads=[b_ring[s], b_h[kc]], writes=[bU])
                    p.op(p.act, lambda h, G=G, pb=pb: h.activation(out=s_sb[pb][:], in_=G[:], func=AF.Silu),
                         reads=[bG], writes=[b_s[pb]])
                    p.op(p.dve, lambda h, U=U, pb=pb, fc=fc: h.tensor_tensor(
                        out=a_sb[:, fc, :], in0=U[:], in1=s_sb[pb][:], op=ALU.mult),
                        reads=[bU, b_s[pb]], writes=[b_a[fc]])
                release_prefetch()
            for dc in range(8):
                s = take_block()
                wv = ring[:, s, 0:NFC * 128].rearrange("p (f m) -> p f m", f=NFC)
                O = bank[4 + dc % 2]
                bO = b_bank[4 + dc % 2]
                for fc in range(NFC):
                    p.op(p.pe, lambda h, wv=wv, fc=fc, O=O: h.matmul(
                        O[:], lhsT=wv[:, fc, :], rhs=a_sb[:, fc, :], start=(fc == 0), stop=(fc == NFC - 1)),
                        reads=[b_ring[s], b_a[fc]], writes=[bO])
                p.op(p.dve, lambda h, dc=dc, O=O: h.scalar_tensor_tensor(
                    out=xs[:, dc, :], in0=O[:], scalar=0.5, in1=xs[:, dc, :], op0=ALU.mult, op1=ALU.add),
                    reads=[bO, b_x[xb][dc]], writes=[b_x[xb][dc]])
                release_prefetch()

        def head_proj(xb, t):
            xs = x_sb[xb]
            rmsnorm_to_h(xb, 16)
            for b0 in range(5):
                s = take_block()
                wv = ring[:, s, :].rearrange("p (c k m) -> p c k m", c=4, k=8)
                for mcl in range(4):
                    mc = 4 * b0 + mcl
                    if mc >= 18:
                        break
                    P = bank[mc % 4]
                    bP = b_bank[mc % 4]
                    for kc in range(8):
                        p.op(p.pe, lambda h, wv=wv, kc=kc, mcl=mcl, P=P: h.matmul(
                            P[:], lhsT=wv[:, mcl, kc, :], rhs=h_sb[:, kc, :], start=(kc == 0), stop=(kc == 7)),
                            reads=[b_ring[s], b_h[kc]], writes=[bP])
                    if mc in QK_CHUNK_GAIN:
                        q = mc % 2
                        sqb = s_sb[q]
                        p.op(p.act, lambda h, P=P, q=q: h.activation(out=sq_sb[:, q, :], in_=P[:], func=AF.Square),
                             reads=[bP], writes=[b_sq[q]])
                        p.op(p.pe, lambda h, q=q: h.matmul(bank[7][:], lhsT=blk_ones, rhs=sq_sb[:, q, :],
                                                           start=True, stop=True),
                             reads=[b_sq[q], b_const], writes=[b_bank[7]])
                        p.op(p.act, lambda h: h.activation(out=rs_sb[0][:], in_=bank[7][:], func=AF.Sqrt,
                                                           bias=epsc[:, 0:1]),
                             reads=[b_bank[7], b_const], writes=[b_rs[0]])
                        p.op(p.dve, lambda h: h.reciprocal(out=rs_sb[1][:], in_=rs_sb[0][:]),
                             reads=[b_rs[0]], writes=[b_rs[1]])
                        p.op(p.dve, lambda h, P=P, mc=mc: h.scalar_tensor_tensor(
                            out=po_sb[:, mc, :], in0=P[:], scalar=cols[:, 24 + mc:25 + mc], in1=rs_sb[1][:],
                            op0=ALU.mult, op1=ALU.mult),
                            reads=[bP, b_rs[1], b_const], writes=[b_po[mc]])
                    else:
                        p.op(p.act, lambda h, P=P, mc=mc: h.activation(out=po_sb[:, mc, :], in_=P[:], func=AF.Copy),
                             reads=[bP], writes=[b_po[mc]])
                release_prefetch()
            dst = pout_d[:, t * TT:(t + 1) * TT].rearrange("(c p) t -> p c t", p=128)
            return p.dma(p.pool, ds_pout, dst, po_sb[:], reads=b_po)

        def tail(xb, t):
            xs = x_sb[xb]
            tsl = slice(t * TT, (t + 1) * TT)
            for (r0, c0) in ((0, 0), (256, 4), (512, 6)):
                src = o_d[r0:r0 + 256, tsl].rearrange("(c p) t -> p c t", p=128)
                p.dma(p.pool, ds_tail, mx_sb[:, c0:c0 + 2, :], src, writes=[b_mx[c0], b_mx[c0 + 1]])
            p.dma(p.pool, ds_tail, yf_sb[:], yf_d[:, tsl].rearrange("(c p) t -> p c t", p=128), writes=[b_y])
            p.dma(p.pool, ds_tail, yb_sb[:], yb_d[:, tsl].rearrange("(c p) t -> p c t", p=128), writes=[b_y])
            ltok = p.dma(p.pool, ds_tail, u_sb[:], u_d[:, tsl].rearrange("(c p) t -> p c t", p=128), writes=[b_y])
            for bb in (b_mx[0], b_mx[1], b_mx[4], b_mx[5], b_mx[6], b_mx[7], b_y):
                bb.w = ltok
            p.op(p.dve, lambda h: h.tensor_tensor(out=yf_sb[:], in0=yf_sb[:], in1=yb_sb[:], op=ALU.add),
                 reads=[b_y], writes=[b_y])
            for c in range(2):
                p.op(p.dve, lambda h, c=c: h.scalar_tensor_tensor(
                    out=yf_sb[:, c, :], in0=u_sb[:, c, :], scalar=cols[:, 42 + c:43 + c], in1=yf_sb[:, c, :],
                    op0=ALU.mult, op1=ALU.add), reads=[b_y, b_const], writes=[b_y])
            p.op(p.dve, lambda h: h.tensor_tensor(out=t_sb[:], in0=yf_sb[:], in1=yf_sb[:], op=ALU.mult),
                 reads=[b_y], writes=[b_tg])
            p.op(p.dve, lambda h: h.tensor_scalar(out=t_sb[:], in0=t_sb[:], scalar1=0.044715, scalar2=1.0,
                                                  op0=ALU.mult, op1=ALU.add), reads=[b_tg], writes=[b_tg])
            p.op(p.dve, lambda h: h.tensor_tensor(out=t_sb[:], in0=t_sb[:], in1=yf_sb[:], op=ALU.mult),
                 reads=[b_tg, b_y], writes=[b_tg])
            p.op(p.act, lambda h: h.activation(out=t_sb[:], in_=t_sb[:], func=AF.Sigmoid, scale=1.5957691216057308),
                 reads=[b_tg], writes=[b_tg])
            p.op(p.dve, lambda h: h.tensor_tensor(out=g_sb[:], in0=t_sb[:], in1=yf_sb[:], op=ALU.mult),
                 reads=[b_tg, b_y], writes=[b_g])
            p.op(p.act, lambda h: h.activation(out=gb_sb[:], in_=g_sb[:], func=AF.Copy), reads=[b_g], writes=[b_gb])
            s = take_block()
            wv = ring[:, s, 0:512].rearrange("p (c k m) -> p c k m", c=2, k=2)
            for mc in range(2):
                Z = bank[mc]
                for kc in range(2):
                    p.op(p.pe, lambda h, wv=wv, mc=mc, kc=kc, Z=Z: h.matmul(
                        Z[:], lhsT=wv[:, mc, kc, :], rhs=gb_sb[:, kc, :], start=(kc == 0), stop=(kc == 1)),
                        reads=[b_ring[s], b_gb], writes=[b_bank[mc]])
                p.op(p.act, lambda h, mc=mc, Z=Z: h.activation(out=s_sb[mc][:], in_=Z[:], func=AF.Sigmoid),
                     reads=[b_bank[mc]], writes=[b_s[mc]])
                p.op(p.dve, lambda h, mc=mc: h.tensor_tensor(out=mx_sb[:, 2 + mc, :], in0=g_sb[:, mc, :],
                                                             in1=s_sb[mc][:], op=ALU.mult),
                     reads=[b_g, b_s[mc]], writes=[b_mx[2 + mc]])
            release_prefetch()
            for b0 in range(2):
                s = take_block()
                wv = ring[:, s, :].rearrange("p (c k m) -> p c k m", c=4, k=8)
                for mcl in range(4):
                    dc = 4 * b0 + mcl
                    O = bank[4 + dc % 2]
                    bO = b_bank[4 + dc % 2]
                    for kc in range(8):
                        p.op(p.pe, lambda h, wv=wv, kc=kc, mcl=mcl, O=O: h.matmul(
                            O[:], lhsT=wv[:, mcl, kc, :], rhs=mx_sb[:, kc, :], start=(kc == 0), stop=(kc == 7)),
                            reads=[b_ring[s], b_mx[kc]], writes=[bO])
                    p.op(p.dve, lambda h, dc=dc, O=O: h.tensor_tensor(
                        out=xs[:, dc, :], in0=O[:], in1=xs[:, dc, :], op=ALU.add),
                        reads=[bO, b_x[xb][dc]], writes=[b_x[xb][dc]])
                release_prefetch()

        out_toks = []
        load_x(0, 0)
        ensure_issued(PREF)
        for t in range(NT):
            xb = t % 2
            if t + 1 < NT:
                load_x(t + 1, 1 - xb)
            if has_tail:
                tail(xb, t)
                ffn(xb, 0)
            if has_head:
                ffn(xb, 8)
                out_toks.append(head_proj(xb, t))
            dst = xout[:, t * TT:(t + 1) * TT].rearrange("(c p) t -> p c t", p=128)
            out_toks.append(p.dma(p.pool, ds_out, dst, x_sb[xb][:], reads=b_x[xb]))
        p.finish(p.pool, [(ds_out.sem, ds_out.cnt), (ds_pout.sem, ds_pout.cnt)] if has_head
                 else [(ds_out.sem, ds_out.cnt)])
        with nc.Block() as block:
            p.replay(block)
    return nc


PAD = 1024
SP_ = S + 2 * PAD
SBLK = 2048
NVT = 405
GRID_W = 64


def fap(ap2d, off, pat):
    return bass.AP(ap2d.tensor, ap2d.offset + off, [list(ap2d.ap[0])] + [list(x) for x in pat])


def attn_plan():
    plans = []
    vt = [(PAD + 128 * t, [[1, 128]]) for t in range(128)]
    qb = []
    for b in range(128):
        kts = []
        for dt_, bi in ((-1, 0), (0, 1), (1, 2)):
            t = b + dt_
            if 0 <= t < 128:
                kts.append((PAD + 128 * t, [[1, 128]], t, bi))
        sb = (128 * b) // SBLK
        qb.append(dict(q=(128 * b, [[1, 128]]), kts=kts, dst=(128 * b - sb * SBLK, [[1, 128]]), add=False, sb=sb))
    plans.append((vt, qb))
    vt = []
    vidx = {}
    for ci, d in enumerate((1, 4, 16)):
        L = S // d
        for rho in range(d):
            for bp in range(L // 128 + 1):
                vidx[(ci, rho, bp)] = len(vt)
                vt.append((PAD + d * 64 * (2 * bp - 1) + rho, [[d, 128]]))
    assert len(vt) == NVT
    qb = []
    for sb in range(S // SBLK):
        for ci, d in enumerate((1, 4, 16)):
            L = S // d
            nb = L // 128
            for rho in range(d):
                for b in range(nb):
                    t0 = d * 128 * b + rho
                    if t0 // SBLK != sb:
                        continue
                    kts = []
                    for bp, base in ((b, 0), (b + 1, 1)):
                        bi = 4 * ci + base
                        if base == 0 and b == 0:
                            bi = 4 * ci + 2
                        if base == 1 and b == nb - 1:
                            bi = 4 * ci + 3
                        off, pat = vt[vidx[(ci, rho, bp)]]
                        kts.append((off, pat, vidx[(ci, rho, bp)], bi))
                    qb.append(dict(q=(t0, [[d, 128]]), kts=kts, dst=(t0 - sb * SBLK, [[d, 128]]), add=(ci > 0), sb=sb))
    plans.append((vt, qb))
    vt = [(PAD + 128 * t, [[1, 128]]) for t in range(128)]
    qb = []
    for i in range(32):
        rs = min(max(8 * i - 4, 0), 240)
        icls = 0 if i == 0 else (2 if i == 31 else 1)
        for j in range(4):
            kts = []
            for m in range(8):
                t = rs // 2 + m
                kts.append((PAD + 128 * t, [[1, 128]], t, (icls * 4 + j) * 8 + m))
            t0 = 8 * i * GRID_W + 16 * j
            sb = t0 // SBLK
            qb.append(dict(q=((i * 4 + j) * 128, [[1, 128]]), kts=kts, dst=(t0 - sb * SBLK, [[GRID_W, 8], [1, 16]]),
                           add=False, sb=sb))
    plans.append((vt, qb))
    return plans


def attn_bias_consts(hd):
    slopes = 2.0 ** (-np.arange(1, 9, dtype=np.float64))
    kk = np.arange(128)[:, None]
    qq = np.arange(128)[None, :]
    A = np.zeros((3, 128, 128), np.float32)
    for bi, sh in enumerate((-128, 0, 128)):
        rel = kk + sh - qq
        A[bi] = np.where(np.abs(rel) <= 128, -slopes[hd] * np.abs(rel) * 8.0, NEG * 8)
    C = np.zeros((12, 128, 128), np.float32)
    for ci, d in enumerate((1, 4, 16)):
        for base, sh in ((0, -64), (1, 64)):
            rel = kk + sh - qq
            m = np.where(np.abs(rel) <= 64, -slopes[4 + hd] * d * np.abs(rel) * 8.0, NEG * 8)
            C[4 * ci + base] = m
        first = C[4 * ci + 0].copy()
        first[:64, :] = NEG * 8
        C[4 * ci + 2] = first
        last = C[4 * ci + 1].copy()
        last[64:, :] = NEG * 8
        C[4 * ci + 3] = last
    ri = np.zeros((96, 128, 128), np.int64)
    cidx = np.zeros((96, 128, 128), np.int64)
    Dm = np.zeros((96, 128, 128), np.float32)
    rows = S // GRID_W
    for icls, i in enumerate((0, 5, 31)):
        rs = min(max(8 * i - 4, 0), rows - 16)
        for j in range(4):
            for m in range(8):
                idx = (icls * 4 + j) * 8 + m
                krow = (rs + 2 * m + np.arange(128) // 64)[:, None]
                kcol = (np.arange(128) % 64)[:, None]
                qrow = (8 * i + np.arange(128) // 16)[None, :]
                qcol = (16 * j + np.arange(128) % 16)[None, :]
                wr = np.clip(qrow - 4, 0, rows - 8)
                wc = np.clip(qcol - 8, 0, GRID_W - 16)
                valid = (krow >= wr) & (krow < wr + 8) & (kcol >= wc) & (kcol < wc + 16)
                ri[idx] = np.clip(krow - qrow + 7, 0, 14) + 0 * qcol
                cidx[idx] = np.clip(kcol - qcol + 15, 0, 30) + 0 * qrow
                Dm[idx] = np.where(valid, 0.0, NEG * 8)
    return A.astype(NPBF), C.astype(NPBF), (ri, cidx), Dm.astype(NPBF)


def vt_indices(ph):
    vts, _ = attn_plan()[ph]
    idx = np.zeros((len(vts), 128), np.int64)
    for n, (off, pat) in enumerate(vts):
        st_, cnt = pat[0]
        assert len(pat) == 1 and cnt == 128
        idx[n] = off - PAD + st_ * np.arange(128)
    idx[(idx < 0) | (idx >= S)] = -1
    return idx


def build_attn(phases=(0, 1, 2)):
    from contextlib import ExitStack
    nc = bass.Bass("TRN2", target_bir_lowering=False)
    qk_d = dram_in(nc, "qk", [6, 64, S], BF16)
    vt_d = dram_in(nc, "vt", [128, 128 + NVT + 128, 65], BF16)
    bA_d = dram_in(nc, "bA", [128, 3, 128], BF16)
    bC_d = dram_in(nc, "bC", [128, 12, 128], BF16)
    bDg_d = dram_in(nc, "bDg", [128, 96, 128], BF16)
    bDm_d = dram_in(nc, "bDm", [128, 96, 128], BF16)
    sink_d = dram_in(nc, "sink", [128, 1], F32)
    idb_d = dram_in(nc, "idb", [128, 128], BF16)
    idf_d = dram_in(nc, "idf", [128, 128], F32)
    o_d = dram_out(nc, "o_out", [3, 64, S], BF16)
    plans = attn_plan()
    vbase = [0, 128, 128 + NVT]
    with ExitStack() as st:
        env = Env(nc, st)
        p = env.p
        qT = env.sb("qT", [64, S], BF16)
        kT = env.sb("kT", [64, SP_], BF16)
        Vt = env.sb("Vt", [128, NVT, 65], BF16)
        bias = env.sb("bias", [128, 96, 128], BF16)
        idb = env.sb("idbs", [128, 128], BF16)
        idf = env.sb("idfs", [128, 128], F32)
        es = env.sb("es", [128, 1], F32)
        Oacc = env.sb("Oacc", [64, 2, SBLK], F32)
        ost = env.sb("ost", [64, SBLK], BF16)
        Pt = [env.sb(f"Pt{i}", [128, 8, 128], BF16) for i in range(2)]
        Osb = [env.sb(f"Osb{i}", [128, 2, 64], F32) for i in range(2)]
        Sps = [env.ps(f"Sps{i}", [128, 8, 128]) for i in range(2)]
        Ops_t = [env.ps(f"Ops{i}", [128, 65]) for i in range(2)]
        Ops = [Ops_t[0][:, :], Ops_t[1][:, :]]
        OTps = [env.ps(f"OTps{i}", [64, 2, 128]) for i in range(2)]
        b_q, b_k, b_vt, b_bias, b_c = Buf(), Buf(), Buf(), Buf(), Buf()
        b_oacc, b_ost = Buf(), Buf()
        b_pt = [Buf(), Buf()]
        b_osb = [Buf(), Buf()]
        b_sps = [Buf(), Buf()]
        b_ops = [Buf(), Buf()]
        b_otps = [Buf(), Buf()]
        ds_c = env.dsem("dc")
        ds_q = env.dsem("dq")
        ds_b = env.dsem("db")
        ds_v = env.dsem("dv")
        ds_o = env.dsem("do")

        p.dma(p.pool, ds_c, idb[:], idb_d[:, :], writes=[b_c])
        p.dma(p.pool, ds_c, idf[:], idf_d[:, :], writes=[b_c])
        p.dma(p.pool, ds_c, es[:], sink_d[:, :], writes=[b_c])
        p.op(p.act, lambda h: h.activation(out=es[:], in_=es[:], func=AF.Exp), reads=[b_c], writes=[b_c])
        p.op(p.dve, lambda h: h.memset(kT[:, 0:PAD], 0.0), writes=[b_k])
        p.op(p.dve, lambda h: h.memset(kT[:, PAD + S:SP_], 0.0), writes=[b_k])

        for ph in phases:
            vts, qbs = plans[ph]
            lt = p.dma(p.sp, ds_q, qT[:], qk_d[2 * ph + 0, :, :], writes=[b_q])
            lt = p.dma(p.sp, ds_q, kT[:, PAD:PAD + S], qk_d[2 * ph + 1, :, :], writes=[b_k])
            b_q.w = lt
            b_k.w = lt
            if ph == 0:
                p.dma(p.pool, ds_b, bias[:, 0:3, :], bA_d[:, :, :], writes=[b_bias])
            elif ph == 1:
                p.dma(p.pool, ds_b, bias[:, 0:12, :], bC_d[:, :, :], writes=[b_bias])
            else:
                p.dma(p.pool, ds_b, bias[:], bDg_d[:, :, :], writes=[b_bias])
                btv = Vt[:].rearrange("p a b -> p (a b)")[:, 0:96 * 128].rearrange("p (a c) -> p a c", a=96)
                p.dma(p.pool, ds_b, btv, bDm_d[:, :, :], writes=[b_vt])
                for c8 in range(4):
                    p.op(p.dve, lambda h, c8=c8, btv=btv: h.scalar_tensor_tensor(
                        out=bias[:, 24 * c8:24 * c8 + 24, :], in0=bias[:, 24 * c8:24 * c8 + 24, :], scalar=8.0,
                        in1=btv[:, 24 * c8:24 * c8 + 24, :], op0=ALU.mult, op1=ALU.add),
                        reads=[b_bias, b_vt], writes=[b_bias])
            p.op(p.dve, lambda h: h.memset(Vt[:], 1.0), writes=[b_vt])
            nv = len(vts)
            lv = None
            for g0 in range(0, nv, 64):
                n = min(64, nv - g0)
                lv = p.dma(p.sp, ds_v, Vt[:, g0:g0 + n, :], vt_d[:, vbase[ph] + g0:vbase[ph] + g0 + n, :], writes=[b_vt])
            b_vt.w = lv
            nq = len(qbs)

            def s1(i):
                qb = qbs[i]
                bi_ = i % 2
                qo, qp = qb["q"]
                for kt, (ko, kp, vi, bidx) in enumerate(qb["kts"]):
                    p.op(p.pe, lambda h, bi_=bi_, kt=kt, ko=ko, kp=kp, qo=qo, qp=qp: h.matmul(
                        Sps[bi_][:, kt, :], lhsT=fap(kT[:], ko, kp), rhs=fap(qT[:], qo, qp), start=True, stop=False),
                        reads=[b_k, b_q], writes=[b_sps[bi_]])
                    p.op(p.pe, lambda h, bi_=bi_, kt=kt, bidx=bidx: h.matmul(
                        Sps[bi_][:, kt, :], lhsT=idb[:], rhs=bias[:, bidx, :], start=False, stop=True),
                        reads=[b_bias, b_c], writes=[b_sps[bi_]])
                nk = len(qb["kts"])
                for k0 in range(0, nk, 4):
                    k1 = min(nk, k0 + 4)
                    p.op(p.act, lambda h, bi_=bi_, k0=k0, k1=k1: h.activation(
                        out=Pt[bi_][:, k0:k1, :], in_=Sps[bi_][:, k0:k1, :], func=AF.Exp, scale=0.125),
                        reads=[b_sps[bi_]], writes=[b_pt[bi_]])

            def s2(i):
                qb = qbs[i]
                bi_ = i % 2
                nk = len(qb["kts"])
                for kt, (ko, kp, vi, bidx) in enumerate(qb["kts"]):
                    p.op(p.pe, lambda h, bi_=bi_, kt=kt, vi=vi, nk=nk: h.matmul(
                        Ops[bi_], lhsT=Pt[bi_][:, kt, :], rhs=Vt[:, vi, :], start=(kt == 0), stop=(kt == nk - 1)),
                        reads=[b_pt[bi_], b_vt], writes=[b_ops[bi_]])
                p.op(p.dve, lambda h, bi_=bi_: h.tensor_copy(out=Osb[bi_][:, 0, :], in_=Ops[bi_][:, 0:64]),
                     reads=[b_ops[bi_]], writes=[b_osb[bi_]])
                p.op(p.dve, lambda h, bi_=bi_: h.tensor_scalar(out=Osb[bi_][:, 1, :], in0=idf[:, 0:64], scalar1=0.0,
                                                               scalar2=Ops[bi_][:, 64:65], op0=ALU.mult, op1=ALU.add),
                     reads=[b_ops[bi_], b_c], writes=[b_osb[bi_]], nosync=True)

            def s3(i):
                qb = qbs[i]
                bi_ = i % 2
                for hh in range(2):
                    p.op(p.pe, lambda h, bi_=bi_, hh=hh: h.transpose(OTps[bi_][:, hh, :], Osb[bi_][:, hh, :], idf[:]),
                         reads=[b_osb[bi_], b_c], writes=[b_otps[bi_]])
                do, dp = qb["dst"]
                for hh in range(2):
                    dst = fap(Oacc[:, hh, :], do, dp)
                    src = OTps[bi_][:, hh, :]
                    if len(dp) == 2:
                        src = src.rearrange("p (a b) -> p a b", a=dp[0][1])
                    if qb["add"]:
                        p.op(p.dve, lambda h, dst=dst, src=src: h.tensor_tensor(out=dst, in0=src, in1=dst, op=ALU.add),
                             reads=[b_otps[bi_], b_oacc], writes=[b_oacc], nosync=True)
                    else:
                        p.op(p.act, lambda h, dst=dst, src=src: h.activation(out=dst, in_=src, func=AF.Copy),
                             reads=[b_otps[bi_]], writes=[b_oacc], nosync=True)
                last = (i == nq - 1) or (qbs[i + 1]["sb"] != qb["sb"])
                if last:
                    finalize(qb["sb"])

            def finalize(sb):
                if ph == 0:
                    p.op(p.dve, lambda h: h.tensor_scalar(out=Oacc[:, 1, :], in0=Oacc[:, 1, :], scalar1=es[0:64, 0:1],
                                                          scalar2=None, op0=ALU.add),
                         reads=[b_oacc, b_c], writes=[b_oacc])
                p.op(p.dve, lambda h: h.reciprocal(out=Oacc[:, 1, :], in_=Oacc[:, 1, :]), reads=[b_oacc], writes=[b_oacc])
                p.op(p.dve, lambda h: h.tensor_tensor(out=ost[:], in0=Oacc[:, 0, :], in1=Oacc[:, 1, :], op=ALU.mult),
                     reads=[b_oacc], writes=[b_ost])
                p.dma(p.pool, ds_o, o_d[ph, :, sb * SBLK:(sb + 1) * SBLK], ost[:], reads=[b_ost])

            for step in range(nq + 2):
                if step < nq:
                    s1(step)
                if 0 <= step - 1 < nq:
                    s2(step - 1)
                if 0 <= step - 2 < nq:
                    s3(step - 2)
        p.finish(p.pool, [(ds_o.sem, ds_o.cnt)])
        with nc.Block() as block:
            p.replay(block)
    return nc


def attn_host_inputs(P, hd, acons_hd, rel_bias_hd, sink_val, idb, idf, vidx):
    bA, bC, (ri, ci), bDm = acons_hd
    bDg = rel_bias_hd[ri, ci].astype(NPBF)
    qk_rows = [0 + hd * 64, 256 + (hd // 2) * 64, 768 + hd * 64, 1024 + hd * 64, 1536 + hd * 64, 1792 + hd * 64]
    v_rows = [384 + (hd // 2) * 64, 1280 + hd * 64, 2048 + hd * 64]
    qk = np.stack([P[r0:r0 + 64] for r0 in qk_rows])
    qk[4] = qk[4].reshape(64, 32, 8, 4, 16).transpose(0, 1, 3, 2, 4).reshape(64, S)
    vts = []
    for ph in range(3):
        v = P[v_rows[ph]:v_rows[ph] + 64]
        vpad = np.concatenate([v, np.zeros((64, 1), v.dtype)], axis=1)
        g = vpad[:, vidx[ph]].transpose(2, 1, 0)
        g1 = np.ones((128, g.shape[1], 65), g.dtype)
        g1[:, :, 0:64] = g
        vts.append(g1)
    return {"qk": np.ascontiguousarray(qk), "vt": np.ascontiguousarray(np.concatenate(vts, axis=1)),
            "bA": np.ascontiguousarray(bA.transpose(1, 0, 2)), "bC": np.ascontiguousarray(bC.transpose(1, 0, 2)),
            "bDg": np.ascontiguousarray(bDg.transpose(1, 0, 2)), "bDm": np.ascontiguousarray(bDm.transpose(1, 0, 2)),
            "sink": np.full((128, 1), sink_val, np.float32), "idb": idb, "idf": idf}


ST = 512
NST = S // ST
C1_2PI = 6.28125
C2_2PI = 2.0 * np.pi - 6.28125


def build_s5():
    from contextlib import ExitStack
    nc = bass.Bass("TRN2", target_bir_lowering=False)
    u_d = dram_in(nc, "u", [2, 64, S], BF16)
    prm_d = dram_in(nc, "prm", [128, 8, 4], F32)
    bm_d = dram_in(nc, "bm", [128, 8, 2, 16], F32)
    cm_d = dram_in(nc, "cm", [128, 8, 2, 16], F32)
    iota_d = dram_in(nc, "iota", [128, ST + 1], F32)
    sgn_d = dram_in(nc, "sgn", [128, 2], F32)
    idf_d = dram_in(nc, "idf", [128, 128], F32)
    swp_d = dram_in(nc, "swp", [128, 128], F32)
    y_d = dram_out(nc, "y_out", [2, 64, S], F32)
    with ExitStack() as st:
        env = Env(nc, st)
        p = env.p
        u_sb = env.sb("u_s", [64, 2, S], BF16)
        prm = env.sb("prm_s", [128, 8, 4], F32)
        bm = env.sb("bm_s", [128, 8, 2, 16], F32)
        cm = env.sb("cm_s", [128, 8, 2, 16], F32)
        iota = env.sb("iota_s", [128, ST + 1], F32)
        sgn = env.sb("sgn_s", [128, 2], F32)
        idf = env.sb("idf_s", [128, 128], F32)
        swp = env.sb("swp_s", [128, 128], F32)
        COS = env.sb("COS", [128, 8, ST + 1], F32)
        SIN = env.sb("SIN", [128, 8, ST + 1], F32)
        Rk = env.sb("Rk", [128, 8, ST], F32)
        Rot = env.sb("Rot", [128, 8, 128], F32)
        Bl = env.sb("Bl", [64, 8, 2, 128], BF16)
        Cl = env.sb("Cl", [128, 8, 2, 64], BF16)
        sc = env.sb("sc", [128, 8, 16], F32)
        ang = env.sb("ang", [128, ST + 1], F32)
        kq = env.sb("kq", [128, ST + 1], I32)
        kf = env.sb("kf", [128, ST + 1], F32)
        bpad = env.sb("bpad", [128, 2, 64], F32)
        tmp16 = env.sb("tmp16", [128, 4, 16], F32)
        init = env.sb("init", [128, 8], F32)
        t1 = [env.sb(f"t1_{i}", [128, ST], F32) for i in range(2)]
        t2 = [env.sb(f"t2_{i}", [128, ST], F32) for i in range(2)]
        v_sb = [env.sb(f"v_{i}", [128, ST], F32) for i in range(2)]
        w_sb = [env.sb(f"w_{i}", [128, ST], F32) for i in range(2)]
        Wc = [env.sb(f"Wc_{i}", [128, ST], BF16) for i in range(2)]
        Ws = [env.sb(f"Ws_{i}", [128, ST], BF16) for i in range(2)]
        yst = [env.sb(f"yst_{i}", [64, ST], F32) for i in range(2)]
        bu_ps = [env.ps(f"bu{i}", [128, ST]) for i in range(2)]
        bs_ps = [env.ps(f"bs{i}", [128, ST]) for i in range(2)]
        y_ps = [env.ps(f"yps{i}", [64, ST]) for i in range(2)]
        i_ps = env.ps("ips", [128, 8])
        s_ps = env.ps("sps", [64, 128])
        b_c, b_u, b_tab, b_set = Buf(), Buf(), Buf(), Buf()
        b_ang, b_kq, b_kf, b_bpad, b_sps, b_tmp = Buf(), Buf(), Buf(), Buf(), Buf(), Buf()
        b_init = [Buf() for _ in range(8)]
        b_ips = [Buf() for _ in range(8)]
        b_t1 = [Buf(), Buf()]
        b_t2 = [Buf(), Buf()]
        b_v = [Buf(), Buf()]
        b_w = [Buf(), Buf()]
        b_wc = [Buf(), Buf()]
        b_ws = [Buf(), Buf()]
        b_yst = [Buf(), Buf()]
        b_bu = [Buf(), Buf()]
        b_bs = [Buf(), Buf()]
        b_yps = [Buf(), Buf()]
        ds_c = env.dsem("dc")
        ds_u = env.dsem("du")
        ds_o = env.dsem("do")

        for dst, src in ((prm, prm_d), (bm, bm_d), (cm, cm_d)):
            p.dma(p.pool, ds_c, dst[:], src[:, :, :] if len(src.shape) == 3 else src[:, :, :, :], writes=[b_c])
        for dst, src in ((iota, iota_d), (sgn, sgn_d), (idf, idf_d), (swp, swp_d)):
            lt = p.dma(p.pool, ds_c, dst[:], src[:, :], writes=[b_c])
        lu = p.dma(p.sp, ds_u, u_sb[:, 0, :], u_d[0, :, :], writes=[b_u])
        lu = p.dma(p.sp, ds_u, u_sb[:, 1, :], u_d[1, :, :], writes=[b_u])
        b_u.w = lu
        p.op(p.dve, lambda h: h.memset(init[:], 0.0), writes=b_init)
        p.op(p.dve, lambda h: h.memset(Bl[:], 0.0), writes=[b_set])

        def col(k, i):
            return sc[:, k, i:i + 1]

        def dv(fn, reads, writes):
            return p.op(p.dve, fn, reads=reads, writes=writes)

        def C_(i):
            return sc[:, :, i]
        LR, LI, LS = prm[:, :, 0], prm[:, :, 1], prm[:, :, 2]
        p.op(p.act, lambda h: h.activation(out=C_(0), in_=LS, func=AF.Exp), reads=[b_c], writes=[b_set])
        dv(lambda h: h.tensor_tensor(out=C_(1), in0=LI, in1=C_(0), op=ALU.mult), [b_set, b_c], [b_set])
        dv(lambda h: h.tensor_tensor(out=C_(2), in0=LR, in1=C_(0), op=ALU.mult), [b_set, b_c], [b_set])
        p.op(p.act, lambda h: h.activation(out=C_(2), in_=C_(2), func=AF.Exp), reads=[b_set], writes=[b_set])
        for k in range(8):
            for which in range(2):
                tab = SIN if which == 0 else COS
                shift = 0.0 if which == 0 else float(np.pi / 2)
                dv(lambda h, k=k, shift=shift: h.tensor_scalar(out=ang[:], in0=iota[:], scalar1=col(k, 1), scalar2=shift,
                                                               op0=ALU.mult, op1=ALU.add), [b_set, b_c], [b_ang])
                dv(lambda h: h.tensor_scalar(out=kq[:], in0=ang[:], scalar1=float(1.0 / (2 * np.pi)), scalar2=None,
                                             op0=ALU.mult), [b_ang], [b_kq])
                dv(lambda h: h.tensor_copy(out=kf[:], in_=kq[:]), [b_kq], [b_kf])
                dv(lambda h: h.scalar_tensor_tensor(out=ang[:], in0=kf[:], scalar=-C1_2PI, in1=ang[:],
                                                    op0=ALU.mult, op1=ALU.add), [b_kf, b_ang], [b_ang])
                dv(lambda h: h.scalar_tensor_tensor(out=ang[:], in0=kf[:], scalar=-C2_2PI, in1=ang[:],
                                                    op0=ALU.mult, op1=ALU.add), [b_kf, b_ang], [b_ang])
                p.op(p.act, lambda h, k=k, tab=tab: h.activation(out=tab[:, k, :], in_=ang[:], func=AF.Sin),
                     reads=[b_ang], writes=[b_tab])
        c1v, s1v = COS[:, :, 1], SIN[:, :, 1]
        s5v = SIN[:, :, ST]
        sg1 = sgn[:, 0:1].to_broadcast([128, 8])
        sg2 = sgn[:, 1:2].to_broadcast([128, 8])
        dv(lambda h: h.tensor_tensor(out=C_(3), in0=C_(2), in1=c1v, op=ALU.mult), [b_set, b_tab], [b_set])
        dv(lambda h: h.tensor_tensor(out=C_(4), in0=C_(2), in1=s1v, op=ALU.mult), [b_set, b_tab], [b_set])
        dv(lambda h: h.tensor_tensor(out=C_(5), in0=LR, in1=LR, op=ALU.mult), [b_c, b_set], [b_set])
        dv(lambda h: h.tensor_tensor(out=C_(9), in0=LI, in1=LI, op=ALU.mult), [b_c, b_set], [b_set])
        dv(lambda h: h.tensor_tensor(out=C_(5), in0=C_(5), in1=C_(9), op=ALU.add), [b_set], [b_set])
        dv(lambda h: h.reciprocal(out=C_(13), in_=C_(5)), [b_set], [b_set])
        dv(lambda h: h.tensor_scalar(out=C_(6), in0=C_(3), scalar1=-1.0, scalar2=None, op0=ALU.add), [b_set], [b_set])
        dv(lambda h: h.tensor_tensor(out=C_(9), in0=C_(4), in1=LI, op=ALU.mult), [b_set, b_c], [b_set])
        dv(lambda h: h.tensor_tensor(out=C_(7), in0=C_(6), in1=LR, op=ALU.mult), [b_set, b_c], [b_set])
        dv(lambda h: h.tensor_tensor(out=C_(7), in0=C_(7), in1=C_(9), op=ALU.add), [b_set], [b_set])
        dv(lambda h: h.tensor_tensor(out=C_(7), in0=C_(7), in1=C_(13), op=ALU.mult), [b_set], [b_set])
        dv(lambda h: h.tensor_tensor(out=C_(9), in0=C_(6), in1=LI, op=ALU.mult), [b_set, b_c], [b_set])
        dv(lambda h: h.tensor_tensor(out=C_(8), in0=C_(4), in1=LR, op=ALU.mult), [b_set, b_c], [b_set])
        dv(lambda h: h.tensor_tensor(out=C_(8), in0=C_(8), in1=C_(9), op=ALU.subtract), [b_set], [b_set])
        dv(lambda h: h.tensor_tensor(out=C_(8), in0=C_(8), in1=C_(13), op=ALU.mult), [b_set], [b_set])
        dv(lambda h: h.tensor_tensor(out=C_(10), in0=C_(8), in1=sg1, op=ALU.mult), [b_set, b_c], [b_set])
        dv(lambda h: h.tensor_tensor(out=C_(11), in0=C_(7), in1=sg2, op=ALU.mult), [b_set, b_c], [b_set])
        dv(lambda h: h.tensor_tensor(out=C_(12), in0=s5v, in1=sg2, op=ALU.mult), [b_tab, b_c], [b_set])
        for k in range(8):
            g = k % 4
            c5 = COS[:, k, ST:ST + 1]
            dv(lambda h: h.memset(bpad[:], 0.0), [], [b_bpad])
            bA, bB = bm[:, k, 0, :], bm[:, k, 1, :]
            dv(lambda h, k=k, bB=bB: h.tensor_scalar(out=tmp16[:, 0, :], in0=bB, scalar1=col(k, 10), scalar2=None, op0=ALU.mult),
               [b_set, b_c], [b_tmp])
            dv(lambda h, k=k, bA=bA, g=g: h.scalar_tensor_tensor(out=bpad[:, 0, 16 * g:16 * g + 16], in0=bA, scalar=col(k, 7),
                                                                 in1=tmp16[:, 0, :], op0=ALU.mult, op1=ALU.add),
               [b_set, b_c, b_tmp], [b_bpad])
            dv(lambda h, k=k, bB=bB: h.tensor_scalar(out=tmp16[:, 1, :], in0=bB, scalar1=col(k, 11), scalar2=None, op0=ALU.mult),
               [b_set, b_c], [b_tmp])
            dv(lambda h, k=k, bA=bA, g=g: h.scalar_tensor_tensor(out=bpad[:, 1, 16 * g:16 * g + 16], in0=bA, scalar=col(k, 8),
                                                                 in1=tmp16[:, 1, :], op0=ALU.mult, op1=ALU.add),
               [b_set, b_c, b_tmp], [b_bpad])
            for which in range(2):
                p.op(p.pe, lambda h, which=which: h.matmul(s_ps[:], lhsT=bpad[:, which, :], rhs=idf[:], start=True, stop=True),
                     reads=[b_bpad, b_c], writes=[b_sps])
                p.op(p.act, lambda h, k=k, which=which: h.activation(out=Bl[:, k, which, :], in_=s_ps[:], func=AF.Copy),
                     reads=[b_sps], writes=[b_set])
            dv(lambda h, k=k: h.memset(Cl[:, k, :, :], 0.0), [], [b_set])
            cA, cB = cm[:, k, 0, :], cm[:, k, 1, :]
            dv(lambda h, k=k, cA=cA, g=g: h.tensor_scalar(out=Cl[:, k, 0, 16 * g:16 * g + 16], in0=cA, scalar1=sgn[:, 1:2],
                                                          scalar2=None, op0=ALU.mult), [b_c], [b_set])
            dv(lambda h, k=k, cB=cB, g=g: h.tensor_scalar(out=Cl[:, k, 1, 16 * g:16 * g + 16], in0=cB, scalar1=-1.0,
                                                          scalar2=None, op0=ALU.mult), [b_c], [b_set])
            dv(lambda h, k=k: h.tensor_scalar(out=Rk[:, k, :], in0=iota[:, 0:ST], scalar1=0.0, scalar2=col(k, 2),
                                              op0=ALU.mult, op1=ALU.add), [b_c, b_set], [b_set])
            dv(lambda h, k=k: h.tensor_scalar(out=Rot[:, k, :], in0=swp[:], scalar1=col(k, 12), scalar2=None, op0=ALU.mult),
               [b_c, b_set], [b_set])
            dv(lambda h, k=k, c5=c5: h.scalar_tensor_tensor(out=Rot[:, k, :], in0=idf[:], scalar=c5, in1=Rot[:, k, :],
                                                            op0=ALU.mult, op1=ALU.add), [b_c, b_tab, b_set], [b_set])

        steps = [(t, d, g) for t in range(NST) for d in range(2) for g in range(4)]

        def dvs(fn, reads, writes):
            return p.op(p.dve, fn, reads=reads, writes=writes, nosync=True)

        def stA_pe(i):
            t, d, g = steps[i]
            k = d * 4 + g
            bi = i % 2
            usl = u_sb[:, d, t * ST:(t + 1) * ST]
            p.op(p.pe, lambda h, k=k, bi=bi, usl=usl: h.matmul(bu_ps[bi][:], lhsT=Bl[:, k, 0, :], rhs=usl, start=True, stop=True),
                 reads=[b_set, b_u], writes=[b_bu[bi]])
            p.op(p.pe, lambda h, k=k, bi=bi, usl=usl: h.matmul(bs_ps[bi][:], lhsT=Bl[:, k, 1, :], rhs=usl, start=True, stop=True),
                 reads=[b_set, b_u], writes=[b_bs[bi]])

        def stA_dve(i):
            t, d, g = steps[i]
            k = d * 4 + g
            bi = i % 2
            dvs(lambda h, k=k, bi=bi: h.tensor_tensor(out=t1[bi][:], in0=bu_ps[bi][:], in1=COS[:, k, 0:ST], op=ALU.mult),
               [b_bu[bi], b_tab], [b_t1[bi]])
            dvs(lambda h, k=k, bi=bi: h.tensor_tensor(out=t2[bi][:], in0=bs_ps[bi][:], in1=SIN[:, k, 0:ST], op=ALU.mult),
               [b_bs[bi], b_tab], [b_t2[bi]])
            dvs(lambda h, bi=bi: h.tensor_tensor(out=v_sb[bi][:], in0=t1[bi][:], in1=t2[bi][:], op=ALU.add),
               [b_t1[bi], b_t2[bi]], [b_v[bi]])

        def stB(i):
            t, d, g = steps[i]
            k = d * 4 + g
            bi = i % 2
            dvs(lambda h, k=k, bi=bi: h.tensor_tensor_scan(out=w_sb[bi][:], data0=Rk[:, k, :], data1=v_sb[bi][:],
                                                          initial=init[:, k:k + 1], op0=ALU.mult, op1=ALU.add),
               [b_v[bi], b_set, b_init[k]], [b_w[bi]])
            p.op(p.pe, lambda h, k=k, bi=bi: h.matmul(i_ps[:, k:k + 1], lhsT=Rot[:, k, :], rhs=w_sb[bi][:, ST - 1:ST],
                                                      start=True, stop=True),
                 reads=[b_w[bi], b_set], writes=[b_ips[k]])
            p.op(p.act, lambda h, k=k: h.activation(out=init[:, k:k + 1], in_=i_ps[:, k:k + 1], func=AF.Copy),
                 reads=[b_ips[k]], writes=[b_init[k]])

        def stC(i):
            t, d, g = steps[i]
            k = d * 4 + g
            bi = i % 2
            p.op(p.pool, lambda h, k=k, bi=bi: h.tensor_tensor(out=Wc[bi][:], in0=w_sb[bi][:], in1=COS[:, k, 0:ST], op=ALU.mult),
                 reads=[b_w[bi], b_tab], writes=[b_wc[bi]])
            dvs(lambda h, k=k, bi=bi: h.tensor_tensor(out=Ws[bi][:], in0=w_sb[bi][:], in1=SIN[:, k, 0:ST], op=ALU.mult),
               [b_w[bi], b_tab], [b_ws[bi]])
            p.op(p.pe, lambda h, k=k, bi=bi, d=d, g=g: h.matmul(y_ps[d][:], lhsT=Cl[:, k, 0, :], rhs=Wc[bi][:],
                                                                start=(g == 0), stop=False),
                 reads=[b_wc[bi], b_set], writes=[b_yps[d]])
            p.op(p.pe, lambda h, k=k, bi=bi, d=d, g=g: h.matmul(y_ps[d][:], lhsT=Cl[:, k, 1, :], rhs=Ws[bi][:],
                                                                start=False, stop=(g == 3)),
                 reads=[b_ws[bi], b_set], writes=[b_yps[d]])
            if g == 3:
                p.op(p.act, lambda h, d=d: h.activation(out=yst[d][:], in_=y_ps[d][:], func=AF.Copy),
                     reads=[b_yps[d]], writes=[b_yst[d]])
                p.dma(p.sp, ds_o, y_d[d, :, t * ST:(t + 1) * ST], yst[d][:], reads=[b_yst[d]])

        n = len(steps)
        stA_pe(0)
        for i in range(n + 1):
            if i < n:
                if i + 1 < n:
                    stA_pe(i + 1)
                stA_dve(i)
                stB(i)
            if i >= 1:
                stC(i - 1)
        p.finish(p.sp, [(ds_o.sem, ds_o.cnt)])
        with nc.Block() as block:
            p.replay(block)
    return nc


_CACHE = {}


def _run(nc, maps):
    res = run_bass_kernel_spmd(nc, maps, core_ids=list(range(NCORES)))
    return res.results


def s5_host_params(inp, l, q):
    prm = np.zeros((128, 8, 4), np.float32)
    bm = np.zeros((128, 8, 2, 16), np.float32)
    cm = np.zeros((128, 8, 2, 16), np.float32)
    for d in range(2):
        for gl in range(4):
            k = d * 4 + gl
            g = 4 * q + gl
            prm[:, k, 0] = np.tile(inp["s5_lam_re"][l, d, g], 2)
            prm[:, k, 1] = np.tile(inp["s5_lam_im"][l, d, g], 2)
            prm[:, k, 2] = inp["s5_log_step"][l, d, g]
            bre, bim = inp["s5_b_re"][l, d, g], inp["s5_b_im"][l, d, g]
            bm[:, k, 0] = np.concatenate([bre, bim], 0)
            bm[:, k, 1] = np.concatenate([bim, bre], 0)
            cre, cim = inp["s5_c_re"][l, d, g].T, inp["s5_c_im"][l, d, g].T
            cm[:, k, 0] = np.concatenate([cre, cim], 0)
            cm[:, k, 1] = np.concatenate([cim, cre], 0)
    return prm, bm, cm


def s5_consts():
    iota = np.broadcast_to(np.arange(ST + 1, dtype=np.float32), (128, ST + 1)).copy()
    sgn = np.zeros((128, 2), np.float32)
    sgn[:64, 0], sgn[64:, 0] = -1.0, 1.0
    sgn[:64, 1], sgn[64:, 1] = 1.0, -1.0
    idf = np.eye(128, dtype=np.float32)
    swp = np.zeros((128, 128), np.float32)
    for p_ in range(128):
        swp[p_, (p_ + 64) % 128] = 1.0
    return iota, sgn, idf, swp


def kernel(**inputs):
    inp = {k: np.asarray(v, dtype=np.float32) for k, v in inputs.items()}
    streams = [np.stack(stream_for_launch(j, inp)) for j in range(DEPTH + 1)]
    sizes = [s.shape[0] for s in streams]
    allb = np.concatenate(streams)
    per = allb.shape[0] // NCORES
    assert per * NCORES == allb.shape[0]
    ncw = build_w(per)
    res = _run(ncw, [{"wf": allb[c * per:(c + 1) * per].reshape(per * 256, 2048)} for c in range(NCORES)])
    wb = np.concatenate([np.asarray(r["wb"]).reshape(per, 128, WSLOT) for r in res])
    del allb, streams
    wbs = np.split(wb, np.cumsum(sizes)[:-1])
    cst = np.zeros((128, 256), np.float32)
    cst[:, :128] = 1.0 / 1024
    for hh in range(2):
        cst[hh * 64:(hh + 1) * 64, 128 + hh * 64:128 + (hh + 1) * 64] = 1.0 / 64
    cst = cst.astype(NPBF)
    idb = np.eye(128, dtype=np.float32).astype(NPBF)
    iota, sgn, idf, swp = s5_consts()
    acons = [attn_bias_consts(hd) for hd in range(4)]
    vidx = [vt_indices(ph) for ph in range(3)]

    x2 = inp["x"].reshape(B * S, D)
    xT = [np.ascontiguousarray(x2[c * TPC:(c + 1) * TPC].T) for c in range(NCORES)]
    tail_in = None
    nct = {}
    nca = None
    ncs = None
    for j in range(DEPTH + 1):
        key = (j >= 1, j <= DEPTH - 1)
        if key not in nct:
            nct[key] = build_t(*key)
        cols = cols_for_launch(j, inp)
        maps = []
        for c in range(NCORES):
            m = {"xin": xT[c], "wst": wbs[j], "cols": cols, "cst": cst}
            if j >= 1:
                m.update(tail_in[c])
            maps.append(m)
        res = _run(nct[key], maps)
        xT = [np.asarray(r["xout"]) for r in res]
        if j == DEPTH:
            break
        l = j
        cpb = NCORES // B
        pT = [np.concatenate([np.asarray(res[b * cpb + i]["pout"]) for i in range(cpb)], axis=1) for b in range(B)]
        if nca is None:
            nca = build_attn()
        maps = []
        for c in range(NCORES):
            b, hd = c // 4, c % 4
            P = pT[b]
            maps.append(attn_host_inputs(P, hd, acons[hd], inp["na_rel_bias"][l][hd], inp["a_sink"][l][hd], idb, idf, vidx))
        ares = _run(nca, maps)
        if ncs is None:
            ncs = build_s5()
        maps = []
        for c in range(NCORES):
            b, q = c // 4, c % 4
            uu = pT[b][512 + 64 * q:512 + 64 * q + 64]
            prm, bm, cm = s5_host_params(inp, l, q)
            maps.append({"u": np.ascontiguousarray(np.stack([uu, uu[:, ::-1]])), "prm": prm, "bm": bm, "cm": cm,
                         "iota": iota, "sgn": sgn, "idf": idf, "swp": swp})
        sres = _run(ncs, maps)
        tail_in = []
        oT, yf, yb = [], [], []
        for b in range(B):
            o = np.zeros((768, S), NPBF)
            f = np.zeros((256, S), np.float32)
            r_ = np.zeros((256, S), np.float32)
            for hd in range(4):
                oo = np.asarray(ares[b * 4 + hd]["o_out"])
                for ph in range(3):
                    o[ph * 256 + hd * 64:ph * 256 + hd * 64 + 64] = oo[ph]
                yy = np.asarray(sres[b * 4 + hd]["y_out"])
                f[hd * 64:hd * 64 + 64] = yy[0]
                r_[hd * 64:hd * 64 + 64] = yy[1][:, ::-1]
            oT.append(o)
            yf.append(f)
            yb.append(r_)
        for c in range(NCORES):
            b, ch = c // cpb, c % cpb
            sl = slice(ch * TPC, (ch + 1) * TPC)
            tail_in.append({"o_in": np.ascontiguousarray(oT[b][:, sl]), "yf_in": np.ascontiguousarray(yf[b][:, sl]),
                            "yb_in": np.ascontiguousarray(yb[b][:, sl]),
                            "u_in": np.ascontiguousarray(pT[b][512:768, sl])})
    out = np.concatenate([x.T for x in xT], axis=0).reshape(B, S, D).astype(np.float32)
    return out
```
